# Optimizing a Trainium2 kernel written in Bass

```python
import math
import jax, jax.numpy as jnp
from jax import lax
import numpy as np

D_MODEL = 2048
BATCH = 4
SEQ = 2048
DEPTH = 2
DEC_BATCH = 128
DEC_SEQ = 1
PAST_LEN = 16384
PAGE_SIZE = 128

N_EVEN = (DEPTH + 1) // 2
N_ODD = DEPTH // 2
MIX_HALF = D_MODEL // 2
RET_HEADS = 4
RET_DK = MIX_HALF // RET_HEADS
RET_CHUNK = 128
HG_HEADS = 8
HG_DK = MIX_HALF // HG_HEADS
HG_DV = MIX_HALF // HG_HEADS
HG_CHUNK = 64
S5_GROUP = 16
S5_GROUPS = D_MODEL // S5_GROUP
S5_STATE = 64
D_FF = -(-8 * D_MODEL // (3 * 256)) * 256
ROPE_BASE = 10000.0
EPS = 1e-6

kernel_name = "retention_hgrn2_s5_hybrid_step"


def _rmsnorm(x, g):
    xf = x.astype(jnp.float32)
    y = xf * lax.rsqrt(jnp.mean(xf * xf, axis=-1, keepdims=True) + EPS)
    return (y * g.astype(jnp.float32)).astype(x.dtype)


def _rotary(x, pos):
    half = x.shape[-1] // 2
    inv = ROPE_BASE ** (-jnp.arange(half, dtype=jnp.float32) / half)
    ang = pos[:, None] * inv[None, :]
    cos = jnp.cos(ang)[None, :, None, :]
    sin = jnp.sin(ang)[None, :, None, :]
    x1, x2 = x[..., :half], x[..., half:]
    return jnp.concatenate([x1 * cos - x2 * sin, x1 * sin + x2 * cos], axis=-1)


def _to_chunks(t, c):
    b, l = t.shape[:2]
    return jnp.moveaxis(t.reshape((b, l // c, c) + t.shape[2:]), 1, 0)


def _from_chunks(t):
    n, b, c = t.shape[:3]
    return jnp.moveaxis(t, 0, 1).reshape((b, n * c) + t.shape[3:])


def _retention(q, k, v, s0):
    L = q.shape[1]
    c = math.gcd(L, RET_CHUNK)
    lg = jnp.log(1.0 - 2.0 ** (-5.0 - jnp.arange(RET_HEADS, dtype=jnp.float32)))
    idx = jnp.arange(c, dtype=jnp.float32)
    causal = idx[:, None] >= idx[None, :]
    intra = jnp.exp(jnp.where(causal[None], (idx[:, None] - idx[None, :])[None] * lg[:, None, None], -jnp.inf))
    q_dec = jnp.exp((idx[:, None] + 1.0) * lg[None, :])
    k_dec = jnp.exp((c - 1.0 - idx[:, None]) * lg[None, :])
    c_dec = jnp.exp(c * lg)

    def step(s, inp):
        qc, kc, vc = inp
        att = jnp.einsum('bihd,bjhd->bhij', qc, kc) * intra[None]
        o = (jnp.einsum('bhij,bjhv->bihv', att, vc)
             + jnp.einsum('bihd,bhdv->bihv', qc * q_dec[None, :, :, None], s))
        s = s * c_dec[None, :, None, None] + jnp.einsum('bjhd,bjhv->bhdv', kc * k_dec[None, :, :, None], vc)
        return s, o

    s, o = lax.scan(step, s0, (_to_chunks(q, c), _to_chunks(k, c), _to_chunks(v, c)))
    return _from_chunks(o), s


def _hgrn2(q, k, v, log_f, s0):
    L = q.shape[1]
    c = math.gcd(L, HG_CHUNK)
    idx = jnp.arange(c)
    causal = (idx[:, None] >= idx[None, :])[None, :, :, None, None]

    def step(s, inp):
        qc, kc, vc, gc = inp
        b = jnp.cumsum(gc, axis=1)
        o_cross = jnp.einsum('bihd,bhdv->bihv', qc * jnp.exp(b), s)
        rel = jnp.exp(jnp.where(causal, b[:, :, None] - b[:, None, :], -jnp.inf))
        att = jnp.einsum('bihd,bjhd,bijhd->bhij', qc, kc, rel)
        o = o_cross + jnp.einsum('bhij,bjhv->bihv', att, vc)
        b_last = b[:, -1]
        s = s * jnp.exp(b_last)[..., None] + jnp.einsum('bjhd,bjhv->bhdv', kc * jnp.exp(b_last[:, None] - b), vc)
        return s, o

    s, o = lax.scan(step, s0, (_to_chunks(q, c), _to_chunks(k, c), _to_chunks(v, c), _to_chunks(log_f, c)))
    return _from_chunks(o), s


def _complex_affine_combine(e1, e2):
    a1r, a1i, b1r, b1i = e1
    a2r, a2i, b2r, b2i = e2
    return (a1r * a2r - a1i * a2i,
            a1r * a2i + a1i * a2r,
            a2r * b1r - a2i * b1i + b2r,
            a2r * b1i + a2i * b1r + b2i)


def _s5(u, h0_re, h0_im, lam_re, lam_im, log_dt, b_re, b_im, c_re, c_im, d_skip):
    B, L, _ = u.shape
    ug = u.astype(jnp.float32).reshape(B, L, S5_GROUPS, S5_GROUP)
    lr = lam_re.astype(jnp.float32)
    li = lam_im.astype(jnp.float32)
    dt = jnp.exp(log_dt.astype(jnp.float32))[:, None]
    mag = jnp.exp(lr * dt)
    ar = mag * jnp.cos(li * dt)
    ai = mag * jnp.sin(li * dt)
    den = lr * lr + li * li
    cr = ((ar - 1.0) * lr + ai * li) / den
    ci = (ai * lr - (ar - 1.0) * li) / den
    br = b_re.astype(jnp.float32)
    bi = b_im.astype(jnp.float32)
    bbr = cr[..., None] * br - ci[..., None] * bi
    bbi = cr[..., None] * bi + ci[..., None] * br
    bu_r = jnp.einsum('blgc,gpc->blgp', ug, bbr)
    bu_i = jnp.einsum('blgc,gpc->blgp', ug, bbi)
    h0r = h0_re.astype(jnp.float32)
    h0i = h0_im.astype(jnp.float32)
    bu_r = bu_r.at[:, 0].add(ar * h0r - ai * h0i)
    bu_i = bu_i.at[:, 0].add(ar * h0i + ai * h0r)
    a_r = jnp.broadcast_to(ar, bu_r.shape)
    a_i = jnp.broadcast_to(ai, bu_i.shape)
    _, _, hr, hi = lax.associative_scan(_complex_affine_combine, (a_r, a_i, bu_r, bu_i), axis=1)
    y = (jnp.einsum('gcp,blgp->blgc', c_re.astype(jnp.float32), hr)
         - jnp.einsum('gcp,blgp->blgc', c_im.astype(jnp.float32), hi)
         + d_skip.astype(jnp.float32).reshape(S5_GROUPS, S5_GROUP) * ug)
    return y.reshape(B, L, D_MODEL), hr[:, -1], hi[:, -1]


def _even_mixer(h, pos, s_ret, s_hg, w_in, ret_gn_g, lb, hg_gn_g, w_out):
    B, L, _ = h.shape
    proj = (h @ w_in).astype(jnp.float32)
    r_q, r_k, r_v, r_g, g_q, g_f, g_i, g_g = jnp.split(proj, 8, axis=-1)
    q = _rotary(r_q.reshape(B, L, RET_HEADS, RET_DK), pos)
    k = _rotary(r_k.reshape(B, L, RET_HEADS, RET_DK), pos) * (RET_DK ** -0.5)
    o_r, s_ret_new = _retention(q, k, r_v.reshape(B, L, RET_HEADS, RET_DK), s_ret.astype(jnp.float32))
    mu = jnp.mean(o_r, axis=-1, keepdims=True)
    var = jnp.mean(jnp.square(o_r - mu), axis=-1, keepdims=True)
    o_r = ((o_r - mu) * lax.rsqrt(var + EPS)).reshape(B, L, MIX_HALF) * ret_gn_g.astype(jnp.float32) * jax.nn.silu(r_g)
    lbf = lb.astype(jnp.float32)
    f = lbf + (1.0 - lbf) * jax.nn.sigmoid(g_f)
    hd = lambda t: t.reshape(B, L, HG_HEADS, HG_DK)
    o_h, s_hg_new = _hgrn2(hd(jax.nn.silu(g_q)), hd(1.0 - f), g_i.reshape(B, L, HG_HEADS, HG_DV),
                           hd(jnp.log(f)), s_hg.astype(jnp.float32))
    o_h = o_h * lax.rsqrt(jnp.mean(o_h * o_h, axis=-1, keepdims=True) + EPS)
    o_h = o_h.reshape(B, L, MIX_HALF) * hg_gn_g.astype(jnp.float32) * jax.nn.silu(g_g)
    mix = jnp.concatenate([o_r, o_h], axis=-1).astype(h.dtype) @ w_out
    return mix, s_ret_new, s_hg_new


def _odd_mixer(h, s_re, s_im, lam_re, lam_im, log_dt, b_re, b_im, c_re, c_im, d_skip, w_a, w_b):
    y, hr, hi = _s5(h, s_re, s_im, lam_re, lam_im, log_dt, b_re, b_im, c_re, c_im, d_skip)
    z = jax.nn.gelu(y).astype(h.dtype)
    mix = (z @ w_a) * jax.nn.sigmoid(z @ w_b)
    return mix, hr, hi


def _swiglu(h, wg, wu, wd):
    return (jax.nn.silu(h @ wg) * (h @ wu)) @ wd


def _trunk(x, pos0, s_ret, s_hg, s5_re, s5_im, weights):
    (attn_norm_g, w_in, ret_gn_g, hg_lb, hg_gn_g, w_out, ssm_norm_g, s5_lam_re, s5_lam_im,
     s5_log_dt, s5_b_re, s5_b_im, s5_c_re, s5_c_im, s5_d, w_glu_a, w_glu_b, ffn_norm_g,
     w_ffn_gate, w_ffn_up, w_ffn_down, final_norm_g) = weights
    L = x.shape[1]
    pos = pos0 + jnp.arange(L, dtype=jnp.float32)
    lb_all = jnp.cumsum(jax.nn.softmax(hg_lb.astype(jnp.float32), axis=0), axis=0)
    new_ret, new_hg, new_re, new_im = [], [], [], []
    for layer in range(DEPTH):
        j = layer // 2
        if layer % 2 == 0:
            h = _rmsnorm(x, attn_norm_g[j])
            mix, sr, sh = _even_mixer(h, pos, s_ret[j], s_hg[j], w_in[j], ret_gn_g[j], lb_all[layer],
                                      hg_gn_g[j], w_out[j])
            new_ret.append(sr.astype(s_ret.dtype))
            new_hg.append(sh.astype(s_hg.dtype))
        else:
            h = _rmsnorm(x, ssm_norm_g[j])
            mix, hr, hi = _odd_mixer(h, s5_re[j], s5_im[j], s5_lam_re[j], s5_lam_im[j], s5_log_dt[j],
                                     s5_b_re[j], s5_b_im[j], s5_c_re[j], s5_c_im[j], s5_d[j],
                                     w_glu_a[j], w_glu_b[j])
            new_re.append(hr.astype(s5_re.dtype))
            new_im.append(hi.astype(s5_im.dtype))
        x = x + mix.astype(x.dtype)
        x = x + _swiglu(_rmsnorm(x, ffn_norm_g[layer]), w_ffn_gate[layer], w_ffn_up[layer],
                        w_ffn_down[layer]).astype(x.dtype)
    y = _rmsnorm(x, final_norm_g)
    return y, jnp.stack(new_ret), jnp.stack(new_hg), jnp.stack(new_re), jnp.stack(new_im)


def setup_inputs(seed: int = 0) -> dict:
    key = jax.random.key(seed)
    ks = jax.random.split(key, 32)
    f32 = jnp.float32
    nrm = lambda k, shape, scale: scale * jax.random.normal(k, shape, f32)
    gain = lambda k, shape: 1.0 + 0.05 * jax.random.normal(k, shape, f32)
    n_idx = jnp.arange(S5_STATE, dtype=f32)
    return {
        "x_prompt": nrm(ks[0], (BATCH, SEQ, D_MODEL), 1.0),
        "x_sample": nrm(ks[1], (DEC_BATCH, DEC_SEQ, D_MODEL), 1.0),
        "state_ret": nrm(ks[2], (N_EVEN, DEC_BATCH, RET_HEADS, RET_DK, RET_DK), 0.1),
        "state_hgrn": nrm(ks[3], (N_EVEN, DEC_BATCH, HG_HEADS, HG_DK, HG_DV), 0.3),
        "state_s5_re": nrm(ks[4], (N_ODD, DEC_BATCH, S5_GROUPS, S5_STATE), 0.1),
        "state_s5_im": nrm(ks[5], (N_ODD, DEC_BATCH, S5_GROUPS, S5_STATE), 0.1),
        "attn_norm_g": gain(ks[6], (N_EVEN, D_MODEL)),
        "w_in": nrm(ks[7], (N_EVEN, D_MODEL, 8 * MIX_HALF), D_MODEL ** -0.5),
        "ret_gn_g": gain(ks[8], (N_EVEN, MIX_HALF)),
        "hg_lb": nrm(ks[9], (DEPTH + 1, MIX_HALF), 0.5),
        "hg_gn_g": gain(ks[10], (N_EVEN, MIX_HALF)),
        "w_out": nrm(ks[11], (N_EVEN, D_MODEL, D_MODEL), D_MODEL ** -0.5),
        "ssm_norm_g": gain(ks[12], (N_ODD, D_MODEL)),
        "s5_lam_re": -0.5 + nrm(ks[13], (N_ODD, S5_GROUPS, S5_STATE), 0.01),
        "s5_lam_im": math.pi * n_idx + nrm(ks[14], (N_ODD, S5_GROUPS, S5_STATE), 0.01),
        "s5_log_dt": jax.random.uniform(ks[15], (N_ODD, S5_GROUPS), f32, math.log(1e-3), math.log(1e-1)),
        "s5_b_re": nrm(ks[16], (N_ODD, S5_GROUPS, S5_STATE, S5_GROUP), (2 * S5_GROUP) ** -0.5),
        "s5_b_im": nrm(ks[17], (N_ODD, S5_GROUPS, S5_STATE, S5_GROUP), (2 * S5_GROUP) ** -0.5),
        "s5_c_re": nrm(ks[18], (N_ODD, S5_GROUPS, S5_GROUP, S5_STATE), S5_STATE ** -0.5),
        "s5_c_im": nrm(ks[19], (N_ODD, S5_GROUPS, S5_GROUP, S5_STATE), S5_STATE ** -0.5),
        "s5_d": nrm(ks[20], (N_ODD, D_MODEL), 1.0),
        "w_glu_a": nrm(ks[21], (N_ODD, D_MODEL, D_MODEL), D_MODEL ** -0.5),
        "w_glu_b": nrm(ks[22], (N_ODD, D_MODEL, D_MODEL), D_MODEL ** -0.5),
        "ffn_norm_g": gain(ks[23], (DEPTH, D_MODEL)),
        "w_ffn_gate": nrm(ks[24], (DEPTH, D_MODEL, D_FF), D_MODEL ** -0.5),
        "w_ffn_up": nrm(ks[25], (DEPTH, D_MODEL, D_FF), D_MODEL ** -0.5),
        "w_ffn_down": nrm(ks[26], (DEPTH, D_FF, D_MODEL), D_FF ** -0.5),
        "final_norm_g": gain(ks[27], (D_MODEL,)),
    }


def reference(x_prompt, x_sample, state_ret, state_hgrn, state_s5_re, state_s5_im, attn_norm_g, w_in,
              ret_gn_g, hg_lb, hg_gn_g, w_out, ssm_norm_g, s5_lam_re, s5_lam_im, s5_log_dt, s5_b_re,
              s5_b_im, s5_c_re, s5_c_im, s5_d, w_glu_a, w_glu_b, ffn_norm_g, w_ffn_gate, w_ffn_up,
              w_ffn_down, final_norm_g):
    weights = (attn_norm_g, w_in, ret_gn_g, hg_lb, hg_gn_g, w_out, ssm_norm_g, s5_lam_re, s5_lam_im,
               s5_log_dt, s5_b_re, s5_b_im, s5_c_re, s5_c_im, s5_d, w_glu_a, w_glu_b, ffn_norm_g,
               w_ffn_gate, w_ffn_up, w_ffn_down, final_norm_g)
    fresh = lambda s: jnp.zeros((s.shape[0], BATCH) + s.shape[2:], s.dtype)
    y_prompt, ret_p, hg_p, s5r_p, s5i_p = _trunk(x_prompt, 0.0, fresh(state_ret), fresh(state_hgrn),
                                                 fresh(state_s5_re), fresh(state_s5_im), weights)
    y_sample, ret_s, hg_s, s5r_s, s5i_s = _trunk(x_sample, float(PAST_LEN), state_ret, state_hgrn,
                                                 state_s5_re, state_s5_im, weights)
    return (y_prompt, y_sample, ret_p, ret_s, hg_p, hg_s, s5r_p, s5i_p, s5r_s, s5i_s)
```

```python
import math
from contextlib import ExitStack
import numpy as np
import concourse.bass as bass
import concourse.mybir as mybir
from concourse.bass_utils import run_bass_kernel_spmd

F32, BF16, I32 = mybir.dt.float32, mybir.dt.bfloat16, mybir.dt.int32
AF = mybir.ActivationFunctionType
ALU = mybir.AluOpType

D = 2048
KC = 16
DFF = 5632
FC = 44
NCORES = 8
NS = 16
EPS = 1e-6
TWO_PI = 2.0 * math.pi


class Res:
    __slots__ = ("name", "w", "r")

    def __init__(self, name=""):
        self.name = name
        self.w = None
        self.r = {}


class Prog:
    def __init__(self):
        self.ops = []

    def op(self, eng, fn, reads=(), writes=(), dma=False):
        i = len(self.ops)
        deps = set()
        raw = set()
        for res in reads:
            if res.w is not None:
                deps.add(res.w)
                raw.add(res.w)
        for res in writes:
            if res.w is not None:
                deps.add(res.w)
            deps.update(res.r.values())
        for res in reads:
            res.r[("dma", i) if dma else eng] = i
        for res in writes:
            res.w = i
            res.r = {}
        deps.discard(i)
        self.ops.append([eng, fn, deps, dma, False, None, 0, raw, 0])
        return i

    def emit(self, nc, stack, ndma_sp=10, ndma_pool=8):
        ops = self.ops
        engs = ("pe", "act", "dve", "pool", "sp")
        cnt_ = {e: 0 for e in engs}
        for o in ops:
            o[8] = cnt_[o[0]]
            cnt_[o[0]] += 1
        WIN = 3

        def need(o, j):
            p = ops[j]
            if p[3] or p[0] != o[0]:
                return True
            return (j in o[7]) and (o[8] - p[8] <= WIN) and not o[3]
        for o in ops:
            for j in o[2]:
                if need(o, j):
                    ops[j][4] = True
        csem = {e: stack.enter_context(nc.semaphore("c_" + e)) for e in engs}
        dsem = {"sp": [stack.enter_context(nc.semaphore("d_sp%d" % i)) for i in range(ndma_sp)],
                "pool": [stack.enter_context(nc.semaphore("d_pl%d" % i)) for i in range(ndma_pool)],
                "act": [stack.enter_context(nc.semaphore("d_ac%d" % i)) for i in range(2)]}
        ccount = {e: 0 for e in engs}
        dcount = {e: 0 for e in dsem}
        dval = {e: [0] * len(dsem[e]) for e in dsem}
        prev_on_slot = {}
        streams = {e: [] for e in engs}
        for i, o in enumerate(ops):
            e = o[0]
            streams[e].append(i)
            if o[3]:
                k = dcount[e] % len(dsem[e])
                dcount[e] += 1
                prev_on_slot[i] = (dsem[e][k], dval[e][k])
                dval[e][k] += 16
                o[5] = dsem[e][k]
                o[6] = dval[e][k]
            elif o[4]:
                ccount[e] += 1
                o[5] = csem[e]
                o[6] = ccount[e]
        block = stack.enter_context(nc.Block())
        ssem = {e: stack.enter_context(nc.semaphore("s_" + e)) for e in ("act", "dve", "pool")}
        scnt = {e: 0 for e in ssem}
        prog = self

        def run(engname, eng):
            seen = {}
            prog.cur = engname

            def selfsync(ins):
                scnt[engname] += 1
                ins.then_inc(ssem[engname], 1)
                eng.wait_ge(ssem[engname], scnt[engname])
            prog.selfsync = selfsync

            def wait(sem, val):
                if val <= 0:
                    return
                key = id(sem)
                if seen.get(key, 0) < val:
                    eng.wait_ge(sem, val)
                    seen[key] = val

            for i in streams[engname]:
                o = ops[i]
                for j in sorted(o[2]):
                    p = ops[j]
                    if need(o, j):
                        wait(p[5], p[6])
                if o[3]:
                    s, v = prev_on_slot[i]
                    wait(s, v)
                ins = o[1](eng)
                if o[3]:
                    ins.then_inc(o[5], 16)
                elif o[4]:
                    ins.then_inc(o[5], 1)
            if engname in dsem:
                for k, s in enumerate(dsem[engname]):
                    wait(s, dval[engname][k])

        @block.tensor
        def _(t):
            run("pe", t)

        @block.scalar
        def _(t):
            run("act", t)

        @block.vector
        def _(t):
            run("dve", t)

        @block.gpsimd
        def _(t):
            run("pool", t)

        @block.sync
        def _(t):
            run("sp", t)


class Synced:
    def __init__(self, prog, eng, on=True):
        self.p, self.e, self.pend, self.on = prog, eng, None, on

    def __getattr__(self, name):
        f = getattr(self.e, name)

        def g(*a, **k):
            if self.pend is not None and self.on:
                self.p.selfsync(self.pend)
            self.pend = f(*a, **k)
            return self.pend
        return g


class Rot:
    def __init__(self, items):
        self.items = items
        self.i = 0

    def get(self):
        it = self.items[self.i % len(self.items)]
        self.i += 1
        return it


RET_H, RET_DK = 4, 256
HG_H = 8


def _fm(vec):
    v = np.asarray(vec, np.float32)
    return np.ascontiguousarray(v.reshape(-1, 128).T)


def _wblocks(w, bc):
    K, N = w.shape
    kc = K // 128
    return np.ascontiguousarray(w.reshape(kc, 128, N // bc, bc).transpose(2, 1, 0, 3)).reshape(N // bc, 128, kc * bc)


def _consts(TT):
    cols = []
    ident = np.eye(128, dtype=np.float64)
    cols.append(ident)
    cols.append(np.ones((128, 128)))
    idx = np.arange(128, dtype=np.float64)
    lg = np.log(1.0 - 2.0 ** (-5.0 - np.arange(RET_H, dtype=np.float64)))
    maskT = np.zeros((128, RET_H, 128))
    qdec = np.zeros((128, RET_H, 128))
    kdec = np.zeros((128, RET_H))
    for h in range(RET_H):
        dji = idx[None, :] - idx[:, None]
        maskT[:, h, :] = np.where(dji >= 0, np.exp(dji * lg[h]), 0.0) / 16.0
        qdec[:, h, :] = np.exp((idx + 1.0) * lg[h])[None, :]
        kdec[:, h] = np.exp((127.0 - idx) * lg[h]) / 16.0
    cols.append(maskT.reshape(128, -1))
    cols.append(qdec.reshape(128, -1))
    cols.append(kdec)
    jj = idx[:, None]
    ii = idx[None, :]
    maskH = ((jj // 64 == ii // 64) & (jj <= ii)).astype(np.float64)
    cols.append(maskH)
    half = np.stack([(idx // 64 == 0), (idx // 64 == 1)], 1).astype(np.float64)
    cols.append(half)
    reset = np.ones((128, TT))
    reset[:, ::64] = 0.0
    cols.append(reset)
    cols.append(np.full((128, 1), EPS))
    c = np.concatenate(cols, 1).astype(np.float32)
    return np.ascontiguousarray(c), lg


C_ID, C_ONES, C_MASKT, C_QDEC, C_KDEC, C_MASKH, C_HALF, C_RESET = 0, 128, 256, 768, 1280, 1284, 1412, 1414


def _rot_tables(pos):
    half = 128
    inv = (10000.0 ** (-np.arange(half, dtype=np.float32) / np.float32(half))).astype(np.float32)
    ang = (pos.astype(np.float32)[None, :] * inv[:, None]).astype(np.float32).astype(np.float64)
    c, s = np.cos(ang), np.sin(ang)
    return np.stack([c, s], 0).astype(np.float32)


def build(T, TT, dbg=None, npre=0):
    NT = TT + NS
    NTILES = T // TT
    NPRE = npre
    SS = NPRE
    TOUT = (NTILES - NPRE) * TT
    NB = TT // 128
    consts_np, lg = _consts(TT)
    NCONST = consts_np.shape[1]
    C_EPS = C_RESET + TT
    gam = [float(np.exp(lg[h])) for h in range(RET_H)]
    cdec = [float(np.exp(128.0 * lg[h])) for h in range(RET_H)]

    nc = bass.Bass("TRN2", target_bir_lowering=False)
    stack = ExitStack()
    P = Prog()

    def din(name, shape):
        return nc.dram_tensor(name, list(shape), F32, kind="ExternalInput").ap()

    def dout(name, shape):
        return nc.dram_tensor(name, list(shape), F32, kind="ExternalOutput").ap()

    xp = din("xp", [T, D])
    xs = din("xs", [NS, D])
    st_ret = din("st_ret", [NS, RET_H, 256, 256])
    st_hg = din("st_hg", [NS, HG_H, 128, 128])
    st_s5r = din("st_s5r", [NS, 8192])
    st_s5i = din("st_s5i", [NS, 8192])
    consts = din("consts", [128, NCONST])
    rot = din("rot", [2, 128, T + NS])
    NPF = 16 * 6 + 8 + 8 + 24
    pf = din("pf", [128, NPF])
    s5p = din("s5p", [128, 192 + 4096])
    w_in = din("w_in", [32, 128, 16 * 256])
    w_out = din("w_out", [8, 128, 16 * 256])
    w_ga = din("w_ga", [8, 128, 16 * 256])
    w_gb = din("w_gb", [8, 128, 16 * 256])
    w_fg = din("w_fg", [2, 22, 128, 16 * 256])
    w_fu = din("w_fu", [2, 22, 128, 16 * 256])
    w_fd = din("w_fd", [2, 2, 16, 128, 22 * 128])

    yp = dout("yp", [TOUT, D])
    ys = dout("ys", [NS, D])
    ret_p = dout("ret_p", [RET_H, 256, 256])
    ret_s = dout("ret_s", [NS, RET_H, 256, 256])
    hg_p = dout("hg_p", [HG_H, 128, 128])
    hg_s = dout("hg_s", [NS, HG_H, 128, 128])
    s5r_p = dout("s5r_p", [64, 128])
    s5i_p = dout("s5i_p", [64, 128])
    s5r_s = dout("s5r_s", [NS, 8192])
    s5i_s = dout("s5i_s", [NS, 8192])
    dbg_out = dout("dbg", [128, KC * NT]) if dbg else None

    def sb(name, shape, dt=F32):
        return stack.enter_context(nc.sbuf_tensor(name, list(shape), dt))[:]

    cst = sb("cst", [128, NCONST]); r_cst = Res("cst")
    pft = sb("pft", [128, NPF]); r_pft = Res("pft")
    identf = cst[:, C_ID:C_ID + 128]
    onesf = cst[:, C_ONES:C_ONES + 128]
    epscol = cst[:, C_EPS:C_EPS + 1]
    identb = sb("identb", [128, 128], BF16); r_identb = Res()
    xT = sb("xT", [128, KC, NT]); r_xT = [Res("xT%d" % k) for k in range(KC)]
    hT = sb("hT", [128, KC, NT], BF16); r_hT = Res("hT")
    mixT = sb("mixT", [128, KC, NT], BF16); r_mix = [Res("mix%d" % k) for k in range(KC)]
    NW = 5
    wsl = Rot([(sb("w%d" % i, [128, 4096], BF16), Res("w%d" % i)) for i in range(NW)])
    sq = Rot([(sb("sq%d" % i, [128, NT], BF16), Res()) for i in range(2)])
    onesb = sb("onesb", [128, 128], BF16)
    rstd = sb("rstd", [128, NT]); r_rstd = Res("rstd")
    tmpf = Rot([(sb("tmpf%d" % i, [128, 512]), Res()) for i in range(3)])
    Sret = sb("Sret", [128, RET_H, 2, 256]); r_Sret = [Res() for _ in range(RET_H)]
    Shg = sb("Shg", [128, HG_H, 128]); r_Shg = [Res() for _ in range(HG_H)]
    lB = sb("lB", [128, KC, 2, 128], BF16); r_lB = Res()
    lC = sb("lC", [128, 64, 2, 32], BF16); r_lC = Res()
    A1 = sb("A1", [128, 64, 2]); A2 = sb("A2", [128, 64, 2]); r_A = Res()
    Hst = sb("Hst", [128, 64, 2]); r_Hst = Res()
    A16a = sb("A16a", [128, 64, 2]); A16b = sb("A16b", [128, 64, 2]); r_A16 = Res()
    fsc = sb("fsc", [128, 2]); r_fsc = Res()

    ARENA = 49408
    arena = sb("arena", [128, ARENA // 4])
    arena_tokens = []

    class Carver:
        def __init__(self):
            self.off = 0

        def get(self, shape, dt=F32):
            esz = 4 if dt == F32 or dt == I32 else 2
            n = 1
            for d_ in shape:
                n *= d_
            nb = (n * esz + 3) // 4 * 4
            assert self.off + nb <= ARENA, (self.off, nb)
            v = arena[:, self.off // 4:(self.off + nb) // 4]
            if dt != F32:
                v = v.bitcast(dt)
            v = v[:, :n]
            if len(shape) == 2:
                v = v.rearrange("p (a b) -> p a b", b=shape[1])
            elif len(shape) == 3:
                v = v.rearrange("p (a b c) -> p a b c", b=shape[1], c=shape[2])
            self.off += nb
            return v

    def tok(name=""):
        r = Res(name)
        arena_tokens.append(r)
        return r

    def fence():
        P.op("dve", lambda e: e.memset(fsc[:, 0:1], 0.0), writes=[r_fsc] + arena_tokens)

    cv = Carver()
    xin = Rot([(cv.get([D]), tok("xin%d" % i)) for i in range(4)])
    cv = Carver()
    act = cv.get([22, NT], BF16); r_act = [tok("act%d" % k) for k in range(22)]
    cv = Carver()
    rott = cv.get([2, NT]); r_rot = tok("rot")
    bfA = cv.get([2, NT], BF16); r_bfA = tok()
    bfB = cv.get([2, NT], BF16); r_bfB = tok()
    bfC = cv.get([2, TT], BF16); r_bfC = tok()
    vtm = cv.get([NB, 256], BF16); r_vtm = tok()
    kdtm = cv.get([NB, 256], BF16); r_kdtm = tok()
    oT = cv.get([2, NT]); r_oT = tok()
    gT = cv.get([2, NT], BF16); r_gT = tok()
    attm = Rot([(cv.get([128], BF16), tok()) for i in range(2)])
    SbAllR = cv.get([NB + 1, 2, 256], BF16); r_SbAllR = [tok() for _ in range(NB + 1)]
    vs = cv.get([256], BF16); r_vs = tok()
    ktms = cv.get([256], BF16); r_ktms = tok()
    vmask = Rot([(cv.get([256], BF16), tok()) for i in range(3)])
    s0 = Rot([(cv.get([2, 256]), tok()) for i in range(3)])
    snb = Rot([(cv.get([2, 256], BF16), tok()) for i in range(3)])
    hf = cv.get([2, NT]); r_hf = tok()
    hb = cv.get([2, NT]); r_hb = tok()
    fsm = cv.get([2, NS]); r_fsm = tok()
    hebl = cv.get([2, TT // 64]); r_hebl = tok()
    hblast = cv.get([2, TT // 64]); r_hblast = tok()
    SbAll = cv.get([TT // 64 + 1, 128], BF16); r_SbAll = [tok() for _ in range(TT // 64 + 1)]
    cv = Carver()
    TC = 16
    HBraw = [cv.get([TC, 64, 2]), cv.get([TC, 64, 2])]
    HBs = [x_.rearrange("p t j r -> p j r t") for x_ in HBraw]; r_HBs = [tok("HBa"), tok("HBb")]
    HBb_raw = cv.get([64 * 2 * TC]); r_HBb = tok()
    HBbs = [HBb_raw[:, 0:64 * TC].bitcast(BF16).rearrange("p (j r t) -> p j r t", r=2, t=TC),
            HBb_raw[:, 64 * TC:128 * TC].bitcast(BF16).rearrange("p (j r t) -> p j r t", r=2, t=TC)]
    r_HBbs = [tok("HBba"), tok("HBbb")]
    st1 = cv.get([64, 2]); st2 = cv.get([64, 2]); r_st = tok()
    Pw = [cv.get([64, 2]), cv.get([64, 2])]; cbuf = cv.get([64, 2]); r_pw = tok()
    dcolT = pft[:, 120:136]

    banks = [(stack.enter_context(nc.psum_tensor("ps%d" % i, [128, 512], F32))[:], Res("ps%d" % i)) for i in range(8)]
    psA = Rot(banks[0:4])
    psB = Rot(banks[4:6])
    psC = Rot(banks[6:8])

    g_attn = pft[:, 0:16]; g_ffn0 = pft[:, 16:32]; g_ssm = pft[:, 32:48]; g_ffn1 = pft[:, 48:64]; g_fin = pft[:, 64:80]
    g_retgn = pft[:, 80:88]; g_hggn = pft[:, 88:96]; lbraw = pft[:, 96:120]

    groups0 = [(0, TT), (TT, NS)]

    P.op("sp", lambda e: e.dma_start(out=cst, in_=consts), writes=[r_cst], dma=True)
    P.op("sp", lambda e: e.dma_start(out=pft, in_=pf), writes=[r_pft], dma=True)
    P.op("dve", lambda e: e.tensor_copy(out=identb, in_=identf), reads=[r_cst], writes=[r_identb])
    P.op("dve", lambda e: e.tensor_copy(out=onesb, in_=onesf), reads=[r_cst], writes=[r_identb])
    for h in range(RET_H):
        P.op("dve", lambda e, h=h: e.memset(Sret[:, h], 0.0), writes=[r_Sret[h]])
    for h in range(HG_H):
        P.op("dve", lambda e, h=h: e.memset(Shg[:, h], 0.0), writes=[r_Shg[h]])
    P.op("dve", lambda e: e.memset(Hst, 0.0), writes=[r_Hst])
    lbe = sb("lbe", [128, 24]); lbc = sb("lbc", [128, 8]); oml = sb("oml", [128, 8]); r_lb = Res()
    P.op("act", lambda e: e.activation(out=lbe, in_=lbraw, func=AF.Exp), reads=[r_pft], writes=[r_lb])

    def _lb(e):
        e = Synced(P, e)
        e.tensor_tensor(out=lbc, in0=lbe[:, 0:8], in1=lbe[:, 8:16], op=ALU.add)
        e.tensor_tensor(out=lbc, in0=lbc, in1=lbe[:, 16:24], op=ALU.add)
        e.reciprocal(out=lbc, in_=lbc)
        e.tensor_tensor(out=lbc, in0=lbc, in1=lbe[:, 0:8], op=ALU.mult)
        return e.tensor_scalar(out=oml, in0=lbc, scalar1=-1.0, scalar2=1.0, op0=ALU.mult, op1=ALU.add)
    P.op("dve", _lb, reads=[r_lb], writes=[r_lb])

    def load_w(dram_blk, nelem):
        ap, res = wsl.get()
        ee = 2048 if nelem % 2048 == 0 else nelem // 2
        src = dram_blk.rearrange("p (s e) -> p s e", e=ee)
        dst = ap[:, :nelem].rearrange("p (s e) -> p s e", e=ee)
        P.op("pool", lambda e: e.dma_start(out=dst, in_=src), writes=[res], dma=True)
        return ap, res

    def groups(it):
        return groups0 if it == SS else [(0, TT)]

    def dense_ps(wap, wres, kcn, bc, m, rhsT, rhs_res, c0, n, pool, msz=128):
        ps, pres = pool.get()
        wv = wap[:, :kcn * bc].rearrange("p (k c) -> p k c", c=bc)

        def f(e):
            for kc in range(kcn):
                ins = e.matmul(ps[:msz, :n], lhsT=wv[:, kc, m * 128:m * 128 + msz], rhs=rhsT[:, kc, c0:c0 + n],
                               start=(kc == 0), stop=(kc == kcn - 1))
            return ins
        P.op("pe", f, reads=[wres] + list(rhs_res), writes=[pres])
        return ps, pres

    def rmsnorm(gcols, out_ap, out_res_list, grps, in_place=False):
        stats = []
        for (c0, n) in grps:
            ps, pres = psC.get()
            for kc in range(KC):
                s_ap, s_res = sq.get()
                P.op("act", lambda e, s_ap=s_ap, kc=kc, c0=c0, n=n: e.activation(out=s_ap[:, :n], in_=xT[:, kc, c0:c0 + n], func=AF.Square),
                     reads=[r_xT[kc]], writes=[s_res])
                P.op("pe", lambda e, ps=ps, s_ap=s_ap, kc=kc, n=n: e.matmul(ps[:, :n], lhsT=onesb, rhs=s_ap[:, :n], start=(kc == 0), stop=(kc == KC - 1)),
                     reads=[s_res, r_cst, r_identb], writes=[pres])
            stats.append((ps, pres, c0, n))
        for (ps, pres, c0, n) in stats:
            P.op("act", lambda e, ps=ps, c0=c0, n=n: e.activation(out=rstd[:, c0:c0 + n], in_=ps[:, :n], func=AF.Sqrt, scale=1.0 / D, bias=epscol),
                 reads=[pres, r_cst], writes=[r_rstd])
            P.op("dve", lambda e, c0=c0, n=n: e.reciprocal(out=rstd[:, c0:c0 + n], in_=rstd[:, c0:c0 + n]), reads=[r_rstd], writes=[r_rstd])
        ntot = grps[-1][0] + grps[-1][1]
        for kc in range(KC):
            wr = [out_res_list[kc]] if len(out_res_list) > 1 else list(out_res_list)
            P.op("dve", lambda e, kc=kc: e.scalar_tensor_tensor(out=out_ap[:, kc, :ntot], in0=xT[:, kc, :ntot], scalar=gcols[:, kc:kc + 1],
                                                                in1=rstd[:, :ntot], op0=ALU.mult, op1=ALU.mult),
                 reads=[r_xT[kc], r_rstd, r_pft], writes=wr)

    def add_resid(ps, pres, oc, c0, n):
        P.op("dve", lambda e: e.tensor_tensor(out=xT[:, oc, c0:c0 + n], in0=xT[:, oc, c0:c0 + n], in1=ps[:, :n], op=ALU.add),
             reads=[pres, r_xT[oc]], writes=[r_xT[oc]])

    def ffn(l, grps):
        rmsnorm(g_ffn0 if l == 0 else g_ffn1, hT, [r_hT], grps)
        for half in range(2):
            for blk in range(11):
                wg, wgr = load_w(w_fg[l, half * 11 + blk], 4096)
                wu, wur = load_w(w_fu[l, half * 11 + blk], 4096)
                for m in range(2):
                    c = blk * 2 + m
                    for (c0, n) in grps:
                        pg, pgr = dense_ps(wg, wgr, KC, 256, m, hT, [r_hT], c0, n, psA)
                        pu, pur = dense_ps(wu, wur, KC, 256, m, hT, [r_hT], c0, n, psA)
                        t_ap, t_res = tmpf.get()
                        P.op("act", lambda e, t_ap=t_ap, pg=pg, n=n: e.activation(out=t_ap[:, :n], in_=pg[:, :n], func=AF.Silu),
                             reads=[pgr], writes=[t_res])
                        P.op("dve", lambda e, t_ap=t_ap, pu=pu, c=c, c0=c0, n=n: e.tensor_tensor(out=act[:, c, c0:c0 + n], in0=t_ap[:, :n], in1=pu[:, :n], op=ALU.mult),
                             reads=[t_res, pur], writes=[r_act[c]])
            for oc in range(KC):
                wd, wdr = load_w(w_fd[l, half, oc], 22 * 128)
                for (c0, n) in grps:
                    ps, pres = dense_ps(wd, wdr, 22, 128, 0, act, r_act, c0, n, psA)
                    add_resid(ps, pres, oc, c0, n)

    dd_list = []

    def dd(name, ap, res_list):
        if not dbg or "dd" not in dbg:
            return
        shp = list(ap.shape)
        n = 1
        for d_ in shp[1:]:
            n *= d_
        t = nc.dram_tensor("dd_" + name, shp, ap.dtype, kind="ExternalOutput").ap()
        P.op("sp", lambda e: e.dma_start(out=t, in_=ap), reads=list(res_list), dma=True)

    def dump_dbg():
        P.op("sp", lambda e: e.dma_start(out=dbg_out, in_=xT.rearrange("p k n -> p (k n)")), reads=r_xT, dma=True)

    def bc_mid(ap2, n):
        a = ap2.ap
        return bass.AP(ap2.tensor, ap2.offset, [list(a[0]), [0, n], list(a[1])])

    def bc_last(apx, n):
        a = apx.ap
        return bass.AP(apx.tensor, apx.offset, [list(x) for x in a] + [[0, n]])

    def bank_bf(ps):
        return ps.bitcast(BF16)

    PI = math.pi

    def s5_setup():
        cvs = Carver()
        raw = cvs.get([192 + 4096]); r_raw = tok("s5raw")
        E = [cvs.get([KC * 128]), cvs.get([KC * 128])]; r_E = tok("E")
        Bb = [cvs.get([64, 16]), cvs.get([64, 16])]; r_Bb = tok("Bb")
        sm = cvs.get([16, 64]); r_sm = tok("sm")
        ki = cvs.get([64], I32)
        lr, li, ldt = raw[:, 0:64], raw[:, 64:128], raw[:, 128:192]
        br = raw[:, 192:1216].rearrange("p (j c) -> p j c", c=16)
        bi = raw[:, 1216:2240].rearrange("p (j c) -> p j c", c=16)
        cre = raw[:, 2240:3264].rearrange("p (j c) -> p j c", c=16)
        cim = raw[:, 3264:4288].rearrange("p (j c) -> p j c", c=16)
        P.op("sp", lambda e: e.dma_start(out=raw, in_=s5p), writes=[r_raw], dma=True)
        dt, z, p_, ang, r_, sin_a, cos_a, arm1, mag, ai, den, cr, ci, t1, t2, ar = [sm[:, i, :] for i in range(16)]
        P.op("act", lambda e: e.activation(out=dt, in_=ldt, func=AF.Exp), reads=[r_raw], writes=[r_sm])

        def reduce_sin(e, src_ang, shift, dst):
            e.tensor_scalar(out=t1, in0=src_ang, scalar1=shift, scalar2=1.0 / TWO_PI, op0=ALU.add, op1=ALU.mult)
            e.tensor_copy(out=ki, in_=t1)
            e.tensor_copy(out=t2, in_=ki)
            e.tensor_scalar(out=t1, in0=src_ang, scalar1=shift, scalar2=None, op0=ALU.add)
            e.scalar_tensor_tensor(out=dst, in0=t2, scalar=-TWO_PI, in1=t1, op0=ALU.mult, op1=ALU.add)
            return e.tensor_scalar(out=dst, in0=dst, scalar1=PI, scalar2=-PI, op0=ALU.min, op1=ALU.max)

        def f1(e):
            e = Synced(P, e)
            e.tensor_tensor(out=z, in0=lr, in1=dt, op=ALU.mult)
            e.tensor_scalar(out=p_, in0=z, scalar1=1.0 / 120, scalar2=1.0 / 24, op0=ALU.mult, op1=ALU.add)
            for cst_ in (1.0 / 6, 0.5, 1.0):
                e.tensor_tensor(out=p_, in0=p_, in1=z, op=ALU.mult)
                e.tensor_scalar(out=p_, in0=p_, scalar1=cst_, scalar2=None, op0=ALU.add)
            e.tensor_tensor(out=p_, in0=p_, in1=z, op=ALU.mult)
            e.tensor_tensor(out=ang, in0=li, in1=dt, op=ALU.mult)
            reduce_sin(e, ang, 0.0, r_)
            return reduce_sin(e, ang, PI / 2, arm1)
        P.op("dve", f1, reads=[r_raw, r_sm], writes=[r_sm])

        def f2(e):
            e.activation(out=sin_a, in_=r_, func=AF.Sin)
            return e.activation(out=cos_a, in_=arm1, func=AF.Sin)
        P.op("act", f2, reads=[r_sm], writes=[r_sm])

        def f3(e):
            e = Synced(P, e)
            e.tensor_tensor(out=t1, in0=p_, in1=cos_a, op=ALU.mult)
            e.tensor_scalar(out=t2, in0=cos_a, scalar1=-1.0, scalar2=None, op0=ALU.add)
            e.tensor_tensor(out=arm1, in0=t1, in1=t2, op=ALU.add)
            e.tensor_scalar(out=ar, in0=arm1, scalar1=1.0, scalar2=None, op0=ALU.add)
            e.tensor_scalar(out=mag, in0=p_, scalar1=1.0, scalar2=None, op0=ALU.add)
            e.tensor_tensor(out=ai, in0=mag, in1=sin_a, op=ALU.mult)
            e.tensor_tensor(out=den, in0=lr, in1=lr, op=ALU.mult)
            e.tensor_tensor(out=t1, in0=li, in1=li, op=ALU.mult)
            e.tensor_tensor(out=den, in0=den, in1=t1, op=ALU.add)
            e.reciprocal(out=den, in_=den)
            e.tensor_tensor(out=t1, in0=arm1, in1=lr, op=ALU.mult)
            e.tensor_tensor(out=t2, in0=ai, in1=li, op=ALU.mult)
            e.tensor_tensor(out=t1, in0=t1, in1=t2, op=ALU.add)
            e.tensor_tensor(out=cr, in0=t1, in1=den, op=ALU.mult)
            e.tensor_tensor(out=t1, in0=ai, in1=lr, op=ALU.mult)
            e.tensor_tensor(out=t2, in0=arm1, in1=li, op=ALU.mult)
            e.tensor_tensor(out=t1, in0=t1, in1=t2, op=ALU.subtract)
            e.tensor_tensor(out=ci, in0=t1, in1=den, op=ALU.mult)
            e.tensor_copy(out=A1[:, :, 0], in_=ar)
            e.tensor_copy(out=A1[:, :, 1], in_=ar)
            e.tensor_scalar(out=A2[:, :, 0], in0=ai, scalar1=-1.0, scalar2=None, op0=ALU.mult)
            e.tensor_copy(out=A2[:, :, 1], in_=ai)
            crb, cib = bc_last(cr, 16), bc_last(ci, 16)
            Er = E[0].rearrange("p (j g c) -> p j g c", g=2, c=16)
            Ei = E[1].rearrange("p (j g c) -> p j g c", g=2, c=16)
            e.tensor_tensor(out=Bb[0], in0=br, in1=crb, op=ALU.mult)
            e.tensor_tensor(out=Bb[1], in0=bi, in1=cib, op=ALU.mult)
            e.tensor_tensor(out=Bb[0], in0=Bb[0], in1=Bb[1], op=ALU.subtract)
            e.tensor_tensor(out=Bb[1], in0=bi, in1=crb, op=ALU.mult)
            for g2 in range(2):
                e.tensor_scalar(out=Er[:, :, g2, :], in0=Bb[0], scalar1=cst[:, C_HALF + g2:C_HALF + g2 + 1], scalar2=None, op0=ALU.mult)
            e.tensor_tensor(out=Bb[0], in0=br, in1=cib, op=ALU.mult)
            e.tensor_tensor(out=Bb[1], in0=Bb[1], in1=Bb[0], op=ALU.add)
            for g2 in range(2):
                e.tensor_scalar(out=Ei[:, :, g2, :], in0=Bb[1], scalar1=cst[:, C_HALF + g2:C_HALF + g2 + 1], scalar2=None, op0=ALU.mult)
            lC5 = lC.rearrange("p j r (g c) -> p j r g c", c=16)
            for ri, (cc, sgn) in enumerate(((cre, 1.0), (cim, -1.0))):
                for g2 in range(2):
                    ins = e.tensor_scalar(out=lC5[:, :, ri, g2, :], in0=cc, scalar1=cst[:, C_HALF + g2:C_HALF + g2 + 1], scalar2=sgn, op0=ALU.mult, op1=ALU.mult)
            return ins
        P.op("dve", f3, reads=[r_raw, r_sm, r_cst], writes=[r_sm, r_A, r_Bb, r_E, r_lC])
        for ri in range(2):
            for g in range(4):
                ps, pres = psB.get()

                def ft(e, ps=ps, ri=ri, g=g):
                    for q in range(4):
                        kc = g * 4 + q
                        ins = e.transpose(ps[:, q * 128:(q + 1) * 128], E[ri][:, kc * 128:(kc + 1) * 128], identf)
                    return ins
                P.op("pe", ft, reads=[r_E, r_cst], writes=[pres])
                P.op("act", lambda e, ps=ps, ri=ri, g=g: e.activation(out=lB[:, g * 4:g * 4 + 4, ri, :], in_=ps.rearrange("p (q c) -> p q c", c=128), func=AF.Copy),
                     reads=[pres], writes=[r_lB])

    s5_setup()
    fence()

    def ret_head(h, it, grps, last):
        samp = (it == SS)
        ntot = grps[-1][0] + grps[-1][1]
        wq, wqr = load_w(w_in[h], 4096)
        wk, wkr = load_w(w_in[4 + h], 4096)
        wv, wvr = load_w(w_in[8 + h], 4096)
        wg, wgr = load_w(w_in[12 + h], 4096)
        for (w, wr, dst, dres) in ((wq, wqr, bfA, r_bfA), (wk, wkr, bfB, r_bfB)):
            for (c0, n) in grps:
                p1, p1r = dense_ps(w, wr, KC, 256, 0, hT, [r_hT], c0, n, psA)
                p2, p2r = dense_ps(w, wr, KC, 256, 1, hT, [r_hT], c0, n, psA)
                ta, tar = tmpf.get()
                tb_, tbr = tmpf.get()

                def fr(e, p1=p1, p2=p2, ta=ta, tb_=tb_, dst=dst, c0=c0, n=n):
                    e = Synced(P, e, n < 300)
                    cos, sin = rott[:, 0, c0:c0 + n], rott[:, 1, c0:c0 + n]
                    e.tensor_tensor(out=ta[:, :n], in0=p1[:, :n], in1=cos, op=ALU.mult)
                    e.tensor_tensor(out=tb_[:, :n], in0=p2[:, :n], in1=sin, op=ALU.mult)
                    e.tensor_tensor(out=dst[:, 0, c0:c0 + n], in0=ta[:, :n], in1=tb_[:, :n], op=ALU.subtract)
                    e.tensor_tensor(out=ta[:, :n], in0=p1[:, :n], in1=sin, op=ALU.mult)
                    e.tensor_tensor(out=tb_[:, :n], in0=p2[:, :n], in1=cos, op=ALU.mult)
                    return e.tensor_tensor(out=dst[:, 1, c0:c0 + n], in0=ta[:, :n], in1=tb_[:, :n], op=ALU.add)
                P.op("dve", fr, reads=[p1r, p2r, r_rot], writes=[tar, tbr, dres])
        qdtab = bc_mid(cst[:, C_QDEC + h * 128:C_QDEC + (h + 1) * 128], NB)

        def fqd(e):
            for dc in range(2):
                ins = e.tensor_tensor(out=bfC[:, dc, :].rearrange("p (b c) -> p b c", c=128),
                                      in0=bfA[:, dc, 0:TT].rearrange("p (b c) -> p b c", c=128), in1=qdtab, op=ALU.mult)
            return ins
        P.op("pool", fqd, reads=[r_bfA, r_cst], writes=[r_bfC])
        for tb in range(NB):
            ps, pres = psB.get()

            def fv(e, ps=ps, tb=tb):
                wv3 = wv.rearrange("p (k c) -> p k c", c=256)
                for kc in range(KC):
                    ins = e.matmul(ps[:, :256], lhsT=hT[:, kc, tb * 128:(tb + 1) * 128], rhs=wv3[:, kc, :], start=(kc == 0), stop=(kc == KC - 1))
                return ins
            P.op("pe", fv, reads=[wvr, r_hT], writes=[pres])
            P.op("act", lambda e, ps=ps, tb=tb: e.activation(out=vtm[:, tb, :], in_=ps[:, :256], func=AF.Copy), reads=[pres], writes=[r_vtm])
        if samp:
            ps, pres = psB.get()

            def fvs(e, ps=ps):
                wv3 = wv.rearrange("p (k c) -> p k c", c=256)
                for kc in range(KC):
                    ins = e.matmul(ps[:NS, :256], lhsT=hT[:, kc, TT:NT], rhs=wv3[:, kc, :], start=(kc == 0), stop=(kc == KC - 1))
                return ins
            P.op("pe", fvs, reads=[wvr, r_hT], writes=[pres])
            P.op("act", lambda e, ps=ps: e.activation(out=vs[:NS, :], in_=ps[:NS, :256], func=AF.Copy, scale=1.0 / 16), reads=[pres], writes=[r_vs])
        gate_jobs = []
        for vc in range(2):
            for (c0, n) in grps:
                def gj(vc=vc, c0=c0, n=n):
                    ps, pres = dense_ps(wg, wgr, KC, 256, vc, hT, [r_hT], c0, n, psA)
                    P.op("act", lambda e, ps=ps, vc=vc, c0=c0, n=n: e.activation(out=gT[:, vc, c0:c0 + n], in_=ps[:, :n], func=AF.Silu), reads=[pres], writes=[r_gT])
                gate_jobs.append(gj)
        for tb in range(NB):
            ps, pres = psC.get()
            pb = bank_bf(ps)

            def fk(e, pb=pb, tb=tb):
                for dc in range(2):
                    ins = e.transpose(pb[:, dc * 128:(dc + 1) * 128], bfB[:, dc, tb * 128:(tb + 1) * 128], identb)
                return ins
            P.op("pe", fk, reads=[r_bfB, r_identb], writes=[pres])
            P.op("act", lambda e, pb=pb, tb=tb: e.activation(out=kdtm[:, tb, :], in_=pb[:, 0:256], func=AF.Copy, scale=cst[:, C_KDEC + h:C_KDEC + h + 1]),
                 reads=[pres, r_cst], writes=[r_kdtm])
        if samp:
            ps, pres = psC.get()
            pb = bank_bf(ps)

            def fks(e, pb=pb):
                for dc in range(2):
                    ins = e.transpose(pb[:NS, dc * 128:(dc + 1) * 128], bfB[:, dc, TT:NT], identb)
                return ins
            P.op("pe", fks, reads=[r_bfB, r_identb], writes=[pres])
            P.op("dve", lambda e, pb=pb: e.tensor_copy(out=ktms[:NS, :], in_=pb[:NS, 0:256]), reads=[pres], writes=[r_ktms])
        if h == 0 and it == 0:
            dd("r_qT", bfA, [r_bfA]); dd("r_kT", bfB, [r_bfB]); dd("r_vs", vs[:NS, :], [r_vs]); dd("r_ktms", ktms[:NS, :], [r_ktms])
            dd("r_vtm", vtm, [r_vtm]); dd("r_kdtm", kdtm, [r_kdtm]); dd("r_hT", hT, [r_hT])
            dd("rott", rott, [r_rot]); dd("lbe", lbe, [r_lb]); dd("lbc", lbc, [r_lb]); dd("oml", oml, [r_lb]); dd("pft", pft, [r_pft])
        P.op("act", lambda e: e.activation(out=SbAllR[:, 0], in_=Sret[:, h], func=AF.Copy), reads=[r_Sret[h]], writes=[r_SbAllR[0]])
        for tb in range(NB):
            for dc in range(2):
                pss, pssr = psC.get()
                P.op("pe", lambda e, pss=pss, tb=tb, dc=dc: e.matmul(pss[:, :256], lhsT=kdtm[:, tb, dc * 128:(dc + 1) * 128], rhs=vtm[:, tb, :], start=True, stop=True),
                     reads=[r_kdtm, r_vtm], writes=[pssr])
                P.op("dve", lambda e, pss=pss, dc=dc: e.scalar_tensor_tensor(out=Sret[:, h, dc, :], in0=Sret[:, h, dc, :], scalar=cdec[h], in1=pss[:, :256], op0=ALU.mult, op1=ALU.add),
                     reads=[pssr, r_Sret[h]], writes=[r_Sret[h]])
            if tb < NB - 1:
                P.op("act", lambda e, tb=tb: e.activation(out=SbAllR[:, tb + 1], in_=Sret[:, h], func=AF.Copy), reads=[r_Sret[h]], writes=[r_SbAllR[tb + 1]])
        mk = cst[:, C_MASKT + h * 128:C_MASKT + (h + 1) * 128]
        for tb in range(NB):
            cs = slice(tb * 128, (tb + 1) * 128)
            ps, pres = psB.get()

            def fa(e, ps=ps, cs=cs):
                for dc in range(2):
                    ins = e.matmul(ps[:, :128], lhsT=bfB[:, dc, cs], rhs=bfA[:, dc, cs], start=(dc == 0), stop=(dc == 1))
                return ins
            P.op("pe", fa, reads=[r_bfA, r_bfB], writes=[pres])
            am, amr = attm.get()
            P.op("dve", lambda e, ps=ps, am=am: e.tensor_tensor(out=am, in0=ps[:, :128], in1=mk, op=ALU.mult), reads=[pres, r_cst], writes=[amr])
            if gate_jobs:
                gate_jobs.pop(0)()
            po, por = psB.get()

            def fo(e, po=po, am=am, tb=tb, cs=cs):
                for vc in range(2):
                    e.matmul(po[:, vc * 128:(vc + 1) * 128], lhsT=vtm[:, tb, vc * 128:(vc + 1) * 128], rhs=am, start=True, stop=False)
                    for dc in range(2):
                        ins = e.matmul(po[:, vc * 128:(vc + 1) * 128], lhsT=SbAllR[:, tb, dc, vc * 128:(vc + 1) * 128], rhs=bfC[:, dc, cs], start=False, stop=(dc == 1))
                return ins
            P.op("pe", fo, reads=[amr, r_vtm, r_SbAllR[tb], r_bfC], writes=[por])
            P.op("act", lambda e, po=po, cs=cs: e.activation(out=oT[:, :, cs], in_=po[:, :256].rearrange("p (v c) -> p v c", c=128), func=AF.Copy), reads=[por], writes=[r_oT])
        while gate_jobs:
            gate_jobs.pop(0)()
        if last:
            P.op("sp", lambda e: e.dma_start(out=ret_p[h].rearrange("(dc p) v -> p dc v", p=128), in_=Sret[:, h]), reads=[r_Sret[h]], dma=True)
        if samp:
            pre = {}

            def ld_ret(b):
                sa, sar = s0.get()
                P.op("sp", lambda e, sa=sa, b=b: e.dma_start(out=sa, in_=st_ret[b, h].rearrange("(dc p) v -> p dc v", p=128)), writes=[sar], dma=True)
                pre[b] = (sa, sar)
            prevm = {}

            def mk_vm(b):
                vm, vmr = vmask.get()
                P.op("act", lambda e, vm=vm, b=b: e.activation(out=vm[:NS, :], in_=vs[:NS, :], func=AF.Copy, scale=identf[:NS, b:b + 1]),
                     reads=[r_vs, r_cst], writes=[vmr])
                prevm[b] = (vm, vmr)
            ld_ret(0)
            ld_ret(1)
            mk_vm(0)
            mk_vm(1)
            for b in range(NS):
                if b + 2 < NS:
                    ld_ret(b + 2)
                    mk_vm(b + 2)
                vm, vmr = prevm.pop(b)
                sa, sar = pre.pop(b)
                for dc in range(2):
                    pss, pssr = psC.get()
                    P.op("pe", lambda e, pss=pss, vm=vm, dc=dc: e.matmul(pss[:, :256], lhsT=ktms[:NS, dc * 128:(dc + 1) * 128], rhs=vm[:NS, :], start=True, stop=True),
                         reads=[r_ktms, vmr], writes=[pssr])
                    P.op("dve", lambda e, pss=pss, sa=sa, dc=dc: e.scalar_tensor_tensor(out=sa[:, dc, :], in0=sa[:, dc, :], scalar=gam[h], in1=pss[:, :256], op0=ALU.mult, op1=ALU.add),
                         reads=[pssr, sar], writes=[sar])
                P.op("sp", lambda e, sa=sa, b=b: e.dma_start(out=ret_s[b, h].rearrange("(dc p) v -> p dc v", p=128), in_=sa), reads=[sar], dma=True)
                sn, snr = snb.get()
                P.op("act", lambda e, sn=sn, sa=sa: e.activation(out=sn, in_=sa, func=AF.Copy), reads=[sar], writes=[snr])

                def part2(sn=sn, snr=snr, b=b):
                    po, por = psB.get()

                    def fso(e, po=po, sn=sn, b=b):
                        for vc in range(2):
                            for dc in range(2):
                                ins = e.matmul(po[:, vc:vc + 1], lhsT=sn[:, dc, vc * 128:(vc + 1) * 128], rhs=bfA[:, dc, TT + b:TT + b + 1], start=(dc == 0), stop=(dc == 1))
                        return ins
                    P.op("pe", fso, reads=[snr, r_bfA], writes=[por])
                    P.op("dve", lambda e, po=po, b=b: e.tensor_copy(out=oT[:, :, TT + b], in_=po[:, 0:2]), reads=[por], writes=[r_oT])
                if b > 0:
                    prev2()
                prev2 = part2
            prev2()
        if h == 0 and it == 0:
            dd("r_oT", oT, [r_oT]); dd("r_gT", gT, [r_gT])
        for (c0, n) in grps:
            psm, psmr = psC.get()
            P.op("pe", lambda e, psm=psm, c0=c0, n=n: [e.matmul(psm[:, :n], lhsT=onesf, rhs=oT[:, vc, c0:c0 + n], start=(vc == 0), stop=(vc == 1)) for vc in range(2)][-1],
                 reads=[r_oT, r_cst], writes=[psmr])
            psq, psqr = psC.get()
            for vc in range(2):
                s_ap, s_res = sq.get()
                P.op("act", lambda e, s_ap=s_ap, vc=vc, c0=c0, n=n: e.activation(out=s_ap[:, :n], in_=oT[:, vc, c0:c0 + n], func=AF.Square), reads=[r_oT], writes=[s_res])
                P.op("pe", lambda e, psq=psq, s_ap=s_ap, vc=vc, n=n: e.matmul(psq[:, :n], lhsT=onesb, rhs=s_ap[:, :n], start=(vc == 0), stop=(vc == 1)),
                     reads=[s_res, r_cst, r_identb], writes=[psqr])
            mean, meanr = tmpf.get()
            rs, rsr = tmpf.get()
            P.op("act", lambda e, mean=mean, psm=psm, n=n: e.activation(out=mean[:, :n], in_=psm[:, :n], func=AF.Copy, scale=1.0 / 256), reads=[psmr], writes=[meanr])

            def fvar(e, mean=mean, rs=rs, psq=psq, n=n):
                e = Synced(P, e, n < 300)
                e.tensor_tensor(out=rs[:, :n], in0=mean[:, :n], in1=mean[:, :n], op=ALU.mult)
                return e.scalar_tensor_tensor(out=rs[:, :n], in0=psq[:, :n], scalar=1.0 / 256, in1=rs[:, :n], op0=ALU.mult, op1=ALU.subtract)
            P.op("dve", fvar, reads=[meanr, psqr], writes=[rsr])
            P.op("act", lambda e, rs=rs, n=n: e.activation(out=rs[:, :n], in_=rs[:, :n], func=AF.Sqrt, bias=epscol), reads=[rsr, r_cst], writes=[rsr])

            def fgn(e, mean=mean, rs=rs, c0=c0, n=n):
                e = Synced(P, e, n < 300)
                e.reciprocal(out=rs[:, :n], in_=rs[:, :n])
                for vc in range(2):
                    o_ = oT[:, vc, c0:c0 + n]
                    e.tensor_tensor(out=o_, in0=o_, in1=mean[:, :n], op=ALU.subtract)
                    e.tensor_tensor(out=o_, in0=o_, in1=rs[:, :n], op=ALU.mult)
                    ins = e.scalar_tensor_tensor(out=mixT[:, h * 2 + vc, c0:c0 + n], in0=o_, scalar=g_retgn[:, h * 2 + vc:h * 2 + vc + 1], in1=gT[:, vc, c0:c0 + n],
                                                 op0=ALU.mult, op1=ALU.mult)
                return ins
            P.op("dve", fgn, reads=[meanr, rsr, r_oT, r_gT, r_pft], writes=[rsr, r_oT, r_mix[h * 2], r_mix[h * 2 + 1]])

    def hg_pair(hp, it, grps, last):
        samp = (it == SS)
        ntot = grps[-1][0] + grps[-1][1]
        NCH = TT // 64
        wq, wqr = load_w(w_in[16 + hp], 4096)
        wf, wfr = load_w(w_in[20 + hp], 4096)
        wi, wir = load_w(w_in[24 + hp], 4096)
        wg, wgr = load_w(w_in[28 + hp], 4096)
        resetm = cst[:, C_RESET:C_RESET + TT]
        for m in range(2):
            hd = hp * 2 + m
            for (c0, n) in grps:
                ps, pres = dense_ps(wf, wfr, KC, 256, m, hT, [r_hT], c0, n, psA)
                ta, tar = tmpf.get()
                P.op("act", lambda e, ps=ps, ta=ta, n=n: e.activation(out=ta[:, :n], in_=ps[:, :n], func=AF.Sigmoid), reads=[pres], writes=[tar])
                P.op("dve", lambda e, ta=ta, m=m, hd=hd, c0=c0, n=n: e.tensor_scalar(out=hf[:, m, c0:c0 + n], in0=ta[:, :n], scalar1=oml[:, hd:hd + 1], scalar2=lbc[:, hd:hd + 1],
                                                                                       op0=ALU.mult, op1=ALU.add), reads=[tar, r_lb], writes=[r_hf])
            ta, tar = tmpf.get()
            P.op("act", lambda e, ta=ta, m=m: e.activation(out=ta[:, :TT], in_=hf[:, m, 0:TT], func=AF.Ln), reads=[r_hf], writes=[tar])
            P.op("dve", lambda e, ta=ta, m=m: e.tensor_tensor_scan(out=hb[:, m, 0:TT], data0=resetm, data1=ta[:, :TT], initial=0.0, op0=ALU.mult, op1=ALU.add),
                 reads=[tar, r_cst], writes=[r_hb])
            if samp:
                P.op("dve", lambda e, m=m: e.tensor_copy(out=fsm[:, m, :], in_=hf[:, m, TT:NT]), reads=[r_hf], writes=[r_fsm])
            P.op("dve", lambda e, m=m: e.tensor_copy(out=hblast[:, m, :], in_=hb[:, m, 63:TT:64]), reads=[r_hb], writes=[r_hblast])
            P.op("act", lambda e, m=m: e.activation(out=hebl[:, m, :], in_=hblast[:, m, :], func=AF.Exp), reads=[r_hblast], writes=[r_hebl])
            P.op("dve", lambda e, m=m: e.tensor_scalar(out=hf[:, m, :ntot], in0=hf[:, m, :ntot], scalar1=-1.0, scalar2=1.0, op0=ALU.mult, op1=ALU.add),
                 reads=[r_hf, r_fsm], writes=[r_hf])
            for (c0, n) in grps:
                ps, pres = dense_ps(wq, wqr, KC, 256, m, hT, [r_hT], c0, n, psA)
                if c0 == 0:
                    ta, tar = tmpf.get()
                    tb_, tbr = tmpf.get()
                    P.op("act", lambda e, ta=ta, m=m: e.activation(out=ta[:, :TT], in_=hb[:, m, 0:TT], func=AF.Exp), reads=[r_hb], writes=[tar])
                    P.op("act", lambda e, tb_=tb_, ps=ps: e.activation(out=tb_[:, :TT], in_=ps[:, :TT], func=AF.Silu), reads=[pres], writes=[tbr])
                    P.op("dve", lambda e, ta=ta, tb_=tb_, m=m: e.tensor_tensor(out=bfA[:, m, 0:TT], in0=ta[:, :TT], in1=tb_[:, :TT], op=ALU.mult), reads=[tar, tbr], writes=[r_bfA])
                else:
                    P.op("act", lambda e, ps=ps, m=m: e.activation(out=bfA[:, m, TT:NT], in_=ps[:, :NS], func=AF.Silu), reads=[pres], writes=[r_bfA])
            ta, tar = tmpf.get()
            P.op("act", lambda e, ta=ta, m=m: e.activation(out=ta[:, :TT], in_=hb[:, m, 0:TT], func=AF.Exp, scale=-1.0), reads=[r_hb], writes=[tar])
            P.op("dve", lambda e, ta=ta, m=m: e.tensor_tensor(out=bfB[:, m, 0:TT], in0=hf[:, m, 0:TT], in1=ta[:, :TT], op=ALU.mult), reads=[tar, r_hf], writes=[r_bfB])
            if samp:
                P.op("dve", lambda e, m=m: e.tensor_copy(out=bfB[:, m, TT:NT], in_=hf[:, m, TT:NT]), reads=[r_hf], writes=[r_bfB])
            ta, tar = tmpf.get()

            def fkd(e, ta=ta, m=m):
                for ch in range(NCH):
                    ins = e.activation(out=ta[:, ch * 64:(ch + 1) * 64], in_=hb[:, m, ch * 64:(ch + 1) * 64], func=AF.Exp, scale=-1.0, bias=hblast[:, m, ch:ch + 1])
                return ins
            P.op("act", fkd, reads=[r_hb, r_hblast], writes=[tar])
            P.op("dve", lambda e, ta=ta, m=m: e.tensor_tensor(out=bfC[:, m, :], in0=hf[:, m, 0:TT], in1=ta[:, :TT], op=ALU.mult), reads=[tar, r_hf], writes=[r_bfC])
        hg_gate_jobs = {0: [], 1: []}
        for m in range(2):
            for (c0, n) in grps:
                def gj(m=m, c0=c0, n=n):
                    ps, pres = dense_ps(wg, wgr, KC, 256, m, hT, [r_hT], c0, n, psA)
                    P.op("act", lambda e, ps=ps, m=m, c0=c0, n=n: e.activation(out=gT[:, m, c0:c0 + n], in_=ps[:, :n], func=AF.Silu), reads=[pres], writes=[r_gT])
                hg_gate_jobs[m].append(gj)
        for tb in range(NB):
            ps, pres = psB.get()

            def fv(e, ps=ps, tb=tb):
                w3 = wi.rearrange("p (k c) -> p k c", c=256)
                for kc in range(KC):
                    ins = e.matmul(ps[:, :256], lhsT=hT[:, kc, tb * 128:(tb + 1) * 128], rhs=w3[:, kc, :], start=(kc == 0), stop=(kc == KC - 1))
                return ins
            P.op("pe", fv, reads=[wir, r_hT], writes=[pres])
            P.op("act", lambda e, ps=ps, tb=tb: e.activation(out=vtm[:, tb, :], in_=ps[:, :256], func=AF.Copy), reads=[pres], writes=[r_vtm])
        if samp:
            ps, pres = psB.get()

            def fvs(e, ps=ps):
                w3 = wi.rearrange("p (k c) -> p k c", c=256)
                for kc in range(KC):
                    ins = e.matmul(ps[:NS, :256], lhsT=hT[:, kc, TT:NT], rhs=w3[:, kc, :], start=(kc == 0), stop=(kc == KC - 1))
                return ins
            P.op("pe", fvs, reads=[wir, r_hT], writes=[pres])
            P.op("act", lambda e, ps=ps: e.activation(out=vs[:NS, :], in_=ps[:NS, :256], func=AF.Copy), reads=[pres], writes=[r_vs])
        for tb in range(NB):
            ps, pres = psC.get()
            pb = bank_bf(ps)

            def fk(e, pb=pb, tb=tb):
                for m in range(2):
                    ins = e.transpose(pb[:, m * 128:(m + 1) * 128], bfC[:, m, tb * 128:(tb + 1) * 128], identb)
                return ins
            P.op("pe", fk, reads=[r_bfC, r_identb], writes=[pres])
            P.op("dve", lambda e, pb=pb, tb=tb: e.tensor_copy(out=kdtm[:, tb, :], in_=pb[:, 0:256]), reads=[pres], writes=[r_kdtm])
        if samp:
            ps, pres = psC.get()
            pb = bank_bf(ps)

            def fks(e, pb=pb):
                for m in range(2):
                    ins = e.transpose(pb[:NS, m * 128:(m + 1) * 128], bfB[:, m, TT:NT], identb)
                return ins
            P.op("pe", fks, reads=[r_bfB, r_identb], writes=[pres])
            P.op("dve", lambda e, pb=pb: e.tensor_copy(out=ktms[:NS, :], in_=pb[:NS, 0:256]), reads=[pres], writes=[r_ktms])
        mH = cst[:, C_MASKH:C_MASKH + 128]
        if hp == 0 and it == 0:
            dd("h_k", hf, [r_hf]); dd("h_b", hb, [r_hb]); dd("h_qe", bfA, [r_bfA]); dd("h_ke", bfB, [r_bfB]); dd("h_kd", bfC, [r_bfC])
            dd("h_vtm", vtm, [r_vtm]); dd("h_kdtm", kdtm, [r_kdtm]); dd("h_ebl", hebl, [r_hebl]); dd("h_gT", gT, [r_gT]); dd("h_fsm", fsm, [r_fsm])
            dd("h_vs", vs[:NS, :], [r_vs]); dd("h_ktms", ktms[:NS, :], [r_ktms])
        for m in range(2):
            hd = hp * 2 + m
            ms = slice(m * 128, (m + 1) * 128)
            P.op("act", lambda e, hd=hd: e.activation(out=SbAll[:, 0, :], in_=Shg[:, hd], func=AF.Copy), reads=[r_Shg[hd]], writes=[r_SbAll[0]])
            for ch in range(NCH):
                tb, sub = ch // 2, ch % 2
                rs_ = slice(sub * 64, (sub + 1) * 64)
                pss, pssr = psC.get()
                P.op("pe", lambda e, pss=pss, tb=tb, ms=ms, rs_=rs_: e.matmul(pss[:, :128], lhsT=kdtm[rs_, tb, ms], rhs=vtm[rs_, tb, ms], start=True, stop=True),
                     reads=[r_kdtm, r_vtm], writes=[pssr])
                P.op("dve", lambda e, pss=pss, hd=hd, m=m, ch=ch: e.scalar_tensor_tensor(out=Shg[:, hd], in0=Shg[:, hd], scalar=hebl[:, m, ch:ch + 1], in1=pss[:, :128],
                                                                                         op0=ALU.mult, op1=ALU.add), reads=[pssr, r_Shg[hd], r_hebl], writes=[r_Shg[hd]])
                if ch < NCH - 1:
                    P.op("act", lambda e, hd=hd, ch=ch: e.activation(out=SbAll[:, ch + 1, :], in_=Shg[:, hd], func=AF.Copy), reads=[r_Shg[hd]], writes=[r_SbAll[ch + 1]])
            for tb in range(NB):
                cs = slice(tb * 128, (tb + 1) * 128)
                ps, pres = psB.get()
                P.op("pe", lambda e, ps=ps, m=m, cs=cs: e.matmul(ps[:, :128], lhsT=bfB[:, m, cs], rhs=bfA[:, m, cs], start=True, stop=True), reads=[r_bfA, r_bfB], writes=[pres])
                am, amr = attm.get()
                P.op("dve", lambda e, ps=ps, am=am: e.tensor_tensor(out=am, in0=ps[:, :128], in1=mH, op=ALU.mult), reads=[pres, r_cst], writes=[amr])
                if hg_gate_jobs[m]:
                    hg_gate_jobs[m].pop(0)()
                po, por = psB.get()

                def fo1(e, po=po, am=am, tb=tb, m=m, ms=ms):
                    e.matmul(po[:, 0:128], lhsT=vtm[:, tb, ms], rhs=am, start=True, stop=False)
                    e.matmul(po[:, 0:64], lhsT=SbAll[:, 2 * tb, :], rhs=bfA[:, m, tb * 128:tb * 128 + 64], start=False, stop=False)
                    return e.matmul(po[:, 64:128], lhsT=SbAll[:, 2 * tb + 1, :], rhs=bfA[:, m, tb * 128 + 64:tb * 128 + 128], start=False, stop=True)
                P.op("pe", fo1, reads=[amr, r_vtm, r_SbAll[2 * tb], r_SbAll[2 * tb + 1], r_bfA], writes=[por])
                P.op("act", lambda e, po=po, m=m, cs=cs: e.activation(out=oT[:, m, cs], in_=po[:, :128], func=AF.Copy), reads=[por], writes=[r_oT])
            while hg_gate_jobs[m]:
                hg_gate_jobs[m].pop(0)()
            if last:
                P.op("sp", lambda e, hd=hd: e.dma_start(out=hg_p[hd], in_=Shg[:, hd]), reads=[r_Shg[hd]], dma=True)
            if samp:
                pre = {}

                def ld_hg(b, hd=hd):
                    sa, sar = s0.get()
                    sa2 = sa[:, 0, 0:128]
                    P.op("sp", lambda e, sa2=sa2, b=b, hd=hd: e.dma_start(out=sa2, in_=st_hg[b, hd]), writes=[sar], dma=True)
                    pre[b] = (sa, sar, sa2)
                prevm = {}

                def mk_vm(b, ms=ms):
                    vm, vmr = vmask.get()
                    P.op("act", lambda e, vm=vm, b=b, ms=ms: e.activation(out=vm[:NS, 0:128], in_=vs[:NS, ms], func=AF.Copy, scale=identf[:NS, b:b + 1]),
                         reads=[r_vs, r_cst], writes=[vmr])
                    prevm[b] = (vm, vmr)
                ld_hg(0)
                ld_hg(1)
                mk_vm(0)
                mk_vm(1)
                for b in range(NS):
                    if b + 2 < NS:
                        ld_hg(b + 2)
                        mk_vm(b + 2)
                    vm, vmr = prevm.pop(b)
                    sa, sar, sa2 = pre.pop(b)
                    pss, pssr = psC.get()
                    P.op("pe", lambda e, pss=pss, vm=vm, ms=ms: e.matmul(pss[:, :128], lhsT=ktms[:NS, ms], rhs=vm[:NS, 0:128], start=True, stop=True), reads=[r_ktms, vmr], writes=[pssr])
                    P.op("dve", lambda e, pss=pss, sa2=sa2, m=m, b=b: e.scalar_tensor_tensor(out=sa2, in0=sa2, scalar=fsm[:, m, b:b + 1], in1=pss[:, :128], op0=ALU.mult, op1=ALU.add),
                         reads=[pssr, sar, r_fsm], writes=[sar])
                    P.op("sp", lambda e, sa2=sa2, b=b, hd=hd: e.dma_start(out=hg_s[b, hd], in_=sa2), reads=[sar], dma=True)
                    sn, snr = snb.get()
                    sn2 = sn[:, 0, 0:128]
                    P.op("act", lambda e, sn2=sn2, sa2=sa2: e.activation(out=sn2, in_=sa2, func=AF.Copy), reads=[sar], writes=[snr])

                    def part2(sn2=sn2, snr=snr, m=m, b=b):
                        po, por = psB.get()
                        P.op("pe", lambda e, po=po, sn2=sn2, m=m, b=b: e.matmul(po[:, 0:1], lhsT=sn2, rhs=bfA[:, m, TT + b:TT + b + 1], start=True, stop=True), reads=[snr, r_bfA], writes=[por])
                        P.op("dve", lambda e, po=po, m=m, b=b: e.tensor_copy(out=oT[:, m, TT + b:TT + b + 1], in_=po[:, 0:1]), reads=[por], writes=[r_oT])
                    if b > 0:
                        prev2()
                    prev2 = part2
                prev2()
            if hp == 0 and it == 0 and m == 1:
                dd("h_oT", oT, [r_oT])
            for (c0, n) in grps:
                s_ap, s_res = sq.get()
                P.op("act", lambda e, s_ap=s_ap, m=m, c0=c0, n=n: e.activation(out=s_ap[:, :n], in_=oT[:, m, c0:c0 + n], func=AF.Square), reads=[r_oT], writes=[s_res])
                psq, psqr = psC.get()
                P.op("pe", lambda e, psq=psq, s_ap=s_ap, n=n: e.matmul(psq[:, :n], lhsT=onesb, rhs=s_ap[:, :n], start=True, stop=True), reads=[s_res, r_cst, r_identb], writes=[psqr])
                rs, rsr = tmpf.get()
                P.op("act", lambda e, rs=rs, psq=psq, n=n: e.activation(out=rs[:, :n], in_=psq[:, :n], func=AF.Sqrt, scale=1.0 / 128, bias=epscol), reads=[psqr, r_cst], writes=[rsr])

                def fn_(e, rs=rs, m=m, hd=hd, c0=c0, n=n):
                    e = Synced(P, e, n < 300)
                    e.reciprocal(out=rs[:, :n], in_=rs[:, :n])
                    e.scalar_tensor_tensor(out=rs[:, :n], in0=oT[:, m, c0:c0 + n], scalar=g_hggn[:, hd:hd + 1], in1=rs[:, :n], op0=ALU.mult, op1=ALU.mult)
                    return e.tensor_tensor(out=mixT[:, 8 + hd, c0:c0 + n], in0=rs[:, :n], in1=gT[:, m, c0:c0 + n], op=ALU.mult)
                P.op("dve", fn_, reads=[rsr, r_oT, r_gT, r_pft], writes=[rsr, r_mix[8 + hd]])

    def s5_bu(cols0, n, hbv, r_HB):
        kper = (512 // n) // 2
        cnt = 0
        for k0 in range(0, KC, kper):
            for j4 in range(4):
                ps, pres = psA.get()

                def fb(e, ps=ps, k0=k0, j4=j4):
                    for kk in range(kper):
                        kc = k0 + kk
                        for ri in range(2):
                            pi = kk * 2 + ri
                            ins = e.matmul(ps[:, pi * n:(pi + 1) * n], lhsT=lB[32 * j4:32 * j4 + 32, kc, ri, :], rhs=hT[32 * j4:32 * j4 + 32, kc, cols0:cols0 + n],
                                           start=True, stop=True, tile_position=(32 * j4, 0))
                    return ins
                P.op("pe", fb, reads=[r_lB, r_hT], writes=[pres])
                js = k0 * 4 + j4
                eng = "act"
                cnt += 1

                def fe(e, ps=ps, js=js, eng=eng):
                    src = ps[:, :kper * 2 * n].rearrange("p (j r t) -> p j r t", r=2, t=n)
                    dst = hbv[:, js:js + 4 * (kper - 1) + 1:4, :, :]
                    if eng == "act":
                        return e.activation(out=dst, in_=src, func=AF.Copy)
                    return e.tensor_copy(out=dst, in_=src)
                P.op(eng, fe, reads=[pres], writes=[r_HB])

    def s5_y_mm(n, hbb, r_hbb):
        ps, pres = psB.get()

        def fy(e, ps=ps):
            for kc in range(KC):
                for j4 in range(4):
                    j = kc * 4 + j4
                    for ri in range(2):
                        ins = e.matmul(ps[32 * j4:32 * j4 + 32, kc * n:(kc + 1) * n], lhsT=lC[:, j, ri, :], rhs=hbb[:, j, ri, :],
                                       start=(ri == 0), stop=(ri == 1), tile_position=(0, 32 * j4))
            return ins
        P.op("pe", fy, reads=[r_lC] + list(r_hbb), writes=[pres])
        return ps, pres

    def s5_gelu(ps, pres, cols0, n, dst_cols):
        yv, yvr = tmpf.get()
        t2, t2r = tmpf.get()

        def fg(e, ps=ps, yv=yv, t2=t2):
            e = Synced(P, e, KC * n < 300)
            y3 = yv[:, :KC * n].rearrange("p (k t) -> p k t", t=n)
            e.tensor_tensor(out=y3, in0=hT[:, :, cols0:cols0 + n], in1=bc_last(dcolT, n), op=ALU.mult)
            e.tensor_tensor(out=yv[:, :KC * n], in0=yv[:, :KC * n], in1=ps[:, :KC * n], op=ALU.add)
            e.tensor_tensor(out=t2[:, :KC * n], in0=yv[:, :KC * n], in1=yv[:, :KC * n], op=ALU.mult)
            e.tensor_scalar(out=t2[:, :KC * n], in0=t2[:, :KC * n], scalar1=0.044715, scalar2=1.0, op0=ALU.mult, op1=ALU.add)
            return e.tensor_tensor(out=t2[:, :KC * n], in0=t2[:, :KC * n], in1=yv[:, :KC * n], op=ALU.mult)
        P.op("dve", fg, reads=[pres, r_hT, r_pft], writes=[yvr, t2r])
        P.op("act", lambda e, t2=t2: e.activation(out=t2[:, :KC * n], in_=t2[:, :KC * n], func=AF.Sigmoid, scale=1.5957691216057308), reads=[t2r], writes=[t2r])
        P.op("dve", lambda e, yv=yv, t2=t2: e.tensor_tensor(out=mixT[:, :, dst_cols:dst_cols + n], in0=yv[:, :KC * n].rearrange("p (k t) -> p k t", t=n),
                                                          in1=t2[:, :KC * n].rearrange("p (k t) -> p k t", t=n), op=ALU.mult), reads=[yvr, t2r], writes=r_mix)

    def s5_layer(it, last, pre=False):
        NSC = TT // TC
        if it == SS:
            stage = cv_s5stage
            H0, Hn = HBs[1], HBs[0]
            for ri, src in enumerate((st_s5r, st_s5i)):
                for half in range(2):
                    P.op("sp", lambda e, src=src, half=half: e.dma_start(out=stage[:NS, :], in_=src[:, half * 4096:(half + 1) * 4096]), writes=[r_stage], dma=True)
                    ps, pres = psB.get()

                    def ftr(e, ps=ps):
                        for jj in range(32):
                            ins = e.transpose(ps[:, jj * NS:(jj + 1) * NS], stage[:NS, jj * 128:(jj + 1) * 128], identf[:NS, :NS])
                        return ins
                    P.op("pe", ftr, reads=[r_stage, r_cst], writes=[pres])
                    P.op("dve", lambda e, ps=ps, ri=ri, half=half: e.tensor_copy(out=H0[:, half * 32:(half + 1) * 32, ri, :], in_=ps.rearrange("p (j b) -> p j b", b=NS)),
                         reads=[pres], writes=[r_HBs[1]])
            s5_bu(TT, NS, Hn, r_HBs[0])
            HS1 = HBb_raw.rearrange("p (j r t) -> p j r t", r=2, t=NS)

            def fs(e):
                e.tensor_tensor(out=HS1, in0=H0, in1=bc_last(A1, NS), op=ALU.mult)
                e.tensor_tensor(out=Hn, in0=Hn, in1=HS1, op=ALU.add)
                e.tensor_tensor(out=HS1[:, :, 0, :], in0=H0[:, :, 1, :], in1=bc_last(A2[:, :, 0], NS), op=ALU.mult)
                e.tensor_tensor(out=HS1[:, :, 1, :], in0=H0[:, :, 0, :], in1=bc_last(A2[:, :, 1], NS), op=ALU.mult)
                return e.tensor_tensor(out=Hn, in0=Hn, in1=HS1, op=ALU.add)
            P.op("dve", fs, reads=[r_HBs[0], r_HBs[1], r_A], writes=[r_HBs[0], r_HBb, r_HBbs[0], r_HBbs[1]])
            hbs = HBbs[0]
            P.op("act", lambda e: e.activation(out=hbs, in_=Hn, func=AF.Copy), reads=[r_HBs[0]], writes=[r_HBb, r_HBbs[0]])
            ps, pres = s5_y_mm(NS, hbs, [r_HBbs[0]])
            s5_gelu(ps, pres, TT, NS, TT)
            for ri, dst in enumerate((s5r_s, s5i_s)):
                for half in range(2):
                    for g in range(8):
                        ps, pres = psB.get()

                        def fto(e, ps=ps, ri=ri, half=half, g=g):
                            for q in range(4):
                                j = half * 32 + g * 4 + q
                                ins = e.transpose(ps[:NS, q * 128:(q + 1) * 128], Hn[:, j, ri, :], identf)
                            return ins
                        P.op("pe", fto, reads=[r_HBs[0], r_cst], writes=[pres])
                        P.op("act", lambda e, ps=ps, g=g: e.activation(out=stage[:NS, g * 512:(g + 1) * 512], in_=ps[:NS, :], func=AF.Copy), reads=[pres], writes=[r_stage])
                    P.op("sp", lambda e, dst=dst, half=half: e.dma_start(out=dst[:, half * 4096:(half + 1) * 4096], in_=stage[:NS, :]), reads=[r_stage], dma=True)
        if pre:
            stage = cv_s5stage
            PW1 = stage[:, 0:2048].rearrange("p (t j r) -> p t j r", j=64, r=2)
            PW2 = stage[:, 2048:4096].rearrange("p (t j r) -> p t j r", j=64, r=2)

            def fgen(e):
                e = Synced(P, e)
                cur = Pw[0]
                e.memset(cur[:, :, 0], 1.0)
                e.memset(cur[:, :, 1], 0.0)
                for k_ in range(TC + 1):
                    if k_ < TC:
                        t_ = TC - 1 - k_
                        e.tensor_copy(out=PW1[:, t_], in_=bc_last(cur[:, :, 0], 2))
                        e.tensor_copy(out=PW2[:, t_, :, 1], in_=cur[:, :, 1])
                        e.tensor_scalar(out=PW2[:, t_, :, 0], in0=cur[:, :, 1], scalar1=-1.0, scalar2=None, op0=ALU.mult)
                    else:
                        e.tensor_copy(out=A16a, in_=bc_last(cur[:, :, 0], 2))
                        e.tensor_copy(out=A16b[:, :, 1], in_=cur[:, :, 1])
                        ins = e.tensor_scalar(out=A16b[:, :, 0], in0=cur[:, :, 1], scalar1=-1.0, scalar2=None, op0=ALU.mult)
                        break
                    nxt = Pw[(k_ + 1) % 2]
                    e.tensor_tensor(out=st1, in0=cur, in1=A1, op=ALU.mult)
                    e.tensor_tensor(out=st2, in0=cur[:, :, ::-1], in1=A2, op=ALU.mult)
                    e.tensor_tensor(out=nxt, in0=st1, in1=st2, op=ALU.add)
                    cur = nxt
                return ins
            P.op("dve", fgen, reads=[r_A], writes=[r_stage, r_pw, r_st, r_A16])
            TMP = HBb_raw.rearrange("p (t j r) -> p t j r", j=64, r=2)
            s5_bu(0, TC, HBs[0], r_HBs[0])
            for sc in range(NSC):
                k = sc % 2
                if sc + 1 < NSC:
                    s5_bu((sc + 1) * TC, TC, HBs[1 - k], r_HBs[1 - k])

                def fpre(e, HR=HBraw[k]):
                    e.tensor_tensor(out=TMP, in0=HR[:, :, :, ::-1], in1=PW2, op=ALU.mult)
                    e.tensor_tensor(out=HR, in0=HR, in1=PW1, op=ALU.mult)
                    e.tensor_tensor(out=HR, in0=HR, in1=TMP, op=ALU.add)
                    e.tensor_reduce(out=cbuf, in_=HR.rearrange("p t j r -> p j r t"), axis=mybir.AxisListType.X, op=ALU.add)
                    es = Synced(P, e)
                    es.tensor_tensor(out=st1, in0=Hst, in1=A16a, op=ALU.mult)
                    es.tensor_tensor(out=st2, in0=Hst[:, :, ::-1], in1=A16b, op=ALU.mult)
                    es.tensor_tensor(out=st1, in0=st1, in1=st2, op=ALU.add)
                    return es.tensor_tensor(out=Hst, in0=st1, in1=cbuf, op=ALU.add)
                P.op("dve", fpre, reads=[r_HBs[k], r_stage, r_A16, r_Hst], writes=[r_HBs[k], r_HBb, r_HBbs[0], r_HBbs[1], r_st, r_pw, r_Hst])
            return
        s5_bu(0, TC, HBs[0], r_HBs[0])
        pend = None
        for sc in range(NSC):
            k = sc % 2
            HB, rHB, HBb, rHBb = HBs[k], r_HBs[k], HBbs[k], r_HBbs[k]
            if sc + 1 < NSC:
                s5_bu((sc + 1) * TC, TC, HBs[1 - k], r_HBs[1 - k])

            def fscan(e, HB=HB, HR=HBraw[k]):
                for t in range(TC):
                    prev = Hst if t == 0 else HR[:, t - 1]
                    prevs = Hst[:, :, ::-1] if t == 0 else HR[:, t - 1, :, ::-1]
                    cur = HR[:, t]
                    e.tensor_tensor(out=st1, in0=prev, in1=A1, op=ALU.mult)
                    e.tensor_tensor(out=st2, in0=prevs, in1=A2, op=ALU.mult)
                    i3 = e.tensor_tensor(out=cur, in0=cur, in1=st1, op=ALU.add)
                    P.selfsync(i3)
                    i4 = e.tensor_tensor(out=cur, in0=cur, in1=st2, op=ALU.add)
                    P.selfsync(i4)
                return e.tensor_copy(out=Hst, in_=HR[:, TC - 1])
            P.op("dve", fscan, reads=[rHB, r_A, r_Hst], writes=[rHB, r_st, r_Hst])
            if pre:
                continue
            P.op("act", lambda e, HB=HB, HBb=HBb: e.activation(out=HBb, in_=HB, func=AF.Copy), reads=[rHB], writes=[rHBb, r_HBb])
            ps, pres = s5_y_mm(TC, HBb, [rHBb])
            if pend is not None:
                s5_gelu(*pend)
            pend = (ps, pres, sc * TC, TC, sc * TC)
        if pend is not None:
            s5_gelu(*pend)
        if last:
            for ri, dst in enumerate((s5r_p, s5i_p)):
                ps, pres = psB.get()
                P.op("pe", lambda e, ps=ps, ri=ri: e.transpose(ps[:64, 0:128], Hst[:, :, ri], identf), reads=[r_Hst, r_cst], writes=[pres])
                ta, tar = tmpf.get()
                P.op("act", lambda e, ps=ps, ta=ta: e.activation(out=ta[:64, 0:128], in_=ps[:64, 0:128], func=AF.Copy), reads=[pres], writes=[tar])
                P.op("sp", lambda e, dst=dst, ta=ta: e.dma_start(out=dst, in_=ta[:64, 0:128]), reads=[tar], dma=True)

    cv_s5stage = cv.get([4096]); r_stage = tok("stage")

    for it in range(NTILES):
        grps = groups(it)
        ntot = grps[-1][0] + grps[-1][1]
        t0 = it * TT
        last = (it == NTILES - 1)
        pre = it < NPRE
        tout = (it - NPRE) * TT
        if it == 0 or pre or it == NPRE:
            fence()
        for tb in range(NB):
            xa, xr = xin.get()
            P.op("sp", lambda e, xa=xa, tb=tb, t0=t0: e.dma_start(out=xa, in_=xp[t0 + tb * 128:t0 + (tb + 1) * 128, :]), writes=[xr], dma=True)
            for g4 in range(4):
                ps, pres = psB.get()

                def f(e, ps=ps, xa=xa, g4=g4):
                    for q in range(4):
                        kc = g4 * 4 + q
                        ins = e.transpose(ps[:, q * 128:(q + 1) * 128], xa[:, kc * 128:(kc + 1) * 128], identf)
                    return ins
                P.op("pe", f, reads=[xr, r_cst], writes=[pres])
                eng = "dve" if g4 % 2 == 0 else "act"

                def cp(e, ps=ps, g4=g4, tb=tb, eng=eng):
                    src = ps.rearrange("p (q c) -> p q c", c=128)
                    dst = xT[:, g4 * 4:g4 * 4 + 4, tb * 128:(tb + 1) * 128]
                    if eng == "dve":
                        return e.tensor_copy(out=dst, in_=src)
                    return e.activation(out=dst, in_=src, func=AF.Copy)
                P.op(eng, cp, reads=[pres], writes=r_xT[g4 * 4:g4 * 4 + 4])
        if it == SS:
            xa, xr = xin.get()
            P.op("sp", lambda e, xa=xa: e.dma_start(out=xa[:NS, :], in_=xs), writes=[xr], dma=True)
            ps, pres = psB.get()

            def f(e, ps=ps, xa=xa):
                for kc in range(KC):
                    ins = e.transpose(ps[:, kc * NS:(kc + 1) * NS], xa[:NS, kc * 128:(kc + 1) * 128], identf[:NS, :NS])
                return ins
            P.op("pe", f, reads=[xr, r_cst], writes=[pres])
            P.op("dve", lambda e, ps=ps: e.tensor_copy(out=xT[:, :, TT:TT + NS], in_=ps[:, :KC * NS].rearrange("p (k c) -> p k c", c=NS)),
                 reads=[pres], writes=r_xT)
        if dbg == "x":
            dump_dbg()
            break
        fence()
        if "nomix0" not in (dbg or ""):
            rmsnorm(g_attn, hT, [r_hT], grps)
            P.op("sp", lambda e, t0=t0: e.dma_start(out=rott[:, :, 0:TT], in_=rot[:, :, t0:t0 + TT].rearrange("a p n -> p a n")), writes=[r_rot], dma=True)
            if it == SS:
                P.op("sp", lambda e: e.dma_start(out=rott[:, :, TT:NT], in_=rot[:, :, T:T + NS].rearrange("a p n -> p a n")), writes=[r_rot], dma=True)
            for h in range(1 if (dbg and "small" in dbg) else RET_H):
                ret_head(h, it, grps, last)
            for hp in range(1 if (dbg and "small" in dbg) else 4):
                hg_pair(hp, it, grps, last)
            for blk in range(8):
                w, wr = load_w(w_out[blk], 4096)
                for m in range(2):
                    oc = blk * 2 + m
                    for (c0, n) in grps:
                        ps, pres = dense_ps(w, wr, KC, 256, m, mixT, r_mix, c0, n, psA)
                        add_resid(ps, pres, oc, c0, n)
        if dbg and dbg.startswith("mix0"):
            dump_dbg()
            break
        fence()
        ffn(0, grps)
        if dbg and dbg.startswith("ffn0"):
            dump_dbg()
            break
        fence()
        rmsnorm(g_ssm, hT, [r_hT], grps)
        s5_layer(it, last, pre)
        if pre:
            continue
        for blk in range(8):
            wa, war = load_w(w_ga[blk], 4096)
            wb, wbr = load_w(w_gb[blk], 4096)
            for m in range(2):
                oc = blk * 2 + m
                for (c0, n) in grps:
                    pa, par = dense_ps(wa, war, KC, 256, m, mixT, r_mix, c0, n, psA)
                    pb_, pbr = dense_ps(wb, wbr, KC, 256, m, mixT, r_mix, c0, n, psA)
                    ta, tar = tmpf.get()
                    P.op("act", lambda e, ta=ta, pb_=pb_, n=n: e.activation(out=ta[:, :n], in_=pb_[:, :n], func=AF.Sigmoid), reads=[pbr], writes=[tar])

                    def fglu(e, ta=ta, pa=pa, oc=oc, c0=c0, n=n):
                        e = Synced(P, e, n < 300)
                        e.tensor_tensor(out=ta[:, :n], in0=ta[:, :n], in1=pa[:, :n], op=ALU.mult)
                        return e.tensor_tensor(out=xT[:, oc, c0:c0 + n], in0=xT[:, oc, c0:c0 + n], in1=ta[:, :n], op=ALU.add)
                    P.op("dve", fglu, reads=[tar, par, r_xT[oc]], writes=[tar, r_xT[oc]])
        if dbg == "mix1":
            dump_dbg()
            break
        fence()
        ffn(1, grps)
        rmsnorm(g_fin, xT, r_xT, grps)
        if dbg == "final":
            dump_dbg()
            break
        fence()
        for tb in range(NB):
            xa, xr = xin.get()
            for g4 in range(4):
                ps, pres = psB.get()

                def f(e, ps=ps, g4=g4, tb=tb):
                    for q in range(4):
                        kc = g4 * 4 + q
                        ins = e.transpose(ps[:, q * 128:(q + 1) * 128], xT[:, kc, tb * 128:(tb + 1) * 128], identf)
                    return ins
                P.op("pe", f, reads=r_xT[g4 * 4:g4 * 4 + 4] + [r_cst], writes=[pres])
                eng = "dve" if g4 % 2 == 0 else "act"

                def cp(e, ps=ps, g4=g4, xa=xa, eng=eng):
                    if eng == "dve":
                        return e.tensor_copy(out=xa[:, g4 * 512:(g4 + 1) * 512], in_=ps)
                    return e.activation(out=xa[:, g4 * 512:(g4 + 1) * 512], in_=ps, func=AF.Copy)
                P.op(eng, cp, reads=[pres], writes=[xr])
            P.op("sp", lambda e, xa=xa, tb=tb, tout=tout: e.dma_start(out=yp[tout + tb * 128:tout + (tb + 1) * 128, :], in_=xa), reads=[xr], dma=True)
        if it == SS:
            xa, xr = xin.get()
            for g4 in range(4):
                ps, pres = psB.get()

                def f(e, ps=ps, g4=g4):
                    for q in range(4):
                        kc = g4 * 4 + q
                        ins = e.transpose(ps[:NS, q * 128:(q + 1) * 128], xT[:, kc, TT:NT], identf)
                    return ins
                P.op("pe", f, reads=r_xT[g4 * 4:g4 * 4 + 4] + [r_cst], writes=[pres])
                P.op("dve", lambda e, ps=ps, g4=g4, xa=xa: e.tensor_copy(out=xa[:NS, g4 * 512:(g4 + 1) * 512], in_=ps[:NS, :]), reads=[pres], writes=[xr])
            P.op("sp", lambda e, xa=xa: e.dma_start(out=ys, in_=xa[:NS, :]), reads=[xr], dma=True)

    P.emit(nc, stack)
    stack.close()
    return nc


def _prep_shared(inp, T, TT):
    consts_np, _ = _consts(TT)
    f = lambda a: np.asarray(a, np.float32)
    pf = np.concatenate([
        _fm(f(inp["attn_norm_g"])[0]), _fm(f(inp["ffn_norm_g"])[0]), _fm(f(inp["ssm_norm_g"])[0]),
        _fm(f(inp["ffn_norm_g"])[1]), _fm(f(inp["final_norm_g"])),
        _fm(f(inp["ret_gn_g"])[0]), _fm(f(inp["hg_gn_g"])[0]),
        np.concatenate([_fm(f(inp["hg_lb"])[l]) for l in range(3)], 1),
        _fm(f(inp["s5_d"])[0]),
    ], 1)

    def qj(a):
        a = f(a)
        rest = a.shape[2:]
        return np.ascontiguousarray(a.reshape((64, 2, 64) + rest).transpose((1, 2, 0) + tuple(range(3, 3 + len(rest)))).reshape((128, 64) + rest))
    lamr = qj(inp["s5_lam_re"][0]); lami = qj(inp["s5_lam_im"][0])
    logdt = qj(np.repeat(f(inp["s5_log_dt"])[0][:, None], 64, 1))
    bre = qj(inp["s5_b_re"][0]); bim = qj(inp["s5_b_im"][0])
    cre = qj(f(inp["s5_c_re"])[0].transpose(0, 2, 1)); cim = qj(f(inp["s5_c_im"])[0].transpose(0, 2, 1))
    s5p = np.concatenate([lamr, lami, logdt, bre.reshape(128, -1), bim.reshape(128, -1), cre.reshape(128, -1), cim.reshape(128, -1)], 1)
    wfd = []
    for l in range(2):
        w = f(inp["w_ffn_down"])[l]
        halves = []
        for hf_ in range(2):
            wh = w[hf_ * 2816:(hf_ + 1) * 2816]
            halves.append(_wblocks(wh, 128))
        wfd.append(np.stack(halves, 0))
    shared = {
        "consts": consts_np, "pf": np.ascontiguousarray(pf.astype(np.float32)), "s5p": np.ascontiguousarray(s5p.astype(np.float32)),
        "w_in": _wblocks(f(inp["w_in"])[0], 256), "w_out": _wblocks(f(inp["w_out"])[0], 256),
        "w_ga": _wblocks(f(inp["w_glu_a"])[0], 256), "w_gb": _wblocks(f(inp["w_glu_b"])[0], 256),
        "w_fg": np.stack([_wblocks(f(inp["w_ffn_gate"])[l], 256) for l in range(2)], 0),
        "w_fu": np.stack([_wblocks(f(inp["w_ffn_up"])[l], 256) for l in range(2)], 0),
        "w_fd": np.stack(wfd, 0),
    }
    return shared


_T, _TT, _NPRE = 2048, 512, 2


def kernel(**inp):
    T, TT, NPRE = _T, _TT, _NPRE
    TH = T // 2
    shared = _prep_shared(inp, T, TT)
    nc = build(T, TT, npre=NPRE)
    xpr = np.asarray(inp["x_prompt"], np.float32)
    xsm = np.asarray(inp["x_sample"], np.float32)[:, 0, :]
    samp_pos = np.full(NS, 16384.0)
    rot_a = np.ascontiguousarray(_rot_tables(np.concatenate([np.arange(TH, dtype=np.float64), np.arange(TH, dtype=np.float64), samp_pos])))
    rot_b = np.ascontiguousarray(_rot_tables(np.concatenate([np.arange(T, dtype=np.float64), samp_pos])))
    zeros_h = np.zeros((TH, D), np.float32)
    in_maps = []
    for c in range(NCORES):
        seq, half = c // 2, c % 2
        m = dict(shared)
        if half == 0:
            m["xp"] = np.ascontiguousarray(np.concatenate([zeros_h, xpr[seq, :TH]], 0))
            m["rot"] = rot_a
        else:
            m["xp"] = np.ascontiguousarray(xpr[seq])
            m["rot"] = rot_b
        m["xs"] = np.ascontiguousarray(xsm[c * NS:(c + 1) * NS])
        m["st_ret"] = np.ascontiguousarray(np.asarray(inp["state_ret"], np.float32)[0, c * NS:(c + 1) * NS])
        m["st_hg"] = np.ascontiguousarray(np.asarray(inp["state_hgrn"], np.float32)[0, c * NS:(c + 1) * NS])
        m["st_s5r"] = np.ascontiguousarray(np.asarray(inp["state_s5_re"], np.float32)[0, c * NS:(c + 1) * NS].reshape(NS, 8192))
        m["st_s5i"] = np.ascontiguousarray(np.asarray(inp["state_s5_im"], np.float32)[0, c * NS:(c + 1) * NS].reshape(NS, 8192))
        in_maps.append(m)
    res = run_bass_kernel_spmd(nc, in_maps, core_ids=list(range(NCORES))).results
    y_prompt = np.stack([np.concatenate([res[2 * q]["yp"], res[2 * q + 1]["yp"]], 0) for q in range(4)], 0)
    y_sample = np.concatenate([res[c]["ys"] for c in range(NCORES)], 0)[:, None, :]
    fin = [2 * q + 1 for q in range(4)]
    ret_p = np.stack([res[c]["ret_p"] for c in fin], 0)[None]
    ret_s = np.concatenate([res[c]["ret_s"] for c in range(NCORES)], 0)[None]
    hg_p = np.stack([res[c]["hg_p"] for c in fin], 0)[None]
    hg_s = np.concatenate([res[c]["hg_s"] for c in range(NCORES)], 0)[None]
    s5r_p = np.stack([res[c]["s5r_p"].reshape(128, 64) for c in fin], 0)[None]
    s5i_p = np.stack([res[c]["s5i_p"].reshape(128, 64) for c in fin], 0)[None]
    s5r_s = np.concatenate([res[c]["s5r_s"] for c in range(NCORES)], 0).reshape(1, 128, 128, 64)
    s5i_s = np.concatenate([res[c]["s5i_s"] for c in range(NCORES)], 0).reshape(1, 128, 128, 64)
    return (y_prompt, y_sample, ret_p, ret_s, hg_p, hg_s, s5r_p, s5i_p, s5r_s, s5i_s)
```

```python
import math
from contextlib import ExitStack
import numpy as np
import concourse.bass as bass
import concourse.mybir as mybir
from concourse.bass_utils import run_bass_kernel_spmd

F32, BF16, I32 = mybir.dt.float32, mybir.dt.bfloat16, mybir.dt.int32
AF = mybir.ActivationFunctionType
ALU = mybir.AluOpType

D = 2048
KC = 16
DFF = 5632
FC = 44
NCORES = 8
NS = 16
EPS = 1e-6
TWO_PI = 2.0 * math.pi


class Res:
    __slots__ = ("name", "w", "r")

    def __init__(self, name=""):
        self.name = name
        self.w = None
        self.r = {}


class Prog:
    def __init__(self):
        self.ops = []

    def op(self, eng, fn, reads=(), writes=(), dma=False):
        i = len(self.ops)
        deps = set()
        raw = set()
        for res in reads:
            if res.w is not None:
                deps.add(res.w)
                raw.add(res.w)
        for res in writes:
            if res.w is not None:
                deps.add(res.w)
            deps.update(res.r.values())
        for res in reads:
            res.r[("dma", i) if dma else eng] = i
        for res in writes:
            res.w = i
            res.r = {}
        deps.discard(i)
        self.ops.append([eng, fn, deps, dma, False, None, 0, raw, 0])
        return i

    def emit(self, nc, stack, ndma_sp=10, ndma_pool=8):
        ops = self.ops
        engs = ("pe", "act", "dve", "pool", "sp")
        cnt_ = {e: 0 for e in engs}
        for o in ops:
            o[8] = cnt_[o[0]]
            cnt_[o[0]] += 1
        WIN = 3

        def need(o, j):
            p = ops[j]
            if p[3] or p[0] != o[0]:
                return True
            return (j in o[7]) and (o[8] - p[8] <= WIN) and not o[3]
        for o in ops:
            for j in o[2]:
                if need(o, j):
                    ops[j][4] = True
        csem = {e: stack.enter_context(nc.semaphore("c_" + e)) for e in engs}
        dsem = {"sp": [stack.enter_context(nc.semaphore("d_sp%d" % i)) for i in range(ndma_sp)],
                "pool": [stack.enter_context(nc.semaphore("d_pl%d" % i)) for i in range(ndma_pool)],
                "act": [stack.enter_context(nc.semaphore("d_ac%d" % i)) for i in range(2)]}
        ccount = {e: 0 for e in engs}
        dcount = {e: 0 for e in dsem}
        dval = {e: [0] * len(dsem[e]) for e in dsem}
        prev_on_slot = {}
        streams = {e: [] for e in engs}
        for i, o in enumerate(ops):
            e = o[0]
            streams[e].append(i)
            if o[3]:
                k = dcount[e] % len(dsem[e])
                dcount[e] += 1
                prev_on_slot[i] = (dsem[e][k], dval[e][k])
                dval[e][k] += 16
                o[5] = dsem[e][k]
                o[6] = dval[e][k]
            elif o[4]:
                ccount[e] += 1
                o[5] = csem[e]
                o[6] = ccount[e]
        block = stack.enter_context(nc.Block())
        ssem = {e: stack.enter_context(nc.semaphore("s_" + e)) for e in ("act", "dve", "pool")}
        scnt = {e: 0 for e in ssem}
        prog = self

        def run(engname, eng):
            seen = {}
            prog.cur = engname

            def selfsync(ins):
                scnt[engname] += 1
                ins.then_inc(ssem[engname], 1)
                eng.wait_ge(ssem[engname], scnt[engname])
            prog.selfsync = selfsync

            def wait(sem, val):
                if val <= 0:
                    return
                key = id(sem)
                if seen.get(key, 0) < val:
                    eng.wait_ge(sem, val)
                    seen[key] = val

            for i in streams[engname]:
                o = ops[i]
                for j in sorted(o[2]):
                    p = ops[j]
                    if need(o, j):
                        wait(p[5], p[6])
                if o[3]:
                    s, v = prev_on_slot[i]
                    wait(s, v)
                ins = o[1](eng)
                if o[3]:
                    ins.then_inc(o[5], 16)
                elif o[4]:
                    ins.then_inc(o[5], 1)
            if engname in dsem:
                for k, s in enumerate(dsem[engname]):
                    wait(s, dval[engname][k])

        @block.tensor
        def _(t):
            run("pe", t)

        @block.scalar
        def _(t):
            run("act", t)

        @block.vector
        def _(t):
            run("dve", t)

        @block.gpsimd
        def _(t):
            run("pool", t)

        @block.sync
        def _(t):
            run("sp", t)


class Synced:
    def __init__(self, prog, eng, on=True):
        self.p, self.e, self.pend, self.on = prog, eng, None, on

    def __getattr__(self, name):
        f = getattr(self.e, name)

        def g(*a, **k):
            if self.pend is not None and self.on:
                self.p.selfsync(self.pend)
            self.pend = f(*a, **k)
            return self.pend
        return g


class Rot:
    def __init__(self, items):
        self.items = items
        self.i = 0

    def get(self):
        it = self.items[self.i % len(self.items)]
        self.i += 1
        return it


RET_H, RET_DK = 4, 256
HG_H = 8


def _fm(vec):
    v = np.asarray(vec, np.float32)
    return np.ascontiguousarray(v.reshape(-1, 128).T)


def _wblocks(w, bc):
    K, N = w.shape
    kc = K // 128
    return np.ascontiguousarray(w.reshape(kc, 128, N // bc, bc).transpose(2, 1, 0, 3)).reshape(N // bc, 128, kc * bc)


def _consts(TT):
    cols = []
    ident = np.eye(128, dtype=np.float64)
    cols.append(ident)
    cols.append(np.ones((128, 128)))
    idx = np.arange(128, dtype=np.float64)
    lg = np.log(1.0 - 2.0 ** (-5.0 - np.arange(RET_H, dtype=np.float64)))
    maskT = np.zeros((128, RET_H, 128))
    qdec = np.zeros((128, RET_H, 128))
    kdec = np.zeros((128, RET_H))
    for h in range(RET_H):
        dji = idx[None, :] - idx[:, None]
        maskT[:, h, :] = np.where(dji >= 0, np.exp(dji * lg[h]), 0.0) / 16.0
        qdec[:, h, :] = np.exp((idx + 1.0) * lg[h])[None, :]
        kdec[:, h] = np.exp((127.0 - idx) * lg[h]) / 16.0
    cols.append(maskT.reshape(128, -1))
    cols.append(qdec.reshape(128, -1))
    cols.append(kdec)
    jj = idx[:, None]
    ii = idx[None, :]
    maskH = ((jj // 64 == ii // 64) & (jj <= ii)).astype(np.float64)
    cols.append(maskH)
    half = np.stack([(idx // 64 == 0), (idx // 64 == 1)], 1).astype(np.float64)
    cols.append(half)
    reset = np.ones((128, TT))
    reset[:, ::64] = 0.0
    cols.append(reset)
    cols.append(np.full((128, 1), EPS))
    c = np.concatenate(cols, 1).astype(np.float32)
    return np.ascontiguousarray(c), lg


C_ID, C_ONES, C_MASKT, C_QDEC, C_KDEC, C_MASKH, C_HALF, C_RESET = 0, 128, 256, 768, 1280, 1284, 1412, 1414


def _rot_tables(pos):
    half = 128
    inv = (10000.0 ** (-np.arange(half, dtype=np.float32) / np.float32(half))).astype(np.float32)
    ang = (pos.astype(np.float32)[None, :] * inv[:, None]).astype(np.float32).astype(np.float64)
    c, s = np.cos(ang), np.sin(ang)
    return np.stack([c, s], 0).astype(np.float32)


def build(T, TT, dbg=None, npre=0):
    NT = TT + NS
    NTILES = T // TT
    NPRE = npre
    SS = NPRE
    TOUT = (NTILES - NPRE) * TT
    NB = TT // 128
    consts_np, lg = _consts(TT)
    NCONST = consts_np.shape[1]
    C_EPS = C_RESET + TT
    gam = [float(np.exp(lg[h])) for h in range(RET_H)]
    cdec = [float(np.exp(128.0 * lg[h])) for h in range(RET_H)]

    nc = bass.Bass("TRN2", target_bir_lowering=False)
    stack = ExitStack()
    P = Prog()

    def din(name, shape):
        return nc.dram_tensor(name, list(shape), F32, kind="ExternalInput").ap()

    def dout(name, shape):
        return nc.dram_tensor(name, list(shape), F32, kind="ExternalOutput").ap()

    xp = din("xp", [T, D])
    xs = din("xs", [NS, D])
    st_ret = din("st_ret", [NS, RET_H, 256, 256])
    st_hg = din("st_hg", [NS, HG_H, 128, 128])
    st_s5r = din("st_s5r", [NS, 8192])
    st_s5i = din("st_s5i", [NS, 8192])
    consts = din("consts", [128, NCONST])
    rot = din("rot", [2, 128, T + NS])
    NPF = 16 * 6 + 8 + 8 + 24
    pf = din("pf", [128, NPF])
    s5p = din("s5p", [128, 192 + 4096])
    w_in = din("w_in", [32, 128, 16 * 256])
    w_out = din("w_out", [8, 128, 16 * 256])
    w_ga = din("w_ga", [8, 128, 16 * 256])
    w_gb = din("w_gb", [8, 128, 16 * 256])
    w_fg = din("w_fg", [2, 22, 128, 16 * 256])
    w_fu = din("w_fu", [2, 22, 128, 16 * 256])
    w_fd = din("w_fd", [2, 2, 16, 128, 22 * 128])

    yp = dout("yp", [TOUT, D])
    ys = dout("ys", [NS, D])
    ret_p = dout("ret_p", [RET_H, 256, 256])
    ret_s = dout("ret_s", [NS, RET_H, 256, 256])
    hg_p = dout("hg_p", [HG_H, 128, 128])
    hg_s = dout("hg_s", [NS, HG_H, 128, 128])
    s5r_p = dout("s5r_p", [64, 128])
    s5i_p = dout("s5i_p", [64, 128])
    s5r_s = dout("s5r_s", [NS, 8192])
    s5i_s = dout("s5i_s", [NS, 8192])
    dbg_out = dout("dbg", [128, KC * NT]) if dbg else None

    def sb(name, shape, dt=F32):
        return stack.enter_context(nc.sbuf_tensor(name, list(shape), dt))[:]

    cst = sb("cst", [128, NCONST]); r_cst = Res("cst")
    pft = sb("pft", [128, NPF]); r_pft = Res("pft")
    identf = cst[:, C_ID:C_ID + 128]
    onesf = cst[:, C_ONES:C_ONES + 128]
    epscol = cst[:, C_EPS:C_EPS + 1]
    identb = sb("identb", [128, 128], BF16); r_identb = Res()
    xT = sb("xT", [128, KC, NT]); r_xT = [Res("xT%d" % k) for k in range(KC)]
    hT = sb("hT", [128, KC, NT], BF16); r_hT = Res("hT")
    mixT = sb("mixT", [128, KC, NT], BF16); r_mix = [Res("mix%d" % k) for k in range(KC)]
    NW = 5
    wsl = Rot([(sb("w%d" % i, [128, 4096], BF16), Res("w%d" % i)) for i in range(NW)])
    sq = Rot([(sb("sq%d" % i, [128, NT], BF16), Res()) for i in range(2)])
    onesb = sb("onesb", [128, 128], BF16)
    rstd = sb("rstd", [128, NT]); r_rstd = Res("rstd")
    tmpf = Rot([(sb("tmpf%d" % i, [128, 512]), Res()) for i in range(3)])
    Sret = sb("Sret", [128, RET_H, 2, 256]); r_Sret = [Res() for _ in range(RET_H)]
    Shg = sb("Shg", [128, HG_H, 128]); r_Shg = [Res() for _ in range(HG_H)]
    lB = sb("lB", [128, KC, 2, 128], BF16); r_lB = Res()
    lC = sb("lC", [128, 64, 2, 32], BF16); r_lC = Res()
    A1 = sb("A1", [128, 64, 2]); A2 = sb("A2", [128, 64, 2]); r_A = Res()
    Hst = sb("Hst", [128, 64, 2]); r_Hst = Res()
    A16a = sb("A16a", [128, 64, 2]); A16b = sb("A16b", [128, 64, 2]); r_A16 = Res()
    fsc = sb("fsc", [128, 2]); r_fsc = Res()

    ARENA = 51968
    arena = sb("arena", [128, ARENA // 4])
    arena_tokens = []

    class Carver:
        def __init__(self):
            self.off = 0

        def get(self, shape, dt=F32):
            esz = 4 if dt == F32 or dt == I32 else 2
            n = 1
            for d_ in shape:
                n *= d_
            nb = (n * esz + 3) // 4 * 4
            assert self.off + nb <= ARENA, (self.off, nb)
            v = arena[:, self.off // 4:(self.off + nb) // 4]
            if dt != F32:
                v = v.bitcast(dt)
            v = v[:, :n]
            if len(shape) == 2:
                v = v.rearrange("p (a b) -> p a b", b=shape[1])
            elif len(shape) == 3:
                v = v.rearrange("p (a b c) -> p a b c", b=shape[1], c=shape[2])
            self.off += nb
            return v

    def tok(name=""):
        r = Res(name)
        arena_tokens.append(r)
        return r

    def fence():
        P.op("dve", lambda e: e.memset(fsc[:, 0:1], 0.0), writes=[r_fsc] + arena_tokens)

    cv = Carver()
    xin = Rot([(cv.get([D]), tok("xin%d" % i)) for i in range(4)])
    cv = Carver()
    act = cv.get([22, NT], BF16); r_act = [tok("act%d" % k) for k in range(22)]
    cv = Carver()
    rott = cv.get([2, NT]); r_rot = tok("rot")
    bfA = cv.get([2, NT], BF16); r_bfA = tok()
    bfB = cv.get([2, NT], BF16); r_bfB = tok()
    bfC = cv.get([2, TT], BF16); r_bfC = tok()
    vtm = cv.get([NB, 256], BF16); r_vtm = tok()
    kdtm = cv.get([NB, 256], BF16); r_kdtm = tok()
    oT = cv.get([2, NT]); r_oT = tok()
    gT = cv.get([2, NT], BF16); r_gT = tok()
    attm = Rot([(cv.get([128], BF16), tok()) for i in range(2)])
    SbAllR = cv.get([NB + 1, 2, 256], BF16); r_SbAllR = [tok() for _ in range(NB + 1)]
    vs = cv.get([256], BF16); r_vs = tok()
    ktms = cv.get([256], BF16); r_ktms = tok()
    vmask = Rot([(cv.get([256], BF16), tok()) for i in range(4)])
    s0 = Rot([(cv.get([2, 256]), tok()) for i in range(4)])
    snb = Rot([(cv.get([2, 256], BF16), tok()) for i in range(3)])
    hf = cv.get([2, NT]); r_hf = tok()
    hb = cv.get([2, NT]); r_hb = tok()
    fsm = cv.get([2, NS]); r_fsm = tok()
    hebl = cv.get([2, TT // 64]); r_hebl = tok()
    hblast = cv.get([2, TT // 64]); r_hblast = tok()
    SbAll = cv.get([TT // 64 + 1, 128], BF16); r_SbAll = [tok() for _ in range(TT // 64 + 1)]
    cv = Carver()
    TC = 16
    HBraw = [cv.get([TC, 64, 2]), cv.get([TC, 64, 2])]
    HBs = [x_.rearrange("p t j r -> p j r t") for x_ in HBraw]; r_HBs = [tok("HBa"), tok("HBb")]
    HBb_raw = cv.get([64 * 2 * TC]); r_HBb = tok()
    HBbs = [HBb_raw[:, 0:64 * TC].bitcast(BF16).rearrange("p (j r t) -> p j r t", r=2, t=TC),
            HBb_raw[:, 64 * TC:128 * TC].bitcast(BF16).rearrange("p (j r t) -> p j r t", r=2, t=TC)]
    r_HBbs = [tok("HBba"), tok("HBbb")]
    st1 = cv.get([64, 2]); st2 = cv.get([64, 2]); r_st = tok()
    Pw = [cv.get([64, 2]), cv.get([64, 2])]; cbuf = cv.get([64, 2]); r_pw = tok()
    dcolT = pft[:, 120:136]

    banks = [(stack.enter_context(nc.psum_tensor("ps%d" % i, [128, 512], F32))[:], Res("ps%d" % i)) for i in range(8)]
    psA = Rot(banks[0:4])
    psB = Rot(banks[4:6])
    psC = Rot(banks[6:8])

    g_attn = pft[:, 0:16]; g_ffn0 = pft[:, 16:32]; g_ssm = pft[:, 32:48]; g_ffn1 = pft[:, 48:64]; g_fin = pft[:, 64:80]
    g_retgn = pft[:, 80:88]; g_hggn = pft[:, 88:96]; lbraw = pft[:, 96:120]

    groups0 = [(0, TT), (TT, NS)]

    P.op("sp", lambda e: e.dma_start(out=cst, in_=consts), writes=[r_cst], dma=True)
    P.op("sp", lambda e: e.dma_start(out=pft, in_=pf), writes=[r_pft], dma=True)
    P.op("dve", lambda e: e.tensor_copy(out=identb, in_=identf), reads=[r_cst], writes=[r_identb])
    P.op("dve", lambda e: e.tensor_copy(out=onesb, in_=onesf), reads=[r_cst], writes=[r_identb])
    for h in range(RET_H):
        P.op("dve", lambda e, h=h: e.memset(Sret[:, h], 0.0), writes=[r_Sret[h]])
    for h in range(HG_H):
        P.op("dve", lambda e, h=h: e.memset(Shg[:, h], 0.0), writes=[r_Shg[h]])
    P.op("dve", lambda e: e.memset(Hst, 0.0), writes=[r_Hst])
    lbe = sb("lbe", [128, 24]); lbc = sb("lbc", [128, 8]); oml = sb("oml", [128, 8]); r_lb = Res()
    P.op("act", lambda e: e.activation(out=lbe, in_=lbraw, func=AF.Exp), reads=[r_pft], writes=[r_lb])

    def _lb(e):
        e = Synced(P, e)
        e.tensor_tensor(out=lbc, in0=lbe[:, 0:8], in1=lbe[:, 8:16], op=ALU.add)
        e.tensor_tensor(out=lbc, in0=lbc, in1=lbe[:, 16:24], op=ALU.add)
        e.reciprocal(out=lbc, in_=lbc)
        e.tensor_tensor(out=lbc, in0=lbc, in1=lbe[:, 0:8], op=ALU.mult)
        return e.tensor_scalar(out=oml, in0=lbc, scalar1=-1.0, scalar2=1.0, op0=ALU.mult, op1=ALU.add)
    P.op("dve", _lb, reads=[r_lb], writes=[r_lb])

    def load_w(dram_blk, nelem):
        ap, res = wsl.get()
        ee = 2048 if nelem % 2048 == 0 else nelem // 2
        src = dram_blk.rearrange("p (s e) -> p s e", e=ee)
        dst = ap[:, :nelem].rearrange("p (s e) -> p s e", e=ee)
        P.op("pool", lambda e: e.dma_start(out=dst, in_=src), writes=[res], dma=True)
        return ap, res

    def groups(it):
        return groups0 if it == SS else [(0, TT)]

    def dense_ps(wap, wres, kcn, bc, m, rhsT, rhs_res, c0, n, pool, msz=128):
        ps, pres = pool.get()
        wv = wap[:, :kcn * bc].rearrange("p (k c) -> p k c", c=bc)

        def f(e):
            for kc in range(kcn):
                ins = e.matmul(ps[:msz, :n], lhsT=wv[:, kc, m * 128:m * 128 + msz], rhs=rhsT[:, kc, c0:c0 + n],
                               start=(kc == 0), stop=(kc == kcn - 1))
            return ins
        P.op("pe", f, reads=[wres] + list(rhs_res), writes=[pres])
        return ps, pres

    def rmsnorm(gcols, out_ap, out_res_list, grps, in_place=False):
        stats = []
        for (c0, n) in grps:
            ps, pres = psC.get()
            for kc in range(KC):
                s_ap, s_res = sq.get()
                P.op("act", lambda e, s_ap=s_ap, kc=kc, c0=c0, n=n: e.activation(out=s_ap[:, :n], in_=xT[:, kc, c0:c0 + n], func=AF.Square),
                     reads=[r_xT[kc]], writes=[s_res])
                P.op("pe", lambda e, ps=ps, s_ap=s_ap, kc=kc, n=n: e.matmul(ps[:, :n], lhsT=onesb, rhs=s_ap[:, :n], start=(kc == 0), stop=(kc == KC - 1)),
                     reads=[s_res, r_cst, r_identb], writes=[pres])
            stats.append((ps, pres, c0, n))
        for (ps, pres, c0, n) in stats:
            P.op("act", lambda e, ps=ps, c0=c0, n=n: e.activation(out=rstd[:, c0:c0 + n], in_=ps[:, :n], func=AF.Sqrt, scale=1.0 / D, bias=epscol),
                 reads=[pres, r_cst], writes=[r_rstd])
            P.op("dve", lambda e, c0=c0, n=n: e.reciprocal(out=rstd[:, c0:c0 + n], in_=rstd[:, c0:c0 + n]), reads=[r_rstd], writes=[r_rstd])
        ntot = grps[-1][0] + grps[-1][1]
        for kc in range(KC):
            wr = [out_res_list[kc]] if len(out_res_list) > 1 else list(out_res_list)
            P.op("dve", lambda e, kc=kc: e.scalar_tensor_tensor(out=out_ap[:, kc, :ntot], in0=xT[:, kc, :ntot], scalar=gcols[:, kc:kc + 1],
                                                                in1=rstd[:, :ntot], op0=ALU.mult, op1=ALU.mult),
                 reads=[r_xT[kc], r_rstd, r_pft], writes=wr)

    def add_resid(ps, pres, oc, c0, n):
        P.op("dve", lambda e: e.tensor_tensor(out=xT[:, oc, c0:c0 + n], in0=xT[:, oc, c0:c0 + n], in1=ps[:, :n], op=ALU.add),
             reads=[pres, r_xT[oc]], writes=[r_xT[oc]])

    def ffn(l, grps):
        rmsnorm(g_ffn0 if l == 0 else g_ffn1, hT, [r_hT], grps)
        for half in range(2):
            for blk in range(11):
                wg, wgr = load_w(w_fg[l, half * 11 + blk], 4096)
                wu, wur = load_w(w_fu[l, half * 11 + blk], 4096)
                for m in range(2):
                    c = blk * 2 + m
                    for (c0, n) in grps:
                        pg, pgr = dense_ps(wg, wgr, KC, 256, m, hT, [r_hT], c0, n, psA)
                        pu, pur = dense_ps(wu, wur, KC, 256, m, hT, [r_hT], c0, n, psA)
                        t_ap, t_res = tmpf.get()
                        P.op("act", lambda e, t_ap=t_ap, pg=pg, n=n: e.activation(out=t_ap[:, :n], in_=pg[:, :n], func=AF.Silu),
                             reads=[pgr], writes=[t_res])
                        P.op("dve", lambda e, t_ap=t_ap, pu=pu, c=c, c0=c0, n=n: e.tensor_tensor(out=act[:, c, c0:c0 + n], in0=t_ap[:, :n], in1=pu[:, :n], op=ALU.mult),
                             reads=[t_res, pur], writes=[r_act[c]])
            for oc in range(KC):
                wd, wdr = load_w(w_fd[l, half, oc], 22 * 128)
                for (c0, n) in grps:
                    ps, pres = dense_ps(wd, wdr, 22, 128, 0, act, r_act, c0, n, psA)
                    add_resid(ps, pres, oc, c0, n)

    dd_list = []

    def dd(name, ap, res_list):
        if not dbg or "dd" not in dbg:
            return
        shp = list(ap.shape)
        n = 1
        for d_ in shp[1:]:
            n *= d_
        t = nc.dram_tensor("dd_" + name, shp, ap.dtype, kind="ExternalOutput").ap()
        P.op("sp", lambda e: e.dma_start(out=t, in_=ap), reads=list(res_list), dma=True)

    def dump_dbg():
        P.op("sp", lambda e: e.dma_start(out=dbg_out, in_=xT.rearrange("p k n -> p (k n)")), reads=r_xT, dma=True)

    def bc_mid(ap2, n):
        a = ap2.ap
        return bass.AP(ap2.tensor, ap2.offset, [list(a[0]), [0, n], list(a[1])])

    def bc_last(apx, n):
        a = apx.ap
        return bass.AP(apx.tensor, apx.offset, [list(x) for x in a] + [[0, n]])

    def bank_bf(ps):
        return ps.bitcast(BF16)

    PI = math.pi

    def s5_setup():
        cvs = Carver()
        raw = cvs.get([192 + 4096]); r_raw = tok("s5raw")
        E = [cvs.get([KC * 128]), cvs.get([KC * 128])]; r_E = tok("E")
        Bb = [cvs.get([64, 16]), cvs.get([64, 16])]; r_Bb = tok("Bb")
        sm = cvs.get([16, 64]); r_sm = tok("sm")
        ki = cvs.get([64], I32)
        lr, li, ldt = raw[:, 0:64], raw[:, 64:128], raw[:, 128:192]
        br = raw[:, 192:1216].rearrange("p (j c) -> p j c", c=16)
        bi = raw[:, 1216:2240].rearrange("p (j c) -> p j c", c=16)
        cre = raw[:, 2240:3264].rearrange("p (j c) -> p j c", c=16)
        cim = raw[:, 3264:4288].rearrange("p (j c) -> p j c", c=16)
        P.op("sp", lambda e: e.dma_start(out=raw, in_=s5p), writes=[r_raw], dma=True)
        dt, z, p_, ang, r_, sin_a, cos_a, arm1, mag, ai, den, cr, ci, t1, t2, ar = [sm[:, i, :] for i in range(16)]
        P.op("act", lambda e: e.activation(out=dt, in_=ldt, func=AF.Exp), reads=[r_raw], writes=[r_sm])

        def reduce_sin(e, src_ang, shift, dst):
            e.tensor_scalar(out=t1, in0=src_ang, scalar1=shift, scalar2=1.0 / TWO_PI, op0=ALU.add, op1=ALU.mult)
            e.tensor_copy(out=ki, in_=t1)
            e.tensor_copy(out=t2, in_=ki)
            e.tensor_scalar(out=t1, in0=src_ang, scalar1=shift, scalar2=None, op0=ALU.add)
            e.scalar_tensor_tensor(out=dst, in0=t2, scalar=-TWO_PI, in1=t1, op0=ALU.mult, op1=ALU.add)
            return e.tensor_scalar(out=dst, in0=dst, scalar1=PI, scalar2=-PI, op0=ALU.min, op1=ALU.max)

        def f1(e):
            e = Synced(P, e)
            e.tensor_tensor(out=z, in0=lr, in1=dt, op=ALU.mult)
            e.tensor_scalar(out=p_, in0=z, scalar1=1.0 / 120, scalar2=1.0 / 24, op0=ALU.mult, op1=ALU.add)
            for cst_ in (1.0 / 6, 0.5, 1.0):
                e.tensor_tensor(out=p_, in0=p_, in1=z, op=ALU.mult)
                e.tensor_scalar(out=p_, in0=p_, scalar1=cst_, scalar2=None, op0=ALU.add)
            e.tensor_tensor(out=p_, in0=p_, in1=z, op=ALU.mult)
            e.tensor_tensor(out=ang, in0=li, in1=dt, op=ALU.mult)
            reduce_sin(e, ang, 0.0, r_)
            return reduce_sin(e, ang, PI / 2, arm1)
        P.op("dve", f1, reads=[r_raw, r_sm], writes=[r_sm])

        def f2(e):
            e.activation(out=sin_a, in_=r_, func=AF.Sin)
            return e.activation(out=cos_a, in_=arm1, func=AF.Sin)
        P.op("act", f2, reads=[r_sm], writes=[r_sm])

        def f3(e):
            e = Synced(P, e)
            e.tensor_tensor(out=t1, in0=p_, in1=cos_a, op=ALU.mult)
            e.tensor_scalar(out=t2, in0=cos_a, scalar1=-1.0, scalar2=None, op0=ALU.add)
            e.tensor_tensor(out=arm1, in0=t1, in1=t2, op=ALU.add)
            e.tensor_scalar(out=ar, in0=arm1, scalar1=1.0, scalar2=None, op0=ALU.add)
            e.tensor_scalar(out=mag, in0=p_, scalar1=1.0, scalar2=None, op0=ALU.add)
            e.tensor_tensor(out=ai, in0=mag, in1=sin_a, op=ALU.mult)
            e.tensor_tensor(out=den, in0=lr, in1=lr, op=ALU.mult)
            e.tensor_tensor(out=t1, in0=li, in1=li, op=ALU.mult)
            e.tensor_tensor(out=den, in0=den, in1=t1, op=ALU.add)
            e.reciprocal(out=den, in_=den)
            e.tensor_tensor(out=t1, in0=arm1, in1=lr, op=ALU.mult)
            e.tensor_tensor(out=t2, in0=ai, in1=li, op=ALU.mult)
            e.tensor_tensor(out=t1, in0=t1, in1=t2, op=ALU.add)
            e.tensor_tensor(out=cr, in0=t1, in1=den, op=ALU.mult)
            e.tensor_tensor(out=t1, in0=ai, in1=lr, op=ALU.mult)
            e.tensor_tensor(out=t2, in0=arm1, in1=li, op=ALU.mult)
            e.tensor_tensor(out=t1, in0=t1, in1=t2, op=ALU.subtract)
            e.tensor_tensor(out=ci, in0=t1, in1=den, op=ALU.mult)
            e.tensor_copy(out=A1[:, :, 0], in_=ar)
            e.tensor_copy(out=A1[:, :, 1], in_=ar)
            e.tensor_scalar(out=A2[:, :, 0], in0=ai, scalar1=-1.0, scalar2=None, op0=ALU.mult)
            e.tensor_copy(out=A2[:, :, 1], in_=ai)
            crb, cib = bc_last(cr, 16), bc_last(ci, 16)
            Er = E[0].rearrange("p (j g c) -> p j g c", g=2, c=16)
            Ei = E[1].rearrange("p (j g c) -> p j g c", g=2, c=16)
            e.tensor_tensor(out=Bb[0], in0=br, in1=crb, op=ALU.mult)
            e.tensor_tensor(out=Bb[1], in0=bi, in1=cib, op=ALU.mult)
            e.tensor_tensor(out=Bb[0], in0=Bb[0], in1=Bb[1], op=ALU.subtract)
            e.tensor_tensor(out=Bb[1], in0=bi, in1=crb, op=ALU.mult)
            for g2 in range(2):
                e.tensor_scalar(out=Er[:, :, g2, :], in0=Bb[0], scalar1=cst[:, C_HALF + g2:C_HALF + g2 + 1], scalar2=None, op0=ALU.mult)
            e.tensor_tensor(out=Bb[0], in0=br, in1=cib, op=ALU.mult)
            e.tensor_tensor(out=Bb[1], in0=Bb[1], in1=Bb[0], op=ALU.add)
            for g2 in range(2):
                e.tensor_scalar(out=Ei[:, :, g2, :], in0=Bb[1], scalar1=cst[:, C_HALF + g2:C_HALF + g2 + 1], scalar2=None, op0=ALU.mult)
            lC5 = lC.rearrange("p j r (g c) -> p j r g c", c=16)
            for ri, (cc, sgn) in enumerate(((cre, 1.0), (cim, -1.0))):
                for g2 in range(2):
                    ins = e.tensor_scalar(out=lC5[:, :, ri, g2, :], in0=cc, scalar1=cst[:, C_HALF + g2:C_HALF + g2 + 1], scalar2=sgn, op0=ALU.mult, op1=ALU.mult)
            return ins
        P.op("dve", f3, reads=[r_raw, r_sm, r_cst], writes=[r_sm, r_A, r_Bb, r_E, r_lC])
        for ri in range(2):
            for g in range(4):
                ps, pres = psB.get()

                def ft(e, ps=ps, ri=ri, g=g):
                    for q in range(4):
                        kc = g * 4 + q
                        ins = e.transpose(ps[:, q * 128:(q + 1) * 128], E[ri][:, kc * 128:(kc + 1) * 128], identf)
                    return ins
                P.op("pe", ft, reads=[r_E, r_cst], writes=[pres])
                P.op("act", lambda e, ps=ps, ri=ri, g=g: e.activation(out=lB[:, g * 4:g * 4 + 4, ri, :], in_=ps.rearrange("p (q c) -> p q c", c=128), func=AF.Copy),
                     reads=[pres], writes=[r_lB])

    s5_setup()
    fence()

    def ret_head(h, it, grps, last):
        samp = (it == SS)
        ntot = grps[-1][0] + grps[-1][1]
        wq, wqr = load_w(w_in[h], 4096)
        wk, wkr = load_w(w_in[4 + h], 4096)
        wv, wvr = load_w(w_in[8 + h], 4096)
        wg, wgr = load_w(w_in[12 + h], 4096)
        for (w, wr, dst, dres) in ((wq, wqr, bfA, r_bfA), (wk, wkr, bfB, r_bfB)):
            for (c0, n) in grps:
                p1, p1r = dense_ps(w, wr, KC, 256, 0, hT, [r_hT], c0, n, psA)
                p2, p2r = dense_ps(w, wr, KC, 256, 1, hT, [r_hT], c0, n, psA)
                ta, tar = tmpf.get()
                tb_, tbr = tmpf.get()

                def fr(e, p1=p1, p2=p2, ta=ta, tb_=tb_, dst=dst, c0=c0, n=n):
                    e = Synced(P, e, n < 300)
                    cos, sin = rott[:, 0, c0:c0 + n], rott[:, 1, c0:c0 + n]
                    e.tensor_tensor(out=ta[:, :n], in0=p1[:, :n], in1=cos, op=ALU.mult)
                    e.tensor_tensor(out=tb_[:, :n], in0=p2[:, :n], in1=sin, op=ALU.mult)
                    e.tensor_tensor(out=dst[:, 0, c0:c0 + n], in0=ta[:, :n], in1=tb_[:, :n], op=ALU.subtract)
                    e.tensor_tensor(out=ta[:, :n], in0=p1[:, :n], in1=sin, op=ALU.mult)
                    e.tensor_tensor(out=tb_[:, :n], in0=p2[:, :n], in1=cos, op=ALU.mult)
                    return e.tensor_tensor(out=dst[:, 1, c0:c0 + n], in0=ta[:, :n], in1=tb_[:, :n], op=ALU.add)
                P.op("dve", fr, reads=[p1r, p2r, r_rot], writes=[tar, tbr, dres])
        qdtab = bc_mid(cst[:, C_QDEC + h * 128:C_QDEC + (h + 1) * 128], NB)

        def fqd(e):
            for dc in range(2):
                ins = e.tensor_tensor(out=bfC[:, dc, :].rearrange("p (b c) -> p b c", c=128),
                                      in0=bfA[:, dc, 0:TT].rearrange("p (b c) -> p b c", c=128), in1=qdtab, op=ALU.mult)
            return ins
        P.op("pool", fqd, reads=[r_bfA, r_cst], writes=[r_bfC])
        for tb in range(NB):
            ps, pres = psB.get()

            def fv(e, ps=ps, tb=tb):
                wv3 = wv.rearrange("p (k c) -> p k c", c=256)
                for kc in range(KC):
                    ins = e.matmul(ps[:, :256], lhsT=hT[:, kc, tb * 128:(tb + 1) * 128], rhs=wv3[:, kc, :], start=(kc == 0), stop=(kc == KC - 1))
                return ins
            P.op("pe", fv, reads=[wvr, r_hT], writes=[pres])
            P.op("act", lambda e, ps=ps, tb=tb: e.activation(out=vtm[:, tb, :], in_=ps[:, :256], func=AF.Copy), reads=[pres], writes=[r_vtm])
        if samp:
            ps, pres = psB.get()

            def fvs(e, ps=ps):
                wv3 = wv.rearrange("p (k c) -> p k c", c=256)
                for kc in range(KC):
                    ins = e.matmul(ps[:NS, :256], lhsT=hT[:, kc, TT:NT], rhs=wv3[:, kc, :], start=(kc == 0), stop=(kc == KC - 1))
                return ins
            P.op("pe", fvs, reads=[wvr, r_hT], writes=[pres])
            P.op("act", lambda e, ps=ps: e.activation(out=vs[:NS, :], in_=ps[:NS, :256], func=AF.Copy, scale=1.0 / 16), reads=[pres], writes=[r_vs])
        gate_jobs = []
        for vc in range(2):
            for (c0, n) in grps:
                def gj(vc=vc, c0=c0, n=n):
                    ps, pres = dense_ps(wg, wgr, KC, 256, vc, hT, [r_hT], c0, n, psA)
                    P.op("act", lambda e, ps=ps, vc=vc, c0=c0, n=n: e.activation(out=gT[:, vc, c0:c0 + n], in_=ps[:, :n], func=AF.Silu), reads=[pres], writes=[r_gT])
                gate_jobs.append(gj)
        for tb in range(NB):
            ps, pres = psC.get()
            pb = bank_bf(ps)

            def fk(e, pb=pb, tb=tb):
                for dc in range(2):
                    ins = e.transpose(pb[:, dc * 128:(dc + 1) * 128], bfB[:, dc, tb * 128:(tb + 1) * 128], identb)
                return ins
            P.op("pe", fk, reads=[r_bfB, r_identb], writes=[pres])
            P.op("act", lambda e, pb=pb, tb=tb: e.activation(out=kdtm[:, tb, :], in_=pb[:, 0:256], func=AF.Copy, scale=cst[:, C_KDEC + h:C_KDEC + h + 1]),
                 reads=[pres, r_cst], writes=[r_kdtm])
        if samp:
            ps, pres = psC.get()
            pb = bank_bf(ps)

            def fks(e, pb=pb):
                for dc in range(2):
                    ins = e.transpose(pb[:NS, dc * 128:(dc + 1) * 128], bfB[:, dc, TT:NT], identb)
                return ins
            P.op("pe", fks, reads=[r_bfB, r_identb], writes=[pres])
            P.op("dve", lambda e, pb=pb: e.tensor_copy(out=ktms[:NS, :], in_=pb[:NS, 0:256]), reads=[pres], writes=[r_ktms])
        if h == 0 and it == 0:
            dd("r_qT", bfA, [r_bfA]); dd("r_kT", bfB, [r_bfB]); dd("r_vs", vs[:NS, :], [r_vs]); dd("r_ktms", ktms[:NS, :], [r_ktms])
            dd("r_vtm", vtm, [r_vtm]); dd("r_kdtm", kdtm, [r_kdtm]); dd("r_hT", hT, [r_hT])
            dd("rott", rott, [r_rot]); dd("lbe", lbe, [r_lb]); dd("lbc", lbc, [r_lb]); dd("oml", oml, [r_lb]); dd("pft", pft, [r_pft])
        P.op("act", lambda e: e.activation(out=SbAllR[:, 0], in_=Sret[:, h], func=AF.Copy), reads=[r_Sret[h]], writes=[r_SbAllR[0]])
        for tb in range(NB):
            for dc in range(2):
                pss, pssr = psC.get()
                P.op("pe", lambda e, pss=pss, tb=tb, dc=dc: e.matmul(pss[:, :256], lhsT=kdtm[:, tb, dc * 128:(dc + 1) * 128], rhs=vtm[:, tb, :], start=True, stop=True),
                     reads=[r_kdtm, r_vtm], writes=[pssr])
                P.op("dve", lambda e, pss=pss, dc=dc: e.scalar_tensor_tensor(out=Sret[:, h, dc, :], in0=Sret[:, h, dc, :], scalar=cdec[h], in1=pss[:, :256], op0=ALU.mult, op1=ALU.add),
                     reads=[pssr, r_Sret[h]], writes=[r_Sret[h]])
            if tb < NB - 1:
                P.op("act", lambda e, tb=tb: e.activation(out=SbAllR[:, tb + 1], in_=Sret[:, h], func=AF.Copy), reads=[r_Sret[h]], writes=[r_SbAllR[tb + 1]])
        mk = cst[:, C_MASKT + h * 128:C_MASKT + (h + 1) * 128]
        for tb in range(NB):
            cs = slice(tb * 128, (tb + 1) * 128)
            ps, pres = psB.get()

            def fa(e, ps=ps, cs=cs):
                for dc in range(2):
                    ins = e.matmul(ps[:, :128], lhsT=bfB[:, dc, cs], rhs=bfA[:, dc, cs], start=(dc == 0), stop=(dc == 1))
                return ins
            P.op("pe", fa, reads=[r_bfA, r_bfB], writes=[pres])
            am, amr = attm.get()
            P.op("dve", lambda e, ps=ps, am=am: e.tensor_tensor(out=am, in0=ps[:, :128], in1=mk, op=ALU.mult), reads=[pres, r_cst], writes=[amr])
            if gate_jobs:
                gate_jobs.pop(0)()
            po, por = psB.get()

            def fo(e, po=po, am=am, tb=tb, cs=cs):
                for vc in range(2):
                    e.matmul(po[:, vc * 128:(vc + 1) * 128], lhsT=vtm[:, tb, vc * 128:(vc + 1) * 128], rhs=am, start=True, stop=False)
                    for dc in range(2):
                        ins = e.matmul(po[:, vc * 128:(vc + 1) * 128], lhsT=SbAllR[:, tb, dc, vc * 128:(vc + 1) * 128], rhs=bfC[:, dc, cs], start=False, stop=(dc == 1))
                return ins
            P.op("pe", fo, reads=[amr, r_vtm, r_SbAllR[tb], r_bfC], writes=[por])
            P.op("act", lambda e, po=po, cs=cs: e.activation(out=oT[:, :, cs], in_=po[:, :256].rearrange("p (v c) -> p v c", c=128), func=AF.Copy), reads=[por], writes=[r_oT])
        while gate_jobs:
            gate_jobs.pop(0)()
        if last:
            P.op("sp", lambda e: e.dma_start(out=ret_p[h].rearrange("(dc p) v -> p dc v", p=128), in_=Sret[:, h]), reads=[r_Sret[h]], dma=True)
        if samp:
            pre = {}

            def ld_ret(b):
                sa, sar = s0.get()
                P.op("sp", lambda e, sa=sa, b=b: e.dma_start(out=sa, in_=st_ret[b, h].rearrange("(dc p) v -> p dc v", p=128)), writes=[sar], dma=True)
                pre[b] = (sa, sar)
            prevm = {}

            def mk_vm(b):
                vm, vmr = vmask.get()
                P.op("act", lambda e, vm=vm, b=b: e.activation(out=vm[:NS, :], in_=vs[:NS, :], func=AF.Copy, scale=identf[:NS, b:b + 1]),
                     reads=[r_vs, r_cst], writes=[vmr])
                prevm[b] = (vm, vmr)
            ld_ret(0)
            ld_ret(1)
            mk_vm(0)
            mk_vm(1)
            for b in range(NS):
                if b + 2 < NS:
                    ld_ret(b + 2)
                    mk_vm(b + 2)
                vm, vmr = prevm.pop(b)
                sa, sar = pre.pop(b)
                for dc in range(2):
                    pss, pssr = psC.get()
                    P.op("pe", lambda e, pss=pss, vm=vm, dc=dc: e.matmul(pss[:, :256], lhsT=ktms[:NS, dc * 128:(dc + 1) * 128], rhs=vm[:NS, :], start=True, stop=True),
                         reads=[r_ktms, vmr], writes=[pssr])
                    P.op("dve", lambda e, pss=pss, sa=sa, dc=dc: e.scalar_tensor_tensor(out=sa[:, dc, :], in0=sa[:, dc, :], scalar=gam[h], in1=pss[:, :256], op0=ALU.mult, op1=ALU.add),
                         reads=[pssr, sar], writes=[sar])
                P.op("sp", lambda e, sa=sa, b=b: e.dma_start(out=ret_s[b, h].rearrange("(dc p) v -> p dc v", p=128), in_=sa), reads=[sar], dma=True)
                sn, snr = snb.get()
                P.op("act", lambda e, sn=sn, sa=sa: e.activation(out=sn, in_=sa, func=AF.Copy), reads=[sar], writes=[snr])

                def part2(sn=sn, snr=snr, b=b):
                    po, por = psB.get()

                    def fso(e, po=po, sn=sn, b=b):
                        for vc in range(2):
                            for dc in range(2):
                                ins = e.matmul(po[:, vc:vc + 1], lhsT=sn[:, dc, vc * 128:(vc + 1) * 128], rhs=bfA[:, dc, TT + b:TT + b + 1], start=(dc == 0), stop=(dc == 1))
                        return ins
                    P.op("pe", fso, reads=[snr, r_bfA], writes=[por])
                    P.op("dve", lambda e, po=po, b=b: e.tensor_copy(out=oT[:, :, TT + b], in_=po[:, 0:2]), reads=[por], writes=[r_oT])
                if b > 0:
                    prev2()
                prev2 = part2
            prev2()
        if h == 0 and it == 0:
            dd("r_oT", oT, [r_oT]); dd("r_gT", gT, [r_gT])
        for (c0, n) in grps:
            psm, psmr = psC.get()
            P.op("pe", lambda e, psm=psm, c0=c0, n=n: [e.matmul(psm[:, :n], lhsT=onesf, rhs=oT[:, vc, c0:c0 + n], start=(vc == 0), stop=(vc == 1)) for vc in range(2)][-1],
                 reads=[r_oT, r_cst], writes=[psmr])
            psq, psqr = psC.get()
            for vc in range(2):
                s_ap, s_res = sq.get()
                P.op("act", lambda e, s_ap=s_ap, vc=vc, c0=c0, n=n: e.activation(out=s_ap[:, :n], in_=oT[:, vc, c0:c0 + n], func=AF.Square), reads=[r_oT], writes=[s_res])
                P.op("pe", lambda e, psq=psq, s_ap=s_ap, vc=vc, n=n: e.matmul(psq[:, :n], lhsT=onesb, rhs=s_ap[:, :n], start=(vc == 0), stop=(vc == 1)),
                     reads=[s_res, r_cst, r_identb], writes=[psqr])
            mean, meanr = tmpf.get()
            rs, rsr = tmpf.get()
            P.op("act", lambda e, mean=mean, psm=psm, n=n: e.activation(out=mean[:, :n], in_=psm[:, :n], func=AF.Copy, scale=1.0 / 256), reads=[psmr], writes=[meanr])

            def fvar(e, mean=mean, rs=rs, psq=psq, n=n):
                e = Synced(P, e, n < 300)
                e.tensor_tensor(out=rs[:, :n], in0=mean[:, :n], in1=mean[:, :n], op=ALU.mult)
                return e.scalar_tensor_tensor(out=rs[:, :n], in0=psq[:, :n], scalar=1.0 / 256, in1=rs[:, :n], op0=ALU.mult, op1=ALU.subtract)
            P.op("dve", fvar, reads=[meanr, psqr], writes=[rsr])
            P.op("act", lambda e, rs=rs, n=n: e.activation(out=rs[:, :n], in_=rs[:, :n], func=AF.Sqrt, bias=epscol), reads=[rsr, r_cst], writes=[rsr])

            def fgn(e, mean=mean, rs=rs, c0=c0, n=n):
                e = Synced(P, e, n < 300)
                e.reciprocal(out=rs[:, :n], in_=rs[:, :n])
                for vc in range(2):
                    o_ = oT[:, vc, c0:c0 + n]
                    e.tensor_tensor(out=o_, in0=o_, in1=mean[:, :n], op=ALU.subtract)
                    e.tensor_tensor(out=o_, in0=o_, in1=rs[:, :n], op=ALU.mult)
                    ins = e.scalar_tensor_tensor(out=mixT[:, h * 2 + vc, c0:c0 + n], in0=o_, scalar=g_retgn[:, h * 2 + vc:h * 2 + vc + 1], in1=gT[:, vc, c0:c0 + n],
                                                 op0=ALU.mult, op1=ALU.mult)
                return ins
            P.op("dve", fgn, reads=[meanr, rsr, r_oT, r_gT, r_pft], writes=[rsr, r_oT, r_mix[h * 2], r_mix[h * 2 + 1]])

    def hg_pair(hp, it, grps, last):
        samp = (it == SS)
        ntot = grps[-1][0] + grps[-1][1]
        NCH = TT // 64
        wq, wqr = load_w(w_in[16 + hp], 4096)
        wf, wfr = load_w(w_in[20 + hp], 4096)
        wi, wir = load_w(w_in[24 + hp], 4096)
        wg, wgr = load_w(w_in[28 + hp], 4096)
        resetm = cst[:, C_RESET:C_RESET + TT]
        for m in range(2):
            hd = hp * 2 + m
            for (c0, n) in grps:
                ps, pres = dense_ps(wf, wfr, KC, 256, m, hT, [r_hT], c0, n, psA)
                ta, tar = tmpf.get()
                P.op("act", lambda e, ps=ps, ta=ta, n=n: e.activation(out=ta[:, :n], in_=ps[:, :n], func=AF.Sigmoid), reads=[pres], writes=[tar])
                P.op("dve", lambda e, ta=ta, m=m, hd=hd, c0=c0, n=n: e.tensor_scalar(out=hf[:, m, c0:c0 + n], in0=ta[:, :n], scalar1=oml[:, hd:hd + 1], scalar2=lbc[:, hd:hd + 1],
                                                                                       op0=ALU.mult, op1=ALU.add), reads=[tar, r_lb], writes=[r_hf])
            ta, tar = tmpf.get()
            P.op("act", lambda e, ta=ta, m=m: e.activation(out=ta[:, :TT], in_=hf[:, m, 0:TT], func=AF.Ln), reads=[r_hf], writes=[tar])
            P.op("dve", lambda e, ta=ta, m=m: e.tensor_tensor_scan(out=hb[:, m, 0:TT], data0=resetm, data1=ta[:, :TT], initial=0.0, op0=ALU.mult, op1=ALU.add),
                 reads=[tar, r_cst], writes=[r_hb])
            if samp:
                P.op("dve", lambda e, m=m: e.tensor_copy(out=fsm[:, m, :], in_=hf[:, m, TT:NT]), reads=[r_hf], writes=[r_fsm])
            P.op("dve", lambda e, m=m: e.tensor_copy(out=hblast[:, m, :], in_=hb[:, m, 63:TT:64]), reads=[r_hb], writes=[r_hblast])
            P.op("act", lambda e, m=m: e.activation(out=hebl[:, m, :], in_=hblast[:, m, :], func=AF.Exp), reads=[r_hblast], writes=[r_hebl])
            P.op("dve", lambda e, m=m: e.tensor_scalar(out=hf[:, m, :ntot], in0=hf[:, m, :ntot], scalar1=-1.0, scalar2=1.0, op0=ALU.mult, op1=ALU.add),
                 reads=[r_hf, r_fsm], writes=[r_hf])
            for (c0, n) in grps:
                ps, pres = dense_ps(wq, wqr, KC, 256, m, hT, [r_hT], c0, n, psA)
                if c0 == 0:
                    ta, tar = tmpf.get()
                    tb_, tbr = tmpf.get()
                    P.op("act", lambda e, ta=ta, m=m: e.activation(out=ta[:, :TT], in_=hb[:, m, 0:TT], func=AF.Exp), reads=[r_hb], writes=[tar])
                    P.op("act", lambda e, tb_=tb_, ps=ps: e.activation(out=tb_[:, :TT], in_=ps[:, :TT], func=AF.Silu), reads=[pres], writes=[tbr])
                    P.op("dve", lambda e, ta=ta, tb_=tb_, m=m: e.tensor_tensor(out=bfA[:, m, 0:TT], in0=ta[:, :TT], in1=tb_[:, :TT], op=ALU.mult), reads=[tar, tbr], writes=[r_bfA])
                else:
                    P.op("act", lambda e, ps=ps, m=m: e.activation(out=bfA[:, m, TT:NT], in_=ps[:, :NS], func=AF.Silu), reads=[pres], writes=[r_bfA])
            ta, tar = tmpf.get()
            P.op("act", lambda e, ta=ta, m=m: e.activation(out=ta[:, :TT], in_=hb[:, m, 0:TT], func=AF.Exp, scale=-1.0), reads=[r_hb], writes=[tar])
            P.op("dve", lambda e, ta=ta, m=m: e.tensor_tensor(out=bfB[:, m, 0:TT], in0=hf[:, m, 0:TT], in1=ta[:, :TT], op=ALU.mult), reads=[tar, r_hf], writes=[r_bfB])
            if samp:
                P.op("dve", lambda e, m=m: e.tensor_copy(out=bfB[:, m, TT:NT], in_=hf[:, m, TT:NT]), reads=[r_hf], writes=[r_bfB])
            ta, tar = tmpf.get()

            def fkd(e, ta=ta, m=m):
                for ch in range(NCH):
                    ins = e.activation(out=ta[:, ch * 64:(ch + 1) * 64], in_=hb[:, m, ch * 64:(ch + 1) * 64], func=AF.Exp, scale=-1.0, bias=hblast[:, m, ch:ch + 1])
                return ins
            P.op("act", fkd, reads=[r_hb, r_hblast], writes=[tar])
            P.op("dve", lambda e, ta=ta, m=m: e.tensor_tensor(out=bfC[:, m, :], in0=hf[:, m, 0:TT], in1=ta[:, :TT], op=ALU.mult), reads=[tar, r_hf], writes=[r_bfC])
        hg_gate_jobs = {0: [], 1: []}
        for m in range(2):
            for (c0, n) in grps:
                def gj(m=m, c0=c0, n=n):
                    ps, pres = dense_ps(wg, wgr, KC, 256, m, hT, [r_hT], c0, n, psA)
                    P.op("act", lambda e, ps=ps, m=m, c0=c0, n=n: e.activation(out=gT[:, m, c0:c0 + n], in_=ps[:, :n], func=AF.Silu), reads=[pres], writes=[r_gT])
                hg_gate_jobs[m].append(gj)
        for tb in range(NB):
            ps, pres = psB.get()

            def fv(e, ps=ps, tb=tb):
                w3 = wi.rearrange("p (k c) -> p k c", c=256)
                for kc in range(KC):
                    ins = e.matmul(ps[:, :256], lhsT=hT[:, kc, tb * 128:(tb + 1) * 128], rhs=w3[:, kc, :], start=(kc == 0), stop=(kc == KC - 1))
                return ins
            P.op("pe", fv, reads=[wir, r_hT], writes=[pres])
            P.op("act", lambda e, ps=ps, tb=tb: e.activation(out=vtm[:, tb, :], in_=ps[:, :256], func=AF.Copy), reads=[pres], writes=[r_vtm])
        if samp:
            ps, pres = psB.get()

            def fvs(e, ps=ps):
                w3 = wi.rearrange("p (k c) -> p k c", c=256)
                for kc in range(KC):
                    ins = e.matmul(ps[:NS, :256], lhsT=hT[:, kc, TT:NT], rhs=w3[:, kc, :], start=(kc == 0), stop=(kc == KC - 1))
                return ins
            P.op("pe", fvs, reads=[wir, r_hT], writes=[pres])
            P.op("act", lambda e, ps=ps: e.activation(out=vs[:NS, :], in_=ps[:NS, :256], func=AF.Copy), reads=[pres], writes=[r_vs])
        for tb in range(NB):
            ps, pres = psC.get()
            pb = bank_bf(ps)

            def fk(e, pb=pb, tb=tb):
                for m in range(2):
                    ins = e.transpose(pb[:, m * 128:(m + 1) * 128], bfC[:, m, tb * 128:(tb + 1) * 128], identb)
                return ins
            P.op("pe", fk, reads=[r_bfC, r_identb], writes=[pres])
            P.op("dve", lambda e, pb=pb, tb=tb: e.tensor_copy(out=kdtm[:, tb, :], in_=pb[:, 0:256]), reads=[pres], writes=[r_kdtm])
        if samp:
            ps, pres = psC.get()
            pb = bank_bf(ps)

            def fks(e, pb=pb):
                for m in range(2):
                    ins = e.transpose(pb[:NS, m * 128:(m + 1) * 128], bfB[:, m, TT:NT], identb)
                return ins
            P.op("pe", fks, reads=[r_bfB, r_identb], writes=[pres])
            P.op("dve", lambda e, pb=pb: e.tensor_copy(out=ktms[:NS, :], in_=pb[:NS, 0:256]), reads=[pres], writes=[r_ktms])
        mH = cst[:, C_MASKH:C_MASKH + 128]
        if hp == 0 and it == 0:
            dd("h_k", hf, [r_hf]); dd("h_b", hb, [r_hb]); dd("h_qe", bfA, [r_bfA]); dd("h_ke", bfB, [r_bfB]); dd("h_kd", bfC, [r_bfC])
            dd("h_vtm", vtm, [r_vtm]); dd("h_kdtm", kdtm, [r_kdtm]); dd("h_ebl", hebl, [r_hebl]); dd("h_gT", gT, [r_gT]); dd("h_fsm", fsm, [r_fsm])
            dd("h_vs", vs[:NS, :], [r_vs]); dd("h_ktms", ktms[:NS, :], [r_ktms])
        for m in range(2):
            hd = hp * 2 + m
            ms = slice(m * 128, (m + 1) * 128)
            P.op("act", lambda e, hd=hd: e.activation(out=SbAll[:, 0, :], in_=Shg[:, hd], func=AF.Copy), reads=[r_Shg[hd]], writes=[r_SbAll[0]])
            for ch in range(NCH):
                tb, sub = ch // 2, ch % 2
                rs_ = slice(sub * 64, (sub + 1) * 64)
                pss, pssr = psC.get()
                P.op("pe", lambda e, pss=pss, tb=tb, ms=ms, rs_=rs_: e.matmul(pss[:, :128], lhsT=kdtm[rs_, tb, ms], rhs=vtm[rs_, tb, ms], start=True, stop=True),
                     reads=[r_kdtm, r_vtm], writes=[pssr])
                P.op("dve", lambda e, pss=pss, hd=hd, m=m, ch=ch: e.scalar_tensor_tensor(out=Shg[:, hd], in0=Shg[:, hd], scalar=hebl[:, m, ch:ch + 1], in1=pss[:, :128],
                                                                                         op0=ALU.mult, op1=ALU.add), reads=[pssr, r_Shg[hd], r_hebl], writes=[r_Shg[hd]])
                if ch < NCH - 1:
                    P.op("act", lambda e, hd=hd, ch=ch: e.activation(out=SbAll[:, ch + 1, :], in_=Shg[:, hd], func=AF.Copy), reads=[r_Shg[hd]], writes=[r_SbAll[ch + 1]])
            for tb in range(NB):
                cs = slice(tb * 128, (tb + 1) * 128)
                ps, pres = psB.get()
                P.op("pe", lambda e, ps=ps, m=m, cs=cs: e.matmul(ps[:, :128], lhsT=bfB[:, m, cs], rhs=bfA[:, m, cs], start=True, stop=True), reads=[r_bfA, r_bfB], writes=[pres])
                am, amr = attm.get()
                P.op("dve", lambda e, ps=ps, am=am: e.tensor_tensor(out=am, in0=ps[:, :128], in1=mH, op=ALU.mult), reads=[pres, r_cst], writes=[amr])
                if hg_gate_jobs[m]:
                    hg_gate_jobs[m].pop(0)()
                po, por = psB.get()

                def fo1(e, po=po, am=am, tb=tb, m=m, ms=ms):
                    e.matmul(po[:, 0:128], lhsT=vtm[:, tb, ms], rhs=am, start=True, stop=False)
                    e.matmul(po[:, 0:64], lhsT=SbAll[:, 2 * tb, :], rhs=bfA[:, m, tb * 128:tb * 128 + 64], start=False, stop=False)
                    return e.matmul(po[:, 64:128], lhsT=SbAll[:, 2 * tb + 1, :], rhs=bfA[:, m, tb * 128 + 64:tb * 128 + 128], start=False, stop=True)
                P.op("pe", fo1, reads=[amr, r_vtm, r_SbAll[2 * tb], r_SbAll[2 * tb + 1], r_bfA], writes=[por])
                P.op("act", lambda e, po=po, m=m, cs=cs: e.activation(out=oT[:, m, cs], in_=po[:, :128], func=AF.Copy), reads=[por], writes=[r_oT])
            while hg_gate_jobs[m]:
                hg_gate_jobs[m].pop(0)()
            if last:
                P.op("sp", lambda e, hd=hd: e.dma_start(out=hg_p[hd], in_=Shg[:, hd]), reads=[r_Shg[hd]], dma=True)
            if samp:
                pre = {}

                def ld_hg(b, hd=hd):
                    sa, sar = s0.get()
                    sa2 = sa[:, 0, 0:128]
                    P.op("sp", lambda e, sa2=sa2, b=b, hd=hd: e.dma_start(out=sa2, in_=st_hg[b, hd]), writes=[sar], dma=True)
                    pre[b] = (sa, sar, sa2)
                prevm = {}

                def mk_vm(b, ms=ms):
                    vm, vmr = vmask.get()
                    P.op("act", lambda e, vm=vm, b=b, ms=ms: e.activation(out=vm[:NS, 0:128], in_=vs[:NS, ms], func=AF.Copy, scale=identf[:NS, b:b + 1]),
                         reads=[r_vs, r_cst], writes=[vmr])
                    prevm[b] = (vm, vmr)
                ld_hg(0)
                ld_hg(1)
                mk_vm(0)
                mk_vm(1)
                for b in range(NS):
                    if b + 2 < NS:
                        ld_hg(b + 2)
                        mk_vm(b + 2)
                    vm, vmr = prevm.pop(b)
                    sa, sar, sa2 = pre.pop(b)
                    pss, pssr = psC.get()
                    P.op("pe", lambda e, pss=pss, vm=vm, ms=ms: e.matmul(pss[:, :128], lhsT=ktms[:NS, ms], rhs=vm[:NS, 0:128], start=True, stop=True), reads=[r_ktms, vmr], writes=[pssr])
                    P.op("dve", lambda e, pss=pss, sa2=sa2, m=m, b=b: e.scalar_tensor_tensor(out=sa2, in0=sa2, scalar=fsm[:, m, b:b + 1], in1=pss[:, :128], op0=ALU.mult, op1=ALU.add),
                         reads=[pssr, sar, r_fsm], writes=[sar])
                    P.op("sp", lambda e, sa2=sa2, b=b, hd=hd: e.dma_start(out=hg_s[b, hd], in_=sa2), reads=[sar], dma=True)
                    sn, snr = snb.get()
                    sn2 = sn[:, 0, 0:128]
                    P.op("act", lambda e, sn2=sn2, sa2=sa2: e.activation(out=sn2, in_=sa2, func=AF.Copy), reads=[sar], writes=[snr])

                    def part2(sn2=sn2, snr=snr, m=m, b=b):
                        po, por = psB.get()
                        P.op("pe", lambda e, po=po, sn2=sn2, m=m, b=b: e.matmul(po[:, 0:1], lhsT=sn2, rhs=bfA[:, m, TT + b:TT + b + 1], start=True, stop=True), reads=[snr, r_bfA], writes=[por])
                        P.op("dve", lambda e, po=po, m=m, b=b: e.tensor_copy(out=oT[:, m, TT + b:TT + b + 1], in_=po[:, 0:1]), reads=[por], writes=[r_oT])
                    if b > 0:
                        prev2()
                    prev2 = part2
                prev2()
            if hp == 0 and it == 0 and m == 1:
                dd("h_oT", oT, [r_oT])
            for (c0, n) in grps:
                s_ap, s_res = sq.get()
                P.op("act", lambda e, s_ap=s_ap, m=m, c0=c0, n=n: e.activation(out=s_ap[:, :n], in_=oT[:, m, c0:c0 + n], func=AF.Square), reads=[r_oT], writes=[s_res])
                psq, psqr = psC.get()
                P.op("pe", lambda e, psq=psq, s_ap=s_ap, n=n: e.matmul(psq[:, :n], lhsT=onesb, rhs=s_ap[:, :n], start=True, stop=True), reads=[s_res, r_cst, r_identb], writes=[psqr])
                rs, rsr = tmpf.get()
                P.op("act", lambda e, rs=rs, psq=psq, n=n: e.activation(out=rs[:, :n], in_=psq[:, :n], func=AF.Sqrt, scale=1.0 / 128, bias=epscol), reads=[psqr, r_cst], writes=[rsr])

                def fn_(e, rs=rs, m=m, hd=hd, c0=c0, n=n):
                    e = Synced(P, e, n < 300)
                    e.reciprocal(out=rs[:, :n], in_=rs[:, :n])
                    e.scalar_tensor_tensor(out=rs[:, :n], in0=oT[:, m, c0:c0 + n], scalar=g_hggn[:, hd:hd + 1], in1=rs[:, :n], op0=ALU.mult, op1=ALU.mult)
                    return e.tensor_tensor(out=mixT[:, 8 + hd, c0:c0 + n], in0=rs[:, :n], in1=gT[:, m, c0:c0 + n], op=ALU.mult)
                P.op("dve", fn_, reads=[rsr, r_oT, r_gT, r_pft], writes=[rsr, r_mix[8 + hd]])

    def s5_bu(cols0, n, hbv, r_HB):
        kper = (512 // n) // 2
        cnt = 0
        for k0 in range(0, KC, kper):
            for j4 in range(4):
                ps, pres = psA.get()

                def fb(e, ps=ps, k0=k0, j4=j4):
                    for kk in range(kper):
                        kc = k0 + kk
                        for ri in range(2):
                            pi = kk * 2 + ri
                            ins = e.matmul(ps[:, pi * n:(pi + 1) * n], lhsT=lB[32 * j4:32 * j4 + 32, kc, ri, :], rhs=hT[32 * j4:32 * j4 + 32, kc, cols0:cols0 + n],
                                           start=True, stop=True, tile_position=(32 * j4, 0))
                    return ins
                P.op("pe", fb, reads=[r_lB, r_hT], writes=[pres])
                js = k0 * 4 + j4
                eng = "act"
                cnt += 1

                def fe(e, ps=ps, js=js, eng=eng):
                    src = ps[:, :kper * 2 * n].rearrange("p (j r t) -> p j r t", r=2, t=n)
                    dst = hbv[:, js:js + 4 * (kper - 1) + 1:4, :, :]
                    if eng == "act":
                        return e.activation(out=dst, in_=src, func=AF.Copy)
                    return e.tensor_copy(out=dst, in_=src)
                P.op(eng, fe, reads=[pres], writes=[r_HB])

    def s5_y_mm(n, hbb, r_hbb):
        ps, pres = psB.get()

        def fy(e, ps=ps):
            for kc in range(KC):
                for j4 in range(4):
                    j = kc * 4 + j4
                    for ri in range(2):
                        ins = e.matmul(ps[32 * j4:32 * j4 + 32, kc * n:(kc + 1) * n], lhsT=lC[:, j, ri, :], rhs=hbb[:, j, ri, :],
                                       start=(ri == 0), stop=(ri == 1), tile_position=(0, 32 * j4))
            return ins
        P.op("pe", fy, reads=[r_lC] + list(r_hbb), writes=[pres])
        return ps, pres

    def s5_gelu(ps, pres, cols0, n, dst_cols):
        yv, yvr = tmpf.get()
        t2, t2r = tmpf.get()

        def fg(e, ps=ps, yv=yv, t2=t2):
            e = Synced(P, e, KC * n < 300)
            y3 = yv[:, :KC * n].rearrange("p (k t) -> p k t", t=n)
            e.tensor_tensor(out=y3, in0=hT[:, :, cols0:cols0 + n], in1=bc_last(dcolT, n), op=ALU.mult)
            e.tensor_tensor(out=yv[:, :KC * n], in0=yv[:, :KC * n], in1=ps[:, :KC * n], op=ALU.add)
            e.tensor_tensor(out=t2[:, :KC * n], in0=yv[:, :KC * n], in1=yv[:, :KC * n], op=ALU.mult)
            e.tensor_scalar(out=t2[:, :KC * n], in0=t2[:, :KC * n], scalar1=0.044715, scalar2=1.0, op0=ALU.mult, op1=ALU.add)
            return e.tensor_tensor(out=t2[:, :KC * n], in0=t2[:, :KC * n], in1=yv[:, :KC * n], op=ALU.mult)
        P.op("dve", fg, reads=[pres, r_hT, r_pft], writes=[yvr, t2r])
        P.op("act", lambda e, t2=t2: e.activation(out=t2[:, :KC * n], in_=t2[:, :KC * n], func=AF.Sigmoid, scale=1.5957691216057308), reads=[t2r], writes=[t2r])
        P.op("dve", lambda e, yv=yv, t2=t2: e.tensor_tensor(out=mixT[:, :, dst_cols:dst_cols + n], in0=yv[:, :KC * n].rearrange("p (k t) -> p k t", t=n),
                                                          in1=t2[:, :KC * n].rearrange("p (k t) -> p k t", t=n), op=ALU.mult), reads=[yvr, t2r], writes=r_mix)

    def s5_layer(it, last, pre=False):
        NSC = TT // TC
        if it == SS:
            stage = cv_s5stage
            H0, Hn = HBs[1], HBs[0]
            for ri, src in enumerate((st_s5r, st_s5i)):
                for half in range(2):
                    P.op("sp", lambda e, src=src, half=half: e.dma_start(out=stage[:NS, :], in_=src[:, half * 4096:(half + 1) * 4096]), writes=[r_stage], dma=True)
                    ps, pres = psB.get()

                    def ftr(e, ps=ps):
                        for jj in range(32):
                            ins = e.transpose(ps[:, jj * NS:(jj + 1) * NS], stage[:NS, jj * 128:(jj + 1) * 128], identf[:NS, :NS])
                        return ins
                    P.op("pe", ftr, reads=[r_stage, r_cst], writes=[pres])
                    P.op("dve", lambda e, ps=ps, ri=ri, half=half: e.tensor_copy(out=H0[:, half * 32:(half + 1) * 32, ri, :], in_=ps.rearrange("p (j b) -> p j b", b=NS)),
                         reads=[pres], writes=[r_HBs[1]])
            s5_bu(TT, NS, Hn, r_HBs[0])
            HS1 = HBb_raw.rearrange("p (j r t) -> p j r t", r=2, t=NS)

            def fs(e):
                e.tensor_tensor(out=HS1, in0=H0, in1=bc_last(A1, NS), op=ALU.mult)
                e.tensor_tensor(out=Hn, in0=Hn, in1=HS1, op=ALU.add)
                e.tensor_tensor(out=HS1[:, :, 0, :], in0=H0[:, :, 1, :], in1=bc_last(A2[:, :, 0], NS), op=ALU.mult)
                e.tensor_tensor(out=HS1[:, :, 1, :], in0=H0[:, :, 0, :], in1=bc_last(A2[:, :, 1], NS), op=ALU.mult)
                return e.tensor_tensor(out=Hn, in0=Hn, in1=HS1, op=ALU.add)
            P.op("dve", fs, reads=[r_HBs[0], r_HBs[1], r_A], writes=[r_HBs[0], r_HBb, r_HBbs[0], r_HBbs[1]])
            hbs = HBbs[0]
            P.op("act", lambda e: e.activation(out=hbs, in_=Hn, func=AF.Copy), reads=[r_HBs[0]], writes=[r_HBb, r_HBbs[0]])
            ps, pres = s5_y_mm(NS, hbs, [r_HBbs[0]])
            s5_gelu(ps, pres, TT, NS, TT)
            for ri, dst in enumerate((s5r_s, s5i_s)):
                for half in range(2):
                    for g in range(8):
                        ps, pres = psB.get()

                        def fto(e, ps=ps, ri=ri, half=half, g=g):
                            for q in range(4):
                                j = half * 32 + g * 4 + q
                                ins = e.transpose(ps[:NS, q * 128:(q + 1) * 128], Hn[:, j, ri, :], identf)
                            return ins
                        P.op("pe", fto, reads=[r_HBs[0], r_cst], writes=[pres])
                        P.op("act", lambda e, ps=ps, g=g: e.activation(out=stage[:NS, g * 512:(g + 1) * 512], in_=ps[:NS, :], func=AF.Copy), reads=[pres], writes=[r_stage])
                    P.op("sp", lambda e, dst=dst, half=half: e.dma_start(out=dst[:, half * 4096:(half + 1) * 4096], in_=stage[:NS, :]), reads=[r_stage], dma=True)
        if pre:
            stage = cv_s5stage
            PW1 = stage[:, 0:2048].rearrange("p (t j r) -> p t j r", j=64, r=2)
            PW2 = stage[:, 2048:4096].rearrange("p (t j r) -> p t j r", j=64, r=2)

            def fgen(e):
                e = Synced(P, e)
                cur = Pw[0]
                e.memset(cur[:, :, 0], 1.0)
                e.memset(cur[:, :, 1], 0.0)
                for k_ in range(TC + 1):
                    if k_ < TC:
                        t_ = TC - 1 - k_
                        e.tensor_copy(out=PW1[:, t_], in_=bc_last(cur[:, :, 0], 2))
                        e.tensor_copy(out=PW2[:, t_, :, 1], in_=cur[:, :, 1])
                        e.tensor_scalar(out=PW2[:, t_, :, 0], in0=cur[:, :, 1], scalar1=-1.0, scalar2=None, op0=ALU.mult)
                    else:
                        e.tensor_copy(out=A16a, in_=bc_last(cur[:, :, 0], 2))
                        e.tensor_copy(out=A16b[:, :, 1], in_=cur[:, :, 1])
                        ins = e.tensor_scalar(out=A16b[:, :, 0], in0=cur[:, :, 1], scalar1=-1.0, scalar2=None, op0=ALU.mult)
                        break
                    nxt = Pw[(k_ + 1) % 2]
                    e.tensor_tensor(out=st1, in0=cur, in1=A1, op=ALU.mult)
                    e.tensor_tensor(out=st2, in0=cur[:, :, ::-1], in1=A2, op=ALU.mult)
                    e.tensor_tensor(out=nxt, in0=st1, in1=st2, op=ALU.add)
                    cur = nxt
                return ins
            P.op("dve", fgen, reads=[r_A], writes=[r_stage, r_pw, r_st, r_A16])
            TMP = HBb_raw.rearrange("p (t j r) -> p t j r", j=64, r=2)
            s5_bu(0, TC, HBs[0], r_HBs[0])
            for sc in range(NSC):
                k = sc % 2
                if sc + 1 < NSC:
                    s5_bu((sc + 1) * TC, TC, HBs[1 - k], r_HBs[1 - k])

                def fpre(e, HR=HBraw[k]):
                    e.tensor_tensor(out=TMP, in0=HR[:, :, :, ::-1], in1=PW2, op=ALU.mult)
                    e.tensor_tensor(out=HR, in0=HR, in1=PW1, op=ALU.mult)
                    e.tensor_tensor(out=HR, in0=HR, in1=TMP, op=ALU.add)
                    e.tensor_reduce(out=cbuf, in_=HR.rearrange("p t j r -> p j r t"), axis=mybir.AxisListType.X, op=ALU.add)
                    es = Synced(P, e)
                    es.tensor_tensor(out=st1, in0=Hst, in1=A16a, op=ALU.mult)
                    es.tensor_tensor(out=st2, in0=Hst[:, :, ::-1], in1=A16b, op=ALU.mult)
                    es.tensor_tensor(out=st1, in0=st1, in1=st2, op=ALU.add)
                    return es.tensor_tensor(out=Hst, in0=st1, in1=cbuf, op=ALU.add)
                P.op("dve", fpre, reads=[r_HBs[k], r_stage, r_A16, r_Hst], writes=[r_HBs[k], r_HBb, r_HBbs[0], r_HBbs[1], r_st, r_pw, r_Hst])
            return
        s5_bu(0, TC, HBs[0], r_HBs[0])
        pend = None
        for sc in range(NSC):
            k = sc % 2
            HB, rHB, HBb, rHBb = HBs[k], r_HBs[k], HBbs[k], r_HBbs[k]
            if sc + 1 < NSC:
                s5_bu((sc + 1) * TC, TC, HBs[1 - k], r_HBs[1 - k])

            def fscan(e, HB=HB, HR=HBraw[k]):
                for t in range(TC):
                    prev = Hst if t == 0 else HR[:, t - 1]
                    prevs = Hst[:, :, ::-1] if t == 0 else HR[:, t - 1, :, ::-1]
                    cur = HR[:, t]
                    e.tensor_tensor(out=st1, in0=prev, in1=A1, op=ALU.mult)
                    e.tensor_tensor(out=st2, in0=prevs, in1=A2, op=ALU.mult)
                    i3 = e.tensor_tensor(out=cur, in0=cur, in1=st1, op=ALU.add)
                    P.selfsync(i3)
                    i4 = e.tensor_tensor(out=cur, in0=cur, in1=st2, op=ALU.add)
                    P.selfsync(i4)
                return e.tensor_copy(out=Hst, in_=HR[:, TC - 1])
            P.op("dve", fscan, reads=[rHB, r_A, r_Hst], writes=[rHB, r_st, r_Hst])
            if pre:
                continue
            P.op("act", lambda e, HB=HB, HBb=HBb: e.activation(out=HBb, in_=HB, func=AF.Copy), reads=[rHB], writes=[rHBb, r_HBb])
            ps, pres = s5_y_mm(TC, HBb, [rHBb])
            if pend is not None:
                s5_gelu(*pend)
            pend = (ps, pres, sc * TC, TC, sc * TC)
        if pend is not None:
            s5_gelu(*pend)
        if last:
            for ri, dst in enumerate((s5r_p, s5i_p)):
                ps, pres = psB.get()
                P.op("pe", lambda e, ps=ps, ri=ri: e.transpose(ps[:64, 0:128], Hst[:, :, ri], identf), reads=[r_Hst, r_cst], writes=[pres])
                ta, tar = tmpf.get()
                P.op("act", lambda e, ps=ps, ta=ta: e.activation(out=ta[:64, 0:128], in_=ps[:64, 0:128], func=AF.Copy), reads=[pres], writes=[tar])
                P.op("sp", lambda e, dst=dst, ta=ta: e.dma_start(out=dst, in_=ta[:64, 0:128]), reads=[tar], dma=True)

    cv_s5stage = cv.get([4096]); r_stage = tok("stage")

    for it in range(NTILES):
        grps = groups(it)
        ntot = grps[-1][0] + grps[-1][1]
        t0 = it * TT
        last = (it == NTILES - 1)
        pre = it < NPRE
        tout = (it - NPRE) * TT
        if it == 0 or pre or it == NPRE:
            fence()
        for tb in range(NB):
            xa, xr = xin.get()
            P.op("sp", lambda e, xa=xa, tb=tb, t0=t0: e.dma_start(out=xa, in_=xp[t0 + tb * 128:t0 + (tb + 1) * 128, :]), writes=[xr], dma=True)
            for g4 in range(4):
                ps, pres = psB.get()

                def f(e, ps=ps, xa=xa, g4=g4):
                    for q in range(4):
                        kc = g4 * 4 + q
                        ins = e.transpose(ps[:, q * 128:(q + 1) * 128], xa[:, kc * 128:(kc + 1) * 128], identf)
                    return ins
                P.op("pe", f, reads=[xr, r_cst], writes=[pres])
                eng = "dve" if g4 % 2 == 0 else "act"

                def cp(e, ps=ps, g4=g4, tb=tb, eng=eng):
                    src = ps.rearrange("p (q c) -> p q c", c=128)
                    dst = xT[:, g4 * 4:g4 * 4 + 4, tb * 128:(tb + 1) * 128]
                    if eng == "dve":
                        return e.tensor_copy(out=dst, in_=src)
                    return e.activation(out=dst, in_=src, func=AF.Copy)
                P.op(eng, cp, reads=[pres], writes=r_xT[g4 * 4:g4 * 4 + 4])
        if it == SS:
            xa, xr = xin.get()
            P.op("sp", lambda e, xa=xa: e.dma_start(out=xa[:NS, :], in_=xs), writes=[xr], dma=True)
            ps, pres = psB.get()

            def f(e, ps=ps, xa=xa):
                for kc in range(KC):
                    ins = e.transpose(ps[:, kc * NS:(kc + 1) * NS], xa[:NS, kc * 128:(kc + 1) * 128], identf[:NS, :NS])
                return ins
            P.op("pe", f, reads=[xr, r_cst], writes=[pres])
            P.op("dve", lambda e, ps=ps: e.tensor_copy(out=xT[:, :, TT:TT + NS], in_=ps[:, :KC * NS].rearrange("p (k c) -> p k c", c=NS)),
                 reads=[pres], writes=r_xT)
        if dbg == "x":
            dump_dbg()
            break
        fence()
        if "nomix0" not in (dbg or ""):
            rmsnorm(g_attn, hT, [r_hT], grps)
            P.op("sp", lambda e, t0=t0: e.dma_start(out=rott[:, :, 0:TT], in_=rot[:, :, t0:t0 + TT].rearrange("a p n -> p a n")), writes=[r_rot], dma=True)
            if it == SS:
                P.op("sp", lambda e: e.dma_start(out=rott[:, :, TT:NT], in_=rot[:, :, T:T + NS].rearrange("a p n -> p a n")), writes=[r_rot], dma=True)
            for h in range(1 if (dbg and "small" in dbg) else RET_H):
                ret_head(h, it, grps, last)
            for hp in range(1 if (dbg and "small" in dbg) else 4):
                hg_pair(hp, it, grps, last)
            for blk in range(8):
                w, wr = load_w(w_out[blk], 4096)
                for m in range(2):
                    oc = blk * 2 + m
                    for (c0, n) in grps:
                        ps, pres = dense_ps(w, wr, KC, 256, m, mixT, r_mix, c0, n, psA)
                        add_resid(ps, pres, oc, c0, n)
        if dbg and dbg.startswith("mix0"):
            dump_dbg()
            break
        fence()
        ffn(0, grps)
        if dbg and dbg.startswith("ffn0"):
            dump_dbg()
            break
        fence()
        rmsnorm(g_ssm, hT, [r_hT], grps)
        s5_layer(it, last, pre)
        if pre:
            continue
        for blk in range(8):
            wa, war = load_w(w_ga[blk], 4096)
            wb, wbr = load_w(w_gb[blk], 4096)
            for m in range(2):
                oc = blk * 2 + m
                for (c0, n) in grps:
                    pa, par = dense_ps(wa, war, KC, 256, m, mixT, r_mix, c0, n, psA)
                    pb_, pbr = dense_ps(wb, wbr, KC, 256, m, mixT, r_mix, c0, n, psA)
                    ta, tar = tmpf.get()
                    P.op("act", lambda e, ta=ta, pb_=pb_, n=n: e.activation(out=ta[:, :n], in_=pb_[:, :n], func=AF.Sigmoid), reads=[pbr], writes=[tar])

                    def fglu(e, ta=ta, pa=pa, oc=oc, c0=c0, n=n):
                        e = Synced(P, e, n < 300)
                        e.tensor_tensor(out=ta[:, :n], in0=ta[:, :n], in1=pa[:, :n], op=ALU.mult)
                        return e.tensor_tensor(out=xT[:, oc, c0:c0 + n], in0=xT[:, oc, c0:c0 + n], in1=ta[:, :n], op=ALU.add)
                    P.op("dve", fglu, reads=[tar, par, r_xT[oc]], writes=[tar, r_xT[oc]])
        if dbg == "mix1":
            dump_dbg()
            break
        fence()
        ffn(1, grps)
        rmsnorm(g_fin, xT, r_xT, grps)
        if dbg == "final":
            dump_dbg()
            break
        fence()
        for tb in range(NB):
            xa, xr = xin.get()
            for g4 in range(4):
                ps, pres = psB.get()

                def f(e, ps=ps, g4=g4, tb=tb):
                    for q in range(4):
                        kc = g4 * 4 + q
                        ins = e.transpose(ps[:, q * 128:(q + 1) * 128], xT[:, kc, tb * 128:(tb + 1) * 128], identf)
                    return ins
                P.op("pe", f, reads=r_xT[g4 * 4:g4 * 4 + 4] + [r_cst], writes=[pres])
                eng = "dve" if g4 % 2 == 0 else "act"

                def cp(e, ps=ps, g4=g4, xa=xa, eng=eng):
                    if eng == "dve":
                        return e.tensor_copy(out=xa[:, g4 * 512:(g4 + 1) * 512], in_=ps)
                    return e.activation(out=xa[:, g4 * 512:(g4 + 1) * 512], in_=ps, func=AF.Copy)
                P.op(eng, cp, reads=[pres], writes=[xr])
            P.op("sp", lambda e, xa=xa, tb=tb, tout=tout: e.dma_start(out=yp[tout + tb * 128:tout + (tb + 1) * 128, :], in_=xa), reads=[xr], dma=True)
        if it == SS:
            xa, xr = xin.get()
            for g4 in range(4):
                ps, pres = psB.get()

                def f(e, ps=ps, g4=g4):
                    for q in range(4):
                        kc = g4 * 4 + q
                        ins = e.transpose(ps[:NS, q * 128:(q + 1) * 128], xT[:, kc, TT:NT], identf)
                    return ins
                P.op("pe", f, reads=r_xT[g4 * 4:g4 * 4 + 4] + [r_cst], writes=[pres])
                P.op("dve", lambda e, ps=ps, g4=g4, xa=xa: e.tensor_copy(out=xa[:NS, g4 * 512:(g4 + 1) * 512], in_=ps[:NS, :]), reads=[pres], writes=[xr])
            P.op("sp", lambda e, xa=xa: e.dma_start(out=ys, in_=xa[:NS, :]), reads=[xr], dma=True)

    P.emit(nc, stack)
    stack.close()
    return nc


def _prep_shared(inp, T, TT):
    consts_np, _ = _consts(TT)
    f = lambda a: np.asarray(a, np.float32)
    pf = np.concatenate([
        _fm(f(inp["attn_norm_g"])[0]), _fm(f(inp["ffn_norm_g"])[0]), _fm(f(inp["ssm_norm_g"])[0]),
        _fm(f(inp["ffn_norm_g"])[1]), _fm(f(inp["final_norm_g"])),
        _fm(f(inp["ret_gn_g"])[0]), _fm(f(inp["hg_gn_g"])[0]),
        np.concatenate([_fm(f(inp["hg_lb"])[l]) for l in range(3)], 1),
        _fm(f(inp["s5_d"])[0]),
    ], 1)

    def qj(a):
        a = f(a)
        rest = a.shape[2:]
        return np.ascontiguousarray(a.reshape((64, 2, 64) + rest).transpose((1, 2, 0) + tuple(range(3, 3 + len(rest)))).reshape((128, 64) + rest))
    lamr = qj(inp["s5_lam_re"][0]); lami = qj(inp["s5_lam_im"][0])
    logdt = qj(np.repeat(f(inp["s5_log_dt"])[0][:, None], 64, 1))
    bre = qj(inp["s5_b_re"][0]); bim = qj(inp["s5_b_im"][0])
    cre = qj(f(inp["s5_c_re"])[0].transpose(0, 2, 1)); cim = qj(f(inp["s5_c_im"])[0].transpose(0, 2, 1))
    s5p = np.concatenate([lamr, lami, logdt, bre.reshape(128, -1), bim.reshape(128, -1), cre.reshape(128, -1), cim.reshape(128, -1)], 1)
    wfd = []
    for l in range(2):
        w = f(inp["w_ffn_down"])[l]
        halves = []
        for hf_ in range(2):
            wh = w[hf_ * 2816:(hf_ + 1) * 2816]
            halves.append(_wblocks(wh, 128))
        wfd.append(np.stack(halves, 0))
    shared = {
        "consts": consts_np, "pf": np.ascontiguousarray(pf.astype(np.float32)), "s5p": np.ascontiguousarray(s5p.astype(np.float32)),
        "w_in": _wblocks(f(inp["w_in"])[0], 256), "w_out": _wblocks(f(inp["w_out"])[0], 256),
        "w_ga": _wblocks(f(inp["w_glu_a"])[0], 256), "w_gb": _wblocks(f(inp["w_glu_b"])[0], 256),
        "w_fg": np.stack([_wblocks(f(inp["w_ffn_gate"])[l], 256) for l in range(2)], 0),
        "w_fu": np.stack([_wblocks(f(inp["w_ffn_up"])[l], 256) for l in range(2)], 0),
        "w_fd": np.stack(wfd, 0),
    }
    return shared


_T, _TT, _NPRE = 2048, 512, 2


def kernel(**inp):
    T, TT, NPRE = _T, _TT, _NPRE
    TH = T // 2
    shared = _prep_shared(inp, T, TT)
    nc = build(T, TT, npre=NPRE)
    xpr = np.asarray(inp["x_prompt"], np.float32)
    xsm = np.asarray(inp["x_sample"], np.float32)[:, 0, :]
    samp_pos = np.full(NS, 16384.0)
    rot_a = np.ascontiguousarray(_rot_tables(np.concatenate([np.arange(TH, dtype=np.float64), np.arange(TH, dtype=np.float64), samp_pos])))
    rot_b = np.ascontiguousarray(_rot_tables(np.concatenate([np.arange(T, dtype=np.float64), samp_pos])))
    zeros_h = np.zeros((TH, D), np.float32)
    in_maps = []
    for c in range(NCORES):
        seq, half = c // 2, c % 2
        m = dict(shared)
        if half == 0:
            m["xp"] = np.ascontiguousarray(np.concatenate([zeros_h, xpr[seq, :TH]], 0))
            m["rot"] = rot_a
        else:
            m["xp"] = np.ascontiguousarray(xpr[seq])
            m["rot"] = rot_b
        m["xs"] = np.ascontiguousarray(xsm[c * NS:(c + 1) * NS])
        m["st_ret"] = np.ascontiguousarray(np.asarray(inp["state_ret"], np.float32)[0, c * NS:(c + 1) * NS])
        m["st_hg"] = np.ascontiguousarray(np.asarray(inp["state_hgrn"], np.float32)[0, c * NS:(c + 1) * NS])
        m["st_s5r"] = np.ascontiguousarray(np.asarray(inp["state_s5_re"], np.float32)[0, c * NS:(c + 1) * NS].reshape(NS, 8192))
        m["st_s5i"] = np.ascontiguousarray(np.asarray(inp["state_s5_im"], np.float32)[0, c * NS:(c + 1) * NS].reshape(NS, 8192))
        in_maps.append(m)
    res = run_bass_kernel_spmd(nc, in_maps, core_ids=list(range(NCORES))).results
    y_prompt = np.stack([np.concatenate([res[2 * q]["yp"], res[2 * q + 1]["yp"]], 0) for q in range(4)], 0)
    y_sample = np.concatenate([res[c]["ys"] for c in range(NCORES)], 0)[:, None, :]
    fin = [2 * q + 1 for q in range(4)]
    ret_p = np.stack([res[c]["ret_p"] for c in fin], 0)[None]
    ret_s = np.concatenate([res[c]["ret_s"] for c in range(NCORES)], 0)[None]
    hg_p = np.stack([res[c]["hg_p"] for c in fin], 0)[None]
    hg_s = np.concatenate([res[c]["hg_s"] for c in range(NCORES)], 0)[None]
    s5r_p = np.stack([res[c]["s5r_p"].reshape(128, 64) for c in fin], 0)[None]
    s5i_p = np.stack([res[c]["s5i_p"].reshape(128, 64) for c in fin], 0)[None]
    s5r_s = np.concatenate([res[c]["s5r_s"] for c in range(NCORES)], 0).reshape(1, 128, 128, 64)
    s5i_s = np.concatenate([res[c]["s5i_s"] for c in range(NCORES)], 0).reshape(1, 128, 128, 64)
    return (y_prompt, y_sample, ret_p, ret_s, hg_p, hg_s, s5r_p, s5i_p, s5r_s, s5i_s)
```

```python
import math
from contextlib import ExitStack
import numpy as np
import concourse.bass as bass
import concourse.mybir as mybir
from concourse.bass_utils import run_bass_kernel_spmd

F32, BF16, I32 = mybir.dt.float32, mybir.dt.bfloat16, mybir.dt.int32
AF = mybir.ActivationFunctionType
ALU = mybir.AluOpType

D = 2048
KC = 16
DFF = 5632
FC = 44
NCORES = 8
NS = 16
EPS = 1e-6
TWO_PI = 2.0 * math.pi


class Res:
    __slots__ = ("name", "w", "r")

    def __init__(self, name=""):
        self.name = name
        self.w = None
        self.r = {}


class Prog:
    def __init__(self):
        self.ops = []

    def op(self, eng, fn, reads=(), writes=(), dma=False):
        i = len(self.ops)
        deps = set()
        raw = set()
        for res in reads:
            if res.w is not None:
                deps.add(res.w)
                raw.add(res.w)
        for res in writes:
            if res.w is not None:
                deps.add(res.w)
            deps.update(res.r.values())
        for res in reads:
            res.r[("dma", i) if dma else eng] = i
        for res in writes:
            res.w = i
            res.r = {}
        deps.discard(i)
        self.ops.append([eng, fn, deps, dma, False, None, 0, raw, 0])
        return i

    def emit(self, nc, stack, ndma_sp=10, ndma_pool=8):
        ops = self.ops
        engs = ("pe", "act", "dve", "pool", "sp")
        cnt_ = {e: 0 for e in engs}
        for o in ops:
            o[8] = cnt_[o[0]]
            cnt_[o[0]] += 1
        WIN = 3

        def need(o, j):
            p = ops[j]
            if p[3] or p[0] != o[0]:
                return True
            return (j in o[7]) and (o[8] - p[8] <= WIN) and not o[3]
        for o in ops:
            for j in o[2]:
                if need(o, j):
                    ops[j][4] = True
        csem = {e: stack.enter_context(nc.semaphore("c_" + e)) for e in engs}
        dsem = {"sp": [stack.enter_context(nc.semaphore("d_sp%d" % i)) for i in range(ndma_sp)],
                "pool": [stack.enter_context(nc.semaphore("d_pl%d" % i)) for i in range(ndma_pool)],
                "act": [stack.enter_context(nc.semaphore("d_ac%d" % i)) for i in range(2)]}
        ccount = {e: 0 for e in engs}
        dcount = {e: 0 for e in dsem}
        dval = {e: [0] * len(dsem[e]) for e in dsem}
        prev_on_slot = {}
        streams = {e: [] for e in engs}
        for i, o in enumerate(ops):
            e = o[0]
            streams[e].append(i)
            if o[3]:
                k = dcount[e] % len(dsem[e])
                dcount[e] += 1
                prev_on_slot[i] = (dsem[e][k], dval[e][k])
                dval[e][k] += 16
                o[5] = dsem[e][k]
                o[6] = dval[e][k]
            elif o[4]:
                ccount[e] += 1
                o[5] = csem[e]
                o[6] = ccount[e]
        block = stack.enter_context(nc.Block())
        ssem = {e: stack.enter_context(nc.semaphore("s_" + e)) for e in ("act", "dve", "pool")}
        scnt = {e: 0 for e in ssem}
        prog = self

        def run(engname, eng):
            seen = {}
            prog.cur = engname

            def selfsync(ins):
                scnt[engname] += 1
                ins.then_inc(ssem[engname], 1)
                eng.wait_ge(ssem[engname], scnt[engname])
            prog.selfsync = selfsync

            def wait(sem, val):
                if val <= 0:
                    return
                key = id(sem)
                if seen.get(key, 0) < val:
                    eng.wait_ge(sem, val)
                    seen[key] = val

            for i in streams[engname]:
                o = ops[i]
                for j in sorted(o[2]):
                    p = ops[j]
                    if need(o, j):
                        wait(p[5], p[6])
                if o[3]:
                    s, v = prev_on_slot[i]
                    wait(s, v)
                ins = o[1](eng)
                if o[3]:
                    ins.then_inc(o[5], 16)
                elif o[4]:
                    ins.then_inc(o[5], 1)
            if engname in dsem:
                for k, s in enumerate(dsem[engname]):
                    wait(s, dval[engname][k])

        @block.tensor
        def _(t):
            run("pe", t)

        @block.scalar
        def _(t):
            run("act", t)

        @block.vector
        def _(t):
            run("dve", t)

        @block.gpsimd
        def _(t):
            run("pool", t)

        @block.sync
        def _(t):
            run("sp", t)


class Synced:
    def __init__(self, prog, eng, on=True):
        self.p, self.e, self.pend, self.on = prog, eng, None, on

    def __getattr__(self, name):
        f = getattr(self.e, name)

        def g(*a, **k):
            if self.pend is not None and self.on:
                self.p.selfsync(self.pend)
            self.pend = f(*a, **k)
            return self.pend
        return g


class Rot:
    def __init__(self, items):
        self.items = items
        self.i = 0

    def get(self):
        it = self.items[self.i % len(self.items)]
        self.i += 1
        return it


RET_H, RET_DK = 4, 256
HG_H = 8


def _fm(vec):
    v = np.asarray(vec, np.float32)
    return np.ascontiguousarray(v.reshape(-1, 128).T)


def _wblocks(w, bc):
    K, N = w.shape
    kc = K // 128
    return np.ascontiguousarray(w.reshape(kc, 128, N // bc, bc).transpose(2, 1, 0, 3)).reshape(N // bc, 128, kc * bc)


def _consts(TT):
    cols = []
    ident = np.eye(128, dtype=np.float64)
    cols.append(ident)
    cols.append(np.ones((128, 128)))
    idx = np.arange(128, dtype=np.float64)
    lg = np.log(1.0 - 2.0 ** (-5.0 - np.arange(RET_H, dtype=np.float64)))
    maskT = np.zeros((128, RET_H, 128))
    qdec = np.zeros((128, RET_H, 128))
    kdec = np.zeros((128, RET_H))
    for h in range(RET_H):
        dji = idx[None, :] - idx[:, None]
        maskT[:, h, :] = np.where(dji >= 0, np.exp(dji * lg[h]), 0.0) / 16.0
        qdec[:, h, :] = np.exp((idx + 1.0) * lg[h])[None, :]
        kdec[:, h] = np.exp((127.0 - idx) * lg[h]) / 16.0
    cols.append(maskT.reshape(128, -1))
    cols.append(qdec.reshape(128, -1))
    cols.append(kdec)
    jj = idx[:, None]
    ii = idx[None, :]
    maskH = ((jj // 64 == ii // 64) & (jj <= ii)).astype(np.float64)
    cols.append(maskH)
    half = np.stack([(idx // 64 == 0), (idx // 64 == 1)], 1).astype(np.float64)
    cols.append(half)
    reset = np.ones((128, TT))
    reset[:, ::64] = 0.0
    cols.append(reset)
    cols.append(np.full((128, 1), EPS))
    c = np.concatenate(cols, 1).astype(np.float32)
    return np.ascontiguousarray(c), lg


C_ID, C_ONES, C_MASKT, C_QDEC, C_KDEC, C_MASKH, C_HALF, C_RESET = 0, 128, 256, 768, 1280, 1284, 1412, 1414


def _rot_tables(pos):
    half = 128
    inv = (10000.0 ** (-np.arange(half, dtype=np.float32) / np.float32(half))).astype(np.float32)
    ang = (pos.astype(np.float32)[None, :] * inv[:, None]).astype(np.float32).astype(np.float64)
    c, s = np.cos(ang), np.sin(ang)
    return np.stack([c, s], 0).astype(np.float32)


def build(T, TT, dbg=None, npre=0):
    NT = TT + NS
    NTILES = T // TT
    NPRE = npre
    SS = NPRE
    TOUT = (NTILES - NPRE) * TT
    NB = TT // 128
    consts_np, lg = _consts(TT)
    NCONST = consts_np.shape[1]
    C_EPS = C_RESET + TT
    gam = [float(np.exp(lg[h])) for h in range(RET_H)]
    cdec = [float(np.exp(128.0 * lg[h])) for h in range(RET_H)]

    nc = bass.Bass("TRN2", target_bir_lowering=False)
    stack = ExitStack()
    P = Prog()

    def din(name, shape):
        return nc.dram_tensor(name, list(shape), F32, kind="ExternalInput").ap()

    def dout(name, shape):
        return nc.dram_tensor(name, list(shape), F32, kind="ExternalOutput").ap()

    xp = din("xp", [T, D])
    xs = din("xs", [NS, D])
    st_ret = din("st_ret", [NS, RET_H, 256, 256])
    st_hg = din("st_hg", [NS, HG_H, 128, 128])
    st_s5r = din("st_s5r", [NS, 8192])
    st_s5i = din("st_s5i", [NS, 8192])
    consts = din("consts", [128, NCONST])
    rot = din("rot", [2, 128, T + NS])
    NPF = 16 * 6 + 8 + 8 + 24
    pf = din("pf", [128, NPF])
    s5p = din("s5p", [128, 192 + 4096])
    w_in = din("w_in", [32, 128, 16 * 256])
    w_out = din("w_out", [8, 128, 16 * 256])
    w_ga = din("w_ga", [8, 128, 16 * 256])
    w_gb = din("w_gb", [8, 128, 16 * 256])
    w_fg = din("w_fg", [2, 22, 128, 16 * 256])
    w_fu = din("w_fu", [2, 22, 128, 16 * 256])
    w_fd = din("w_fd", [2, 2, 16, 128, 22 * 128])

    yp = dout("yp", [TOUT, D])
    ys = dout("ys", [NS, D])
    ret_p = dout("ret_p", [RET_H, 256, 256])
    ret_s = dout("ret_s", [NS, RET_H, 256, 256])
    hg_p = dout("hg_p", [HG_H, 128, 128])
    hg_s = dout("hg_s", [NS, HG_H, 128, 128])
    s5r_p = dout("s5r_p", [64, 128])
    s5i_p = dout("s5i_p", [64, 128])
    s5r_s = dout("s5r_s", [NS, 8192])
    s5i_s = dout("s5i_s", [NS, 8192])
    dbg_out = dout("dbg", [128, KC * NT]) if dbg else None

    def sb(name, shape, dt=F32):
        return stack.enter_context(nc.sbuf_tensor(name, list(shape), dt))[:]

    cst = sb("cst", [128, NCONST]); r_cst = Res("cst")
    pft = sb("pft", [128, NPF]); r_pft = Res("pft")
    identf = cst[:, C_ID:C_ID + 128]
    onesf = cst[:, C_ONES:C_ONES + 128]
    epscol = cst[:, C_EPS:C_EPS + 1]
    identb = sb("identb", [128, 128], BF16); r_identb = Res()
    xT = sb("xT", [128, KC, NT]); r_xT = [Res("xT%d" % k) for k in range(KC)]
    hT = sb("hT", [128, KC, NT], BF16); r_hT = Res("hT")
    mixT = sb("mixT", [128, KC, NT], BF16); r_mix = [Res("mix%d" % k) for k in range(KC)]
    NW = 5
    wsl = Rot([(sb("w%d" % i, [128, 4096], BF16), Res("w%d" % i)) for i in range(NW)])
    sq = Rot([(sb("sq%d" % i, [128, NT], BF16), Res()) for i in range(2)])
    onesb = sb("onesb", [128, 128], BF16)
    rstd = sb("rstd", [128, NT]); r_rstd = Res("rstd")
    tmpf = Rot([(sb("tmpf%d" % i, [128, 512]), Res()) for i in range(3)])
    Sret = sb("Sret", [128, RET_H, 2, 256]); r_Sret = [Res() for _ in range(RET_H)]
    Shg = sb("Shg", [128, HG_H, 128]); r_Shg = [Res() for _ in range(HG_H)]
    lB = sb("lB", [128, KC, 2, 128], BF16); r_lB = Res()
    lC = sb("lC", [128, 64, 2, 32], BF16); r_lC = Res()
    A1 = sb("A1", [128, 64, 2]); A2 = sb("A2", [128, 64, 2]); r_A = Res()
    Hst = sb("Hst", [128, 64, 2]); r_Hst = Res()
    A16a = sb("A16a", [128, 64, 2]); A16b = sb("A16b", [128, 64, 2]); r_A16 = Res()
    fsc = sb("fsc", [128, 2]); r_fsc = Res()

    ARENA = 47872
    arena = sb("arena", [128, ARENA // 4])
    arena_tokens = []

    class Carver:
        def __init__(self):
            self.off = 0

        def get(self, shape, dt=F32):
            esz = 4 if dt == F32 or dt == I32 else 2
            n = 1
            for d_ in shape:
                n *= d_
            nb = (n * esz + 3) // 4 * 4
            assert self.off + nb <= ARENA, (self.off, nb)
            v = arena[:, self.off // 4:(self.off + nb) // 4]
            if dt != F32:
                v = v.bitcast(dt)
            v = v[:, :n]
            if len(shape) == 2:
                v = v.rearrange("p (a b) -> p a b", b=shape[1])
            elif len(shape) == 3:
                v = v.rearrange("p (a b c) -> p a b c", b=shape[1], c=shape[2])
            self.off += nb
            return v

    def tok(name=""):
        r = Res(name)
        arena_tokens.append(r)
        return r

    def fence():
        P.op("dve", lambda e: e.memset(fsc[:, 0:1], 0.0), writes=[r_fsc] + arena_tokens)

    cv = Carver()
    xin = Rot([(cv.get([D]), tok("xin%d" % i)) for i in range(4)])
    cv = Carver()
    act = cv.get([22, NT], BF16); r_act = [tok("act%d" % k) for k in range(22)]
    cv = Carver()
    rott = cv.get([2, NT]); r_rot = tok("rot")
    bfA = cv.get([2, NT], BF16); r_bfA = tok()
    bfB = cv.get([2, NT], BF16); r_bfB = tok()
    bfC = cv.get([2, TT], BF16); r_bfC = tok()
    vtm = cv.get([NB, 256], BF16); r_vtm = tok()
    kdtm = cv.get([NB, 256], BF16); r_kdtm = tok()
    oT = cv.get([2, NT]); r_oT = tok()
    gT = cv.get([2, NT], BF16); r_gT = tok()
    attm = Rot([(cv.get([128], BF16), tok()) for i in range(2)])
    Sbf = cv.get([2, 256], BF16); r_Sbf = tok()
    vs = cv.get([256], BF16); r_vs = tok()
    ktms = cv.get([256], BF16); r_ktms = tok()
    vmask = Rot([(cv.get([256], BF16), tok()) for i in range(4)])
    s0 = Rot([(cv.get([2, 256]), tok()) for i in range(4)])
    snb = Rot([(cv.get([2, 256], BF16), tok()) for i in range(3)])
    hf = cv.get([2, NT]); r_hf = tok()
    hb = cv.get([2, NT]); r_hb = tok()
    fsm = cv.get([2, NS]); r_fsm = tok()
    hebl = cv.get([2, TT // 64]); r_hebl = tok()
    hblast = cv.get([2, TT // 64]); r_hblast = tok()
    SbAll = cv.get([TT // 64 + 1, 128], BF16); r_SbAll = [tok() for _ in range(TT // 64 + 1)]
    cv = Carver()
    TC = 16
    HBraw = [cv.get([TC, 64, 2]), cv.get([TC, 64, 2])]
    HBs = [x_.rearrange("p t j r -> p j r t") for x_ in HBraw]; r_HBs = [tok("HBa"), tok("HBb")]
    HBb_raw = cv.get([64 * 2 * TC]); r_HBb = tok()
    HBbs = [HBb_raw[:, 0:64 * TC].bitcast(BF16).rearrange("p (j r t) -> p j r t", r=2, t=TC),
            HBb_raw[:, 64 * TC:128 * TC].bitcast(BF16).rearrange("p (j r t) -> p j r t", r=2, t=TC)]
    r_HBbs = [tok("HBba"), tok("HBbb")]
    st1 = cv.get([64, 2]); st2 = cv.get([64, 2]); r_st = tok()
    Pw = [cv.get([64, 2]), cv.get([64, 2])]; cbuf = cv.get([64, 2]); r_pw = tok()
    dcolT = pft[:, 120:136]

    banks = [(stack.enter_context(nc.psum_tensor("ps%d" % i, [128, 512], F32))[:], Res("ps%d" % i)) for i in range(8)]
    psA = Rot(banks[0:4])
    psB = Rot(banks[4:6])
    psC = Rot(banks[6:8])

    g_attn = pft[:, 0:16]; g_ffn0 = pft[:, 16:32]; g_ssm = pft[:, 32:48]; g_ffn1 = pft[:, 48:64]; g_fin = pft[:, 64:80]
    g_retgn = pft[:, 80:88]; g_hggn = pft[:, 88:96]; lbraw = pft[:, 96:120]

    groups0 = [(0, TT), (TT, NS)]

    P.op("sp", lambda e: e.dma_start(out=cst, in_=consts), writes=[r_cst], dma=True)
    P.op("sp", lambda e: e.dma_start(out=pft, in_=pf), writes=[r_pft], dma=True)
    P.op("dve", lambda e: e.tensor_copy(out=identb, in_=identf), reads=[r_cst], writes=[r_identb])
    P.op("dve", lambda e: e.tensor_copy(out=onesb, in_=onesf), reads=[r_cst], writes=[r_identb])
    for h in range(RET_H):
        P.op("dve", lambda e, h=h: e.memset(Sret[:, h], 0.0), writes=[r_Sret[h]])
    for h in range(HG_H):
        P.op("dve", lambda e, h=h: e.memset(Shg[:, h], 0.0), writes=[r_Shg[h]])
    P.op("dve", lambda e: e.memset(Hst, 0.0), writes=[r_Hst])
    lbe = sb("lbe", [128, 24]); lbc = sb("lbc", [128, 8]); oml = sb("oml", [128, 8]); r_lb = Res()
    P.op("act", lambda e: e.activation(out=lbe, in_=lbraw, func=AF.Exp), reads=[r_pft], writes=[r_lb])

    def _lb(e):
        e = Synced(P, e)
        e.tensor_tensor(out=lbc, in0=lbe[:, 0:8], in1=lbe[:, 8:16], op=ALU.add)
        e.tensor_tensor(out=lbc, in0=lbc, in1=lbe[:, 16:24], op=ALU.add)
        e.reciprocal(out=lbc, in_=lbc)
        e.tensor_tensor(out=lbc, in0=lbc, in1=lbe[:, 0:8], op=ALU.mult)
        return e.tensor_scalar(out=oml, in0=lbc, scalar1=-1.0, scalar2=1.0, op0=ALU.mult, op1=ALU.add)
    P.op("dve", _lb, reads=[r_lb], writes=[r_lb])

    def load_w(dram_blk, nelem):
        ap, res = wsl.get()
        ee = 2048 if nelem % 2048 == 0 else nelem // 2
        src = dram_blk.rearrange("p (s e) -> p s e", e=ee)
        dst = ap[:, :nelem].rearrange("p (s e) -> p s e", e=ee)
        P.op("pool", lambda e: e.dma_start(out=dst, in_=src), writes=[res], dma=True)
        return ap, res

    def groups(it):
        return groups0 if it == SS else [(0, TT)]

    def dense_ps(wap, wres, kcn, bc, m, rhsT, rhs_res, c0, n, pool, msz=128):
        ps, pres = pool.get()
        wv = wap[:, :kcn * bc].rearrange("p (k c) -> p k c", c=bc)

        def f(e):
            for kc in range(kcn):
                ins = e.matmul(ps[:msz, :n], lhsT=wv[:, kc, m * 128:m * 128 + msz], rhs=rhsT[:, kc, c0:c0 + n],
                               start=(kc == 0), stop=(kc == kcn - 1))
            return ins
        P.op("pe", f, reads=[wres] + list(rhs_res), writes=[pres])
        return ps, pres

    def rmsnorm(gcols, out_ap, out_res_list, grps, in_place=False):
        stats = []
        for (c0, n) in grps:
            ps, pres = psC.get()
            for kc in range(KC):
                s_ap, s_res = sq.get()
                P.op("act", lambda e, s_ap=s_ap, kc=kc, c0=c0, n=n: e.activation(out=s_ap[:, :n], in_=xT[:, kc, c0:c0 + n], func=AF.Square),
                     reads=[r_xT[kc]], writes=[s_res])
                P.op("pe", lambda e, ps=ps, s_ap=s_ap, kc=kc, n=n: e.matmul(ps[:, :n], lhsT=onesb, rhs=s_ap[:, :n], start=(kc == 0), stop=(kc == KC - 1)),
                     reads=[s_res, r_cst, r_identb], writes=[pres])
            stats.append((ps, pres, c0, n))
        for (ps, pres, c0, n) in stats:
            P.op("act", lambda e, ps=ps, c0=c0, n=n: e.activation(out=rstd[:, c0:c0 + n], in_=ps[:, :n], func=AF.Sqrt, scale=1.0 / D, bias=epscol),
                 reads=[pres, r_cst], writes=[r_rstd])
            P.op("dve", lambda e, c0=c0, n=n: e.reciprocal(out=rstd[:, c0:c0 + n], in_=rstd[:, c0:c0 + n]), reads=[r_rstd], writes=[r_rstd])
        ntot = grps[-1][0] + grps[-1][1]
        for kc in range(KC):
            wr = [out_res_list[kc]] if len(out_res_list) > 1 else list(out_res_list)
            P.op("dve", lambda e, kc=kc: e.scalar_tensor_tensor(out=out_ap[:, kc, :ntot], in0=xT[:, kc, :ntot], scalar=gcols[:, kc:kc + 1],
                                                                in1=rstd[:, :ntot], op0=ALU.mult, op1=ALU.mult),
                 reads=[r_xT[kc], r_rstd, r_pft], writes=wr)

    def add_resid(ps, pres, oc, c0, n):
        P.op("dve", lambda e: e.tensor_tensor(out=xT[:, oc, c0:c0 + n], in0=xT[:, oc, c0:c0 + n], in1=ps[:, :n], op=ALU.add),
             reads=[pres, r_xT[oc]], writes=[r_xT[oc]])

    def ffn(l, grps):
        rmsnorm(g_ffn0 if l == 0 else g_ffn1, hT, [r_hT], grps)
        for half in range(2):
            for blk in range(11):
                wg, wgr = load_w(w_fg[l, half * 11 + blk], 4096)
                wu, wur = load_w(w_fu[l, half * 11 + blk], 4096)
                for m in range(2):
                    c = blk * 2 + m
                    for (c0, n) in grps:
                        pg, pgr = dense_ps(wg, wgr, KC, 256, m, hT, [r_hT], c0, n, psA)
                        pu, pur = dense_ps(wu, wur, KC, 256, m, hT, [r_hT], c0, n, psA)
                        t_ap, t_res = tmpf.get()
                        P.op("act", lambda e, t_ap=t_ap, pg=pg, n=n: e.activation(out=t_ap[:, :n], in_=pg[:, :n], func=AF.Silu),
                             reads=[pgr], writes=[t_res])
                        P.op("dve", lambda e, t_ap=t_ap, pu=pu, c=c, c0=c0, n=n: e.tensor_tensor(out=act[:, c, c0:c0 + n], in0=t_ap[:, :n], in1=pu[:, :n], op=ALU.mult),
                             reads=[t_res, pur], writes=[r_act[c]])
            for oc in range(KC):
                wd, wdr = load_w(w_fd[l, half, oc], 22 * 128)
                for (c0, n) in grps:
                    ps, pres = dense_ps(wd, wdr, 22, 128, 0, act, r_act, c0, n, psA)
                    add_resid(ps, pres, oc, c0, n)

    dd_list = []

    def dd(name, ap, res_list):
        if not dbg or "dd" not in dbg:
            return
        shp = list(ap.shape)
        n = 1
        for d_ in shp[1:]:
            n *= d_
        t = nc.dram_tensor("dd_" + name, shp, ap.dtype, kind="ExternalOutput").ap()
        P.op("sp", lambda e: e.dma_start(out=t, in_=ap), reads=list(res_list), dma=True)

    def dump_dbg():
        P.op("sp", lambda e: e.dma_start(out=dbg_out, in_=xT.rearrange("p k n -> p (k n)")), reads=r_xT, dma=True)

    def bc_mid(ap2, n):
        a = ap2.ap
        return bass.AP(ap2.tensor, ap2.offset, [list(a[0]), [0, n], list(a[1])])

    def bc_last(apx, n):
        a = apx.ap
        return bass.AP(apx.tensor, apx.offset, [list(x) for x in a] + [[0, n]])

    def bank_bf(ps):
        return ps.bitcast(BF16)

    PI = math.pi

    def s5_setup():
        cvs = Carver()
        raw = cvs.get([192 + 4096]); r_raw = tok("s5raw")
        E = [cvs.get([KC * 128]), cvs.get([KC * 128])]; r_E = tok("E")
        Bb = [cvs.get([64, 16]), cvs.get([64, 16])]; r_Bb = tok("Bb")
        sm = cvs.get([16, 64]); r_sm = tok("sm")
        ki = cvs.get([64], I32)
        lr, li, ldt = raw[:, 0:64], raw[:, 64:128], raw[:, 128:192]
        br = raw[:, 192:1216].rearrange("p (j c) -> p j c", c=16)
        bi = raw[:, 1216:2240].rearrange("p (j c) -> p j c", c=16)
        cre = raw[:, 2240:3264].rearrange("p (j c) -> p j c", c=16)
        cim = raw[:, 3264:4288].rearrange("p (j c) -> p j c", c=16)
        P.op("sp", lambda e: e.dma_start(out=raw, in_=s5p), writes=[r_raw], dma=True)
        dt, z, p_, ang, r_, sin_a, cos_a, arm1, mag, ai, den, cr, ci, t1, t2, ar = [sm[:, i, :] for i in range(16)]
        P.op("act", lambda e: e.activation(out=dt, in_=ldt, func=AF.Exp), reads=[r_raw], writes=[r_sm])

        def reduce_sin(e, src_ang, shift, dst):
            e.tensor_scalar(out=t1, in0=src_ang, scalar1=shift, scalar2=1.0 / TWO_PI, op0=ALU.add, op1=ALU.mult)
            e.tensor_copy(out=ki, in_=t1)
            e.tensor_copy(out=t2, in_=ki)
            e.tensor_scalar(out=t1, in0=src_ang, scalar1=shift, scalar2=None, op0=ALU.add)
            e.scalar_tensor_tensor(out=dst, in0=t2, scalar=-TWO_PI, in1=t1, op0=ALU.mult, op1=ALU.add)
            return e.tensor_scalar(out=dst, in0=dst, scalar1=PI, scalar2=-PI, op0=ALU.min, op1=ALU.max)

        def f1(e):
            e = Synced(P, e)
            e.tensor_tensor(out=z, in0=lr, in1=dt, op=ALU.mult)
            e.tensor_scalar(out=p_, in0=z, scalar1=1.0 / 120, scalar2=1.0 / 24, op0=ALU.mult, op1=ALU.add)
            for cst_ in (1.0 / 6, 0.5, 1.0):
                e.tensor_tensor(out=p_, in0=p_, in1=z, op=ALU.mult)
                e.tensor_scalar(out=p_, in0=p_, scalar1=cst_, scalar2=None, op0=ALU.add)
            e.tensor_tensor(out=p_, in0=p_, in1=z, op=ALU.mult)
            e.tensor_tensor(out=ang, in0=li, in1=dt, op=ALU.mult)
            reduce_sin(e, ang, 0.0, r_)
            return reduce_sin(e, ang, PI / 2, arm1)
        P.op("dve", f1, reads=[r_raw, r_sm], writes=[r_sm])

        def f2(e):
            e.activation(out=sin_a, in_=r_, func=AF.Sin)
            return e.activation(out=cos_a, in_=arm1, func=AF.Sin)
        P.op("act", f2, reads=[r_sm], writes=[r_sm])

        def f3(e):
            e = Synced(P, e)
            e.tensor_tensor(out=t1, in0=p_, in1=cos_a, op=ALU.mult)
            e.tensor_scalar(out=t2, in0=cos_a, scalar1=-1.0, scalar2=None, op0=ALU.add)
            e.tensor_tensor(out=arm1, in0=t1, in1=t2, op=ALU.add)
            e.tensor_scalar(out=ar, in0=arm1, scalar1=1.0, scalar2=None, op0=ALU.add)
            e.tensor_scalar(out=mag, in0=p_, scalar1=1.0, scalar2=None, op0=ALU.add)
            e.tensor_tensor(out=ai, in0=mag, in1=sin_a, op=ALU.mult)
            e.tensor_tensor(out=den, in0=lr, in1=lr, op=ALU.mult)
            e.tensor_tensor(out=t1, in0=li, in1=li, op=ALU.mult)
            e.tensor_tensor(out=den, in0=den, in1=t1, op=ALU.add)
            e.reciprocal(out=den, in_=den)
            e.tensor_tensor(out=t1, in0=arm1, in1=lr, op=ALU.mult)
            e.tensor_tensor(out=t2, in0=ai, in1=li, op=ALU.mult)
            e.tensor_tensor(out=t1, in0=t1, in1=t2, op=ALU.add)
            e.tensor_tensor(out=cr, in0=t1, in1=den, op=ALU.mult)
            e.tensor_tensor(out=t1, in0=ai, in1=lr, op=ALU.mult)
            e.tensor_tensor(out=t2, in0=arm1, in1=li, op=ALU.mult)
            e.tensor_tensor(out=t1, in0=t1, in1=t2, op=ALU.subtract)
            e.tensor_tensor(out=ci, in0=t1, in1=den, op=ALU.mult)
            e.tensor_copy(out=A1[:, :, 0], in_=ar)
            e.tensor_copy(out=A1[:, :, 1], in_=ar)
            e.tensor_scalar(out=A2[:, :, 0], in0=ai, scalar1=-1.0, scalar2=None, op0=ALU.mult)
            e.tensor_copy(out=A2[:, :, 1], in_=ai)
            crb, cib = bc_last(cr, 16), bc_last(ci, 16)
            Er = E[0].rearrange("p (j g c) -> p j g c", g=2, c=16)
            Ei = E[1].rearrange("p (j g c) -> p j g c", g=2, c=16)
            e.tensor_tensor(out=Bb[0], in0=br, in1=crb, op=ALU.mult)
            e.tensor_tensor(out=Bb[1], in0=bi, in1=cib, op=ALU.mult)
            e.tensor_tensor(out=Bb[0], in0=Bb[0], in1=Bb[1], op=ALU.subtract)
            e.tensor_tensor(out=Bb[1], in0=bi, in1=crb, op=ALU.mult)
            for g2 in range(2):
                e.tensor_scalar(out=Er[:, :, g2, :], in0=Bb[0], scalar1=cst[:, C_HALF + g2:C_HALF + g2 + 1], scalar2=None, op0=ALU.mult)
            e.tensor_tensor(out=Bb[0], in0=br, in1=cib, op=ALU.mult)
            e.tensor_tensor(out=Bb[1], in0=Bb[1], in1=Bb[0], op=ALU.add)
            for g2 in range(2):
                e.tensor_scalar(out=Ei[:, :, g2, :], in0=Bb[1], scalar1=cst[:, C_HALF + g2:C_HALF + g2 + 1], scalar2=None, op0=ALU.mult)
            lC5 = lC.rearrange("p j r (g c) -> p j r g c", c=16)
            for ri, (cc, sgn) in enumerate(((cre, 1.0), (cim, -1.0))):
                for g2 in range(2):
                    ins = e.tensor_scalar(out=lC5[:, :, ri, g2, :], in0=cc, scalar1=cst[:, C_HALF + g2:C_HALF + g2 + 1], scalar2=sgn, op0=ALU.mult, op1=ALU.mult)
            return ins
        P.op("dve", f3, reads=[r_raw, r_sm, r_cst], writes=[r_sm, r_A, r_Bb, r_E, r_lC])
        for ri in range(2):
            for g in range(4):
                ps, pres = psB.get()

                def ft(e, ps=ps, ri=ri, g=g):
                    for q in range(4):
                        kc = g * 4 + q
                        ins = e.transpose(ps[:, q * 128:(q + 1) * 128], E[ri][:, kc * 128:(kc + 1) * 128], identf)
                    return ins
                P.op("pe", ft, reads=[r_E, r_cst], writes=[pres])
                P.op("act", lambda e, ps=ps, ri=ri, g=g: e.activation(out=lB[:, g * 4:g * 4 + 4, ri, :], in_=ps.rearrange("p (q c) -> p q c", c=128), func=AF.Copy),
                     reads=[pres], writes=[r_lB])

    s5_setup()
    fence()

    def ret_head(h, it, grps, last):
        samp = (it == SS)
        ntot = grps[-1][0] + grps[-1][1]
        wq, wqr = load_w(w_in[h], 4096)
        wk, wkr = load_w(w_in[4 + h], 4096)
        wv, wvr = load_w(w_in[8 + h], 4096)
        wg, wgr = load_w(w_in[12 + h], 4096)
        for (w, wr, dst, dres) in ((wq, wqr, bfA, r_bfA), (wk, wkr, bfB, r_bfB)):
            for (c0, n) in grps:
                p1, p1r = dense_ps(w, wr, KC, 256, 0, hT, [r_hT], c0, n, psA)
                p2, p2r = dense_ps(w, wr, KC, 256, 1, hT, [r_hT], c0, n, psA)
                ta, tar = tmpf.get()
                tb_, tbr = tmpf.get()

                def fr(e, p1=p1, p2=p2, ta=ta, tb_=tb_, dst=dst, c0=c0, n=n):
                    e = Synced(P, e, n < 300)
                    cos, sin = rott[:, 0, c0:c0 + n], rott[:, 1, c0:c0 + n]
                    e.tensor_tensor(out=ta[:, :n], in0=p1[:, :n], in1=cos, op=ALU.mult)
                    e.tensor_tensor(out=tb_[:, :n], in0=p2[:, :n], in1=sin, op=ALU.mult)
                    e.tensor_tensor(out=dst[:, 0, c0:c0 + n], in0=ta[:, :n], in1=tb_[:, :n], op=ALU.subtract)
                    e.tensor_tensor(out=ta[:, :n], in0=p1[:, :n], in1=sin, op=ALU.mult)
                    e.tensor_tensor(out=tb_[:, :n], in0=p2[:, :n], in1=cos, op=ALU.mult)
                    return e.tensor_tensor(out=dst[:, 1, c0:c0 + n], in0=ta[:, :n], in1=tb_[:, :n], op=ALU.add)
                P.op("dve", fr, reads=[p1r, p2r, r_rot], writes=[tar, tbr, dres])
        qdtab = bc_mid(cst[:, C_QDEC + h * 128:C_QDEC + (h + 1) * 128], NB)

        def fqd(e):
            for dc in range(2):
                ins = e.tensor_tensor(out=bfC[:, dc, :].rearrange("p (b c) -> p b c", c=128),
                                      in0=bfA[:, dc, 0:TT].rearrange("p (b c) -> p b c", c=128), in1=qdtab, op=ALU.mult)
            return ins
        P.op("pool", fqd, reads=[r_bfA, r_cst], writes=[r_bfC])
        for tb in range(NB):
            ps, pres = psA.get()

            def fv(e, ps=ps, tb=tb):
                wv3 = wv.rearrange("p (k c) -> p k c", c=256)
                for kc in range(KC):
                    ins = e.matmul(ps[:, :256], lhsT=hT[:, kc, tb * 128:(tb + 1) * 128], rhs=wv3[:, kc, :], start=(kc == 0), stop=(kc == KC - 1))
                return ins
            P.op("pe", fv, reads=[wvr, r_hT], writes=[pres])
            P.op("act", lambda e, ps=ps, tb=tb: e.activation(out=vtm[:, tb, :], in_=ps[:, :256], func=AF.Copy), reads=[pres], writes=[r_vtm])
        if samp:
            ps, pres = psA.get()

            def fvs(e, ps=ps):
                wv3 = wv.rearrange("p (k c) -> p k c", c=256)
                for kc in range(KC):
                    ins = e.matmul(ps[:NS, :256], lhsT=hT[:, kc, TT:NT], rhs=wv3[:, kc, :], start=(kc == 0), stop=(kc == KC - 1))
                return ins
            P.op("pe", fvs, reads=[wvr, r_hT], writes=[pres])
            P.op("act", lambda e, ps=ps: e.activation(out=vs[:NS, :], in_=ps[:NS, :256], func=AF.Copy, scale=1.0 / 16), reads=[pres], writes=[r_vs])
        gate_jobs = []
        for vc in range(2):
            for (c0, n) in grps:
                def gj(vc=vc, c0=c0, n=n):
                    ps, pres = dense_ps(wg, wgr, KC, 256, vc, hT, [r_hT], c0, n, psA)
                    P.op("act", lambda e, ps=ps, vc=vc, c0=c0, n=n: e.activation(out=gT[:, vc, c0:c0 + n], in_=ps[:, :n], func=AF.Silu), reads=[pres], writes=[r_gT])
                gate_jobs.append(gj)
        for tb in range(NB):
            ps, pres = psC.get()
            pb = bank_bf(ps)

            def fk(e, pb=pb, tb=tb):
                for dc in range(2):
                    ins = e.transpose(pb[:, dc * 128:(dc + 1) * 128], bfB[:, dc, tb * 128:(tb + 1) * 128], identb)
                return ins
            P.op("pe", fk, reads=[r_bfB, r_identb], writes=[pres])
            P.op("act", lambda e, pb=pb, tb=tb: e.activation(out=kdtm[:, tb, :], in_=pb[:, 0:256], func=AF.Copy, scale=cst[:, C_KDEC + h:C_KDEC + h + 1]),
                 reads=[pres, r_cst], writes=[r_kdtm])
        if samp:
            ps, pres = psC.get()
            pb = bank_bf(ps)

            def fks(e, pb=pb):
                for dc in range(2):
                    ins = e.transpose(pb[:NS, dc * 128:(dc + 1) * 128], bfB[:, dc, TT:NT], identb)
                return ins
            P.op("pe", fks, reads=[r_bfB, r_identb], writes=[pres])
            P.op("dve", lambda e, pb=pb: e.tensor_copy(out=ktms[:NS, :], in_=pb[:NS, 0:256]), reads=[pres], writes=[r_ktms])
        if h == 0 and it == 0:
            dd("r_qT", bfA, [r_bfA]); dd("r_kT", bfB, [r_bfB]); dd("r_vs", vs[:NS, :], [r_vs]); dd("r_ktms", ktms[:NS, :], [r_ktms])
            dd("r_vtm", vtm, [r_vtm]); dd("r_kdtm", kdtm, [r_kdtm]); dd("r_hT", hT, [r_hT])
            dd("rott", rott, [r_rot]); dd("lbe", lbe, [r_lb]); dd("lbc", lbc, [r_lb]); dd("oml", oml, [r_lb]); dd("pft", pft, [r_pft])
        P.op("act", lambda e: e.activation(out=Sbf, in_=Sret[:, h], func=AF.Copy), reads=[r_Sret[h]], writes=[r_Sbf])
        mk = cst[:, C_MASKT + h * 128:C_MASKT + (h + 1) * 128]
        for tb in range(NB):
            cs = slice(tb * 128, (tb + 1) * 128)
            ps, pres = psB.get()

            def fa(e, ps=ps, cs=cs):
                for dc in range(2):
                    ins = e.matmul(ps[:, :128], lhsT=bfB[:, dc, cs], rhs=bfA[:, dc, cs], start=(dc == 0), stop=(dc == 1))
                return ins
            P.op("pe", fa, reads=[r_bfA, r_bfB], writes=[pres])
            am, amr = attm.get()
            P.op("dve", lambda e, ps=ps, am=am: e.tensor_tensor(out=am, in0=ps[:, :128], in1=mk, op=ALU.mult), reads=[pres, r_cst], writes=[amr])
            if gate_jobs:
                gate_jobs.pop(0)()
            po, por = psB.get()

            def fo(e, po=po, am=am, tb=tb, cs=cs):
                for vc in range(2):
                    e.matmul(po[:, vc * 128:(vc + 1) * 128], lhsT=vtm[:, tb, vc * 128:(vc + 1) * 128], rhs=am, start=True, stop=False)
                    for dc in range(2):
                        ins = e.matmul(po[:, vc * 128:(vc + 1) * 128], lhsT=Sbf[:, dc, vc * 128:(vc + 1) * 128], rhs=bfC[:, dc, cs], start=False, stop=(dc == 1))
                return ins
            P.op("pe", fo, reads=[amr, r_vtm, r_Sbf, r_bfC], writes=[por])
            P.op("act", lambda e, po=po, cs=cs: e.activation(out=oT[:, :, cs], in_=po[:, :256].rearrange("p (v c) -> p v c", c=128), func=AF.Copy), reads=[por], writes=[r_oT])
            for dc in range(2):
                pss, pssr = psC.get()
                P.op("pe", lambda e, pss=pss, tb=tb, dc=dc: e.matmul(pss[:, :256], lhsT=kdtm[:, tb, dc * 128:(dc + 1) * 128], rhs=vtm[:, tb, :], start=True, stop=True),
                     reads=[r_kdtm, r_vtm], writes=[pssr])
                P.op("dve", lambda e, pss=pss, dc=dc: e.scalar_tensor_tensor(out=Sret[:, h, dc, :], in0=Sret[:, h, dc, :], scalar=cdec[h], in1=pss[:, :256], op0=ALU.mult, op1=ALU.add),
                     reads=[pssr, r_Sret[h]], writes=[r_Sret[h]])
            if tb < NB - 1:
                P.op("act", lambda e: e.activation(out=Sbf, in_=Sret[:, h], func=AF.Copy), reads=[r_Sret[h]], writes=[r_Sbf])
        while gate_jobs:
            gate_jobs.pop(0)()
        if last:
            P.op("sp", lambda e: e.dma_start(out=ret_p[h].rearrange("(dc p) v -> p dc v", p=128), in_=Sret[:, h]), reads=[r_Sret[h]], dma=True)
        if samp:
            pre = {}

            def ld_ret(b):
                sa, sar = s0.get()
                P.op("sp", lambda e, sa=sa, b=b: e.dma_start(out=sa, in_=st_ret[b, h].rearrange("(dc p) v -> p dc v", p=128)), writes=[sar], dma=True)
                pre[b] = (sa, sar)
            prevm = {}

            def mk_vm(b):
                vm, vmr = vmask.get()
                P.op("act", lambda e, vm=vm, b=b: e.activation(out=vm[:NS, :], in_=vs[:NS, :], func=AF.Copy, scale=identf[:NS, b:b + 1]),
                     reads=[r_vs, r_cst], writes=[vmr])
                prevm[b] = (vm, vmr)
            ld_ret(0)
            ld_ret(1)
            mk_vm(0)
            mk_vm(1)
            for b in range(NS):
                if b + 2 < NS:
                    ld_ret(b + 2)
                    mk_vm(b + 2)
                vm, vmr = prevm.pop(b)
                sa, sar = pre.pop(b)
                for dc in range(2):
                    pss, pssr = psC.get()
                    P.op("pe", lambda e, pss=pss, vm=vm, dc=dc: e.matmul(pss[:, :256], lhsT=ktms[:NS, dc * 128:(dc + 1) * 128], rhs=vm[:NS, :], start=True, stop=True),
                         reads=[r_ktms, vmr], writes=[pssr])
                    P.op("dve", lambda e, pss=pss, sa=sa, dc=dc: e.scalar_tensor_tensor(out=sa[:, dc, :], in0=sa[:, dc, :], scalar=gam[h], in1=pss[:, :256], op0=ALU.mult, op1=ALU.add),
                         reads=[pssr, sar], writes=[sar])
                P.op("sp", lambda e, sa=sa, b=b: e.dma_start(out=ret_s[b, h].rearrange("(dc p) v -> p dc v", p=128), in_=sa), reads=[sar], dma=True)
                sn, snr = snb.get()
                P.op("act", lambda e, sn=sn, sa=sa: e.activation(out=sn, in_=sa, func=AF.Copy), reads=[sar], writes=[snr])

                def part2(sn=sn, snr=snr, b=b):
                    po, por = psB.get()

                    def fso(e, po=po, sn=sn, b=b):
                        for vc in range(2):
                            for dc in range(2):
                                ins = e.matmul(po[:, vc:vc + 1], lhsT=sn[:, dc, vc * 128:(vc + 1) * 128], rhs=bfA[:, dc, TT + b:TT + b + 1], start=(dc == 0), stop=(dc == 1))
                        return ins
                    P.op("pe", fso, reads=[snr, r_bfA], writes=[por])
                    P.op("dve", lambda e, po=po, b=b: e.tensor_copy(out=oT[:, :, TT + b], in_=po[:, 0:2]), reads=[por], writes=[r_oT])
                if b > 0:
                    prev2()
                prev2 = part2
            prev2()
        if h == 0 and it == 0:
            dd("r_oT", oT, [r_oT]); dd("r_gT", gT, [r_gT])
        for (c0, n) in grps:
            psm, psmr = psC.get()
            P.op("pe", lambda e, psm=psm, c0=c0, n=n: [e.matmul(psm[:, :n], lhsT=onesf, rhs=oT[:, vc, c0:c0 + n], start=(vc == 0), stop=(vc == 1)) for vc in range(2)][-1],
                 reads=[r_oT, r_cst], writes=[psmr])
            psq, psqr = psC.get()
            for vc in range(2):
                s_ap, s_res = sq.get()
                P.op("act", lambda e, s_ap=s_ap, vc=vc, c0=c0, n=n: e.activation(out=s_ap[:, :n], in_=oT[:, vc, c0:c0 + n], func=AF.Square), reads=[r_oT], writes=[s_res])
                P.op("pe", lambda e, psq=psq, s_ap=s_ap, vc=vc, n=n: e.matmul(psq[:, :n], lhsT=onesb, rhs=s_ap[:, :n], start=(vc == 0), stop=(vc == 1)),
                     reads=[s_res, r_cst, r_identb], writes=[psqr])
            mean, meanr = tmpf.get()
            rs, rsr = tmpf.get()
            P.op("act", lambda e, mean=mean, psm=psm, n=n: e.activation(out=mean[:, :n], in_=psm[:, :n], func=AF.Copy, scale=1.0 / 256), reads=[psmr], writes=[meanr])

            def fvar(e, mean=mean, rs=rs, psq=psq, n=n):
                e = Synced(P, e, n < 300)
                e.tensor_tensor(out=rs[:, :n], in0=mean[:, :n], in1=mean[:, :n], op=ALU.mult)
                return e.scalar_tensor_tensor(out=rs[:, :n], in0=psq[:, :n], scalar=1.0 / 256, in1=rs[:, :n], op0=ALU.mult, op1=ALU.subtract)
            P.op("dve", fvar, reads=[meanr, psqr], writes=[rsr])
            P.op("act", lambda e, rs=rs, n=n: e.activation(out=rs[:, :n], in_=rs[:, :n], func=AF.Sqrt, bias=epscol), reads=[rsr, r_cst], writes=[rsr])

            def fgn(e, mean=mean, rs=rs, c0=c0, n=n):
                e = Synced(P, e, n < 300)
                e.reciprocal(out=rs[:, :n], in_=rs[:, :n])
                for vc in range(2):
                    o_ = oT[:, vc, c0:c0 + n]
                    e.tensor_tensor(out=o_, in0=o_, in1=mean[:, :n], op=ALU.subtract)
                    e.tensor_tensor(out=o_, in0=o_, in1=rs[:, :n], op=ALU.mult)
                    ins = e.scalar_tensor_tensor(out=mixT[:, h * 2 + vc, c0:c0 + n], in0=o_, scalar=g_retgn[:, h * 2 + vc:h * 2 + vc + 1], in1=gT[:, vc, c0:c0 + n],
                                                 op0=ALU.mult, op1=ALU.mult)
                return ins
            P.op("dve", fgn, reads=[meanr, rsr, r_oT, r_gT, r_pft], writes=[rsr, r_oT, r_mix[h * 2], r_mix[h * 2 + 1]])

    def hg_pair(hp, it, grps, last):
        samp = (it == SS)
        ntot = grps[-1][0] + grps[-1][1]
        NCH = TT // 64
        wq, wqr = load_w(w_in[16 + hp], 4096)
        wf, wfr = load_w(w_in[20 + hp], 4096)
        wi, wir = load_w(w_in[24 + hp], 4096)
        wg, wgr = load_w(w_in[28 + hp], 4096)
        resetm = cst[:, C_RESET:C_RESET + TT]
        for m in range(2):
            hd = hp * 2 + m
            for (c0, n) in grps:
                ps, pres = dense_ps(wf, wfr, KC, 256, m, hT, [r_hT], c0, n, psA)
                ta, tar = tmpf.get()
                P.op("act", lambda e, ps=ps, ta=ta, n=n: e.activation(out=ta[:, :n], in_=ps[:, :n], func=AF.Sigmoid), reads=[pres], writes=[tar])
                P.op("dve", lambda e, ta=ta, m=m, hd=hd, c0=c0, n=n: e.tensor_scalar(out=hf[:, m, c0:c0 + n], in0=ta[:, :n], scalar1=oml[:, hd:hd + 1], scalar2=lbc[:, hd:hd + 1],
                                                                                       op0=ALU.mult, op1=ALU.add), reads=[tar, r_lb], writes=[r_hf])
            ta, tar = tmpf.get()
            P.op("act", lambda e, ta=ta, m=m: e.activation(out=ta[:, :TT], in_=hf[:, m, 0:TT], func=AF.Ln), reads=[r_hf], writes=[tar])
            P.op("dve", lambda e, ta=ta, m=m: e.tensor_tensor_scan(out=hb[:, m, 0:TT], data0=resetm, data1=ta[:, :TT], initial=0.0, op0=ALU.mult, op1=ALU.add),
                 reads=[tar, r_cst], writes=[r_hb])
            if samp:
                P.op("dve", lambda e, m=m: e.tensor_copy(out=fsm[:, m, :], in_=hf[:, m, TT:NT]), reads=[r_hf], writes=[r_fsm])
            P.op("dve", lambda e, m=m: e.tensor_copy(out=hblast[:, m, :], in_=hb[:, m, 63:TT:64]), reads=[r_hb], writes=[r_hblast])
            P.op("act", lambda e, m=m: e.activation(out=hebl[:, m, :], in_=hblast[:, m, :], func=AF.Exp), reads=[r_hblast], writes=[r_hebl])
            P.op("dve", lambda e, m=m: e.tensor_scalar(out=hf[:, m, :ntot], in0=hf[:, m, :ntot], scalar1=-1.0, scalar2=1.0, op0=ALU.mult, op1=ALU.add),
                 reads=[r_hf, r_fsm], writes=[r_hf])
            for (c0, n) in grps:
                ps, pres = dense_ps(wq, wqr, KC, 256, m, hT, [r_hT], c0, n, psA)
                if c0 == 0:
                    ta, tar = tmpf.get()
                    tb_, tbr = tmpf.get()
                    P.op("act", lambda e, ta=ta, m=m: e.activation(out=ta[:, :TT], in_=hb[:, m, 0:TT], func=AF.Exp), reads=[r_hb], writes=[tar])
                    P.op("act", lambda e, tb_=tb_, ps=ps: e.activation(out=tb_[:, :TT], in_=ps[:, :TT], func=AF.Silu), reads=[pres], writes=[tbr])
                    P.op("dve", lambda e, ta=ta, tb_=tb_, m=m: e.tensor_tensor(out=bfA[:, m, 0:TT], in0=ta[:, :TT], in1=tb_[:, :TT], op=ALU.mult), reads=[tar, tbr], writes=[r_bfA])
                else:
                    P.op("act", lambda e, ps=ps, m=m: e.activation(out=bfA[:, m, TT:NT], in_=ps[:, :NS], func=AF.Silu), reads=[pres], writes=[r_bfA])
            ta, tar = tmpf.get()
            P.op("act", lambda e, ta=ta, m=m: e.activation(out=ta[:, :TT], in_=hb[:, m, 0:TT], func=AF.Exp, scale=-1.0), reads=[r_hb], writes=[tar])
            P.op("dve", lambda e, ta=ta, m=m: e.tensor_tensor(out=bfB[:, m, 0:TT], in0=hf[:, m, 0:TT], in1=ta[:, :TT], op=ALU.mult), reads=[tar, r_hf], writes=[r_bfB])
            if samp:
                P.op("dve", lambda e, m=m: e.tensor_copy(out=bfB[:, m, TT:NT], in_=hf[:, m, TT:NT]), reads=[r_hf], writes=[r_bfB])
            ta, tar = tmpf.get()

            def fkd(e, ta=ta, m=m):
                for ch in range(NCH):
                    ins = e.activation(out=ta[:, ch * 64:(ch + 1) * 64], in_=hb[:, m, ch * 64:(ch + 1) * 64], func=AF.Exp, scale=-1.0, bias=hblast[:, m, ch:ch + 1])
                return ins
            P.op("act", fkd, reads=[r_hb, r_hblast], writes=[tar])
            P.op("dve", lambda e, ta=ta, m=m: e.tensor_tensor(out=bfC[:, m, :], in0=hf[:, m, 0:TT], in1=ta[:, :TT], op=ALU.mult), reads=[tar, r_hf], writes=[r_bfC])
        hg_gate_jobs = {0: [], 1: []}
        for m in range(2):
            for (c0, n) in grps:
                def gj(m=m, c0=c0, n=n):
                    ps, pres = dense_ps(wg, wgr, KC, 256, m, hT, [r_hT], c0, n, psA)
                    P.op("act", lambda e, ps=ps, m=m, c0=c0, n=n: e.activation(out=gT[:, m, c0:c0 + n], in_=ps[:, :n], func=AF.Silu), reads=[pres], writes=[r_gT])
                hg_gate_jobs[m].append(gj)
        for tb in range(NB):
            ps, pres = psA.get()

            def fv(e, ps=ps, tb=tb):
                w3 = wi.rearrange("p (k c) -> p k c", c=256)
                for kc in range(KC):
                    ins = e.matmul(ps[:, :256], lhsT=hT[:, kc, tb * 128:(tb + 1) * 128], rhs=w3[:, kc, :], start=(kc == 0), stop=(kc == KC - 1))
                return ins
            P.op("pe", fv, reads=[wir, r_hT], writes=[pres])
            P.op("act", lambda e, ps=ps, tb=tb: e.activation(out=vtm[:, tb, :], in_=ps[:, :256], func=AF.Copy), reads=[pres], writes=[r_vtm])
        if samp:
            ps, pres = psA.get()

            def fvs(e, ps=ps):
                w3 = wi.rearrange("p (k c) -> p k c", c=256)
                for kc in range(KC):
                    ins = e.matmul(ps[:NS, :256], lhsT=hT[:, kc, TT:NT], rhs=w3[:, kc, :], start=(kc == 0), stop=(kc == KC - 1))
                return ins
            P.op("pe", fvs, reads=[wir, r_hT], writes=[pres])
            P.op("act", lambda e, ps=ps: e.activation(out=vs[:NS, :], in_=ps[:NS, :256], func=AF.Copy), reads=[pres], writes=[r_vs])
        for tb in range(NB):
            ps, pres = psC.get()
            pb = bank_bf(ps)

            def fk(e, pb=pb, tb=tb):
                for m in range(2):
                    ins = e.transpose(pb[:, m * 128:(m + 1) * 128], bfC[:, m, tb * 128:(tb + 1) * 128], identb)
                return ins
            P.op("pe", fk, reads=[r_bfC, r_identb], writes=[pres])
            P.op("dve", lambda e, pb=pb, tb=tb: e.tensor_copy(out=kdtm[:, tb, :], in_=pb[:, 0:256]), reads=[pres], writes=[r_kdtm])
        if samp:
            ps, pres = psC.get()
            pb = bank_bf(ps)

            def fks(e, pb=pb):
                for m in range(2):
                    ins = e.transpose(pb[:NS, m * 128:(m + 1) * 128], bfB[:, m, TT:NT], identb)
                return ins
            P.op("pe", fks, reads=[r_bfB, r_identb], writes=[pres])
            P.op("dve", lambda e, pb=pb: e.tensor_copy(out=ktms[:NS, :], in_=pb[:NS, 0:256]), reads=[pres], writes=[r_ktms])
        mH = cst[:, C_MASKH:C_MASKH + 128]
        if hp == 0 and it == 0:
            dd("h_k", hf, [r_hf]); dd("h_b", hb, [r_hb]); dd("h_qe", bfA, [r_bfA]); dd("h_ke", bfB, [r_bfB]); dd("h_kd", bfC, [r_bfC])
            dd("h_vtm", vtm, [r_vtm]); dd("h_kdtm", kdtm, [r_kdtm]); dd("h_ebl", hebl, [r_hebl]); dd("h_gT", gT, [r_gT]); dd("h_fsm", fsm, [r_fsm])
            dd("h_vs", vs[:NS, :], [r_vs]); dd("h_ktms", ktms[:NS, :], [r_ktms])
        for m in range(2):
            hd = hp * 2 + m
            ms = slice(m * 128, (m + 1) * 128)
            P.op("act", lambda e, hd=hd: e.activation(out=SbAll[:, 0, :], in_=Shg[:, hd], func=AF.Copy), reads=[r_Shg[hd]], writes=[r_SbAll[0]])
            for ch in range(NCH):
                tb, sub = ch // 2, ch % 2
                rs_ = slice(sub * 64, (sub + 1) * 64)
                pss, pssr = psC.get()
                P.op("pe", lambda e, pss=pss, tb=tb, ms=ms, rs_=rs_: e.matmul(pss[:, :128], lhsT=kdtm[rs_, tb, ms], rhs=vtm[rs_, tb, ms], start=True, stop=True),
                     reads=[r_kdtm, r_vtm], writes=[pssr])
                P.op("dve", lambda e, pss=pss, hd=hd, m=m, ch=ch: e.scalar_tensor_tensor(out=Shg[:, hd], in0=Shg[:, hd], scalar=hebl[:, m, ch:ch + 1], in1=pss[:, :128],
                                                                                         op0=ALU.mult, op1=ALU.add), reads=[pssr, r_Shg[hd], r_hebl], writes=[r_Shg[hd]])
                if ch < NCH - 1:
                    P.op("act", lambda e, hd=hd, ch=ch: e.activation(out=SbAll[:, ch + 1, :], in_=Shg[:, hd], func=AF.Copy), reads=[r_Shg[hd]], writes=[r_SbAll[ch + 1]])
            for tb in range(NB):
                cs = slice(tb * 128, (tb + 1) * 128)
                ps, pres = psB.get()
                P.op("pe", lambda e, ps=ps, m=m, cs=cs: e.matmul(ps[:, :128], lhsT=bfB[:, m, cs], rhs=bfA[:, m, cs], start=True, stop=True), reads=[r_bfA, r_bfB], writes=[pres])
                am, amr = attm.get()
                P.op("dve", lambda e, ps=ps, am=am: e.tensor_tensor(out=am, in0=ps[:, :128], in1=mH, op=ALU.mult), reads=[pres, r_cst], writes=[amr])
                if hg_gate_jobs[m]:
                    hg_gate_jobs[m].pop(0)()
                po, por = psB.get()

                def fo1(e, po=po, am=am, tb=tb, m=m, ms=ms):
                    e.matmul(po[:, 0:128], lhsT=vtm[:, tb, ms], rhs=am, start=True, stop=False)
                    e.matmul(po[:, 0:64], lhsT=SbAll[:, 2 * tb, :], rhs=bfA[:, m, tb * 128:tb * 128 + 64], start=False, stop=False)
                    return e.matmul(po[:, 64:128], lhsT=SbAll[:, 2 * tb + 1, :], rhs=bfA[:, m, tb * 128 + 64:tb * 128 + 128], start=False, stop=True)
                P.op("pe", fo1, reads=[amr, r_vtm, r_SbAll[2 * tb], r_SbAll[2 * tb + 1], r_bfA], writes=[por])
                P.op("act", lambda e, po=po, m=m, cs=cs: e.activation(out=oT[:, m, cs], in_=po[:, :128], func=AF.Copy), reads=[por], writes=[r_oT])
            while hg_gate_jobs[m]:
                hg_gate_jobs[m].pop(0)()
            if last:
                P.op("sp", lambda e, hd=hd: e.dma_start(out=hg_p[hd], in_=Shg[:, hd]), reads=[r_Shg[hd]], dma=True)
            if samp:
                pre = {}

                def ld_hg(b, hd=hd):
                    sa, sar = s0.get()
                    sa2 = sa[:, 0, 0:128]
                    P.op("sp", lambda e, sa2=sa2, b=b, hd=hd: e.dma_start(out=sa2, in_=st_hg[b, hd]), writes=[sar], dma=True)
                    pre[b] = (sa, sar, sa2)
                prevm = {}

                def mk_vm(b, ms=ms):
                    vm, vmr = vmask.get()
                    P.op("act", lambda e, vm=vm, b=b, ms=ms: e.activation(out=vm[:NS, 0:128], in_=vs[:NS, ms], func=AF.Copy, scale=identf[:NS, b:b + 1]),
                         reads=[r_vs, r_cst], writes=[vmr])
                    prevm[b] = (vm, vmr)
                ld_hg(0)
                ld_hg(1)
                mk_vm(0)
                mk_vm(1)
                for b in range(NS):
                    if b + 2 < NS:
                        ld_hg(b + 2)
                        mk_vm(b + 2)
                    vm, vmr = prevm.pop(b)
                    sa, sar, sa2 = pre.pop(b)
                    pss, pssr = psC.get()
                    P.op("pe", lambda e, pss=pss, vm=vm, ms=ms: e.matmul(pss[:, :128], lhsT=ktms[:NS, ms], rhs=vm[:NS, 0:128], start=True, stop=True), reads=[r_ktms, vmr], writes=[pssr])
                    P.op("dve", lambda e, pss=pss, sa2=sa2, m=m, b=b: e.scalar_tensor_tensor(out=sa2, in0=sa2, scalar=fsm[:, m, b:b + 1], in1=pss[:, :128], op0=ALU.mult, op1=ALU.add),
                         reads=[pssr, sar, r_fsm], writes=[sar])
                    P.op("sp", lambda e, sa2=sa2, b=b, hd=hd: e.dma_start(out=hg_s[b, hd], in_=sa2), reads=[sar], dma=True)
                    sn, snr = snb.get()
                    sn2 = sn[:, 0, 0:128]
                    P.op("act", lambda e, sn2=sn2, sa2=sa2: e.activation(out=sn2, in_=sa2, func=AF.Copy), reads=[sar], writes=[snr])

                    def part2(sn2=sn2, snr=snr, m=m, b=b):
                        po, por = psB.get()
                        P.op("pe", lambda e, po=po, sn2=sn2, m=m, b=b: e.matmul(po[:, 0:1], lhsT=sn2, rhs=bfA[:, m, TT + b:TT + b + 1], start=True, stop=True), reads=[snr, r_bfA], writes=[por])
                        P.op("dve", lambda e, po=po, m=m, b=b: e.tensor_copy(out=oT[:, m, TT + b:TT + b + 1], in_=po[:, 0:1]), reads=[por], writes=[r_oT])
                    if b > 0:
                        prev2()
                    prev2 = part2
                prev2()
            if hp == 0 and it == 0 and m == 1:
                dd("h_oT", oT, [r_oT])
            for (c0, n) in grps:
                s_ap, s_res = sq.get()
                P.op("act", lambda e, s_ap=s_ap, m=m, c0=c0, n=n: e.activation(out=s_ap[:, :n], in_=oT[:, m, c0:c0 + n], func=AF.Square), reads=[r_oT], writes=[s_res])
                psq, psqr = psC.get()
                P.op("pe", lambda e, psq=psq, s_ap=s_ap, n=n: e.matmul(psq[:, :n], lhsT=onesb, rhs=s_ap[:, :n], start=True, stop=True), reads=[s_res, r_cst, r_identb], writes=[psqr])
                rs, rsr = tmpf.get()
                P.op("act", lambda e, rs=rs, psq=psq, n=n: e.activation(out=rs[:, :n], in_=psq[:, :n], func=AF.Sqrt, scale=1.0 / 128, bias=epscol), reads=[psqr, r_cst], writes=[rsr])

                def fn_(e, rs=rs, m=m, hd=hd, c0=c0, n=n):
                    e = Synced(P, e, n < 300)
                    e.reciprocal(out=rs[:, :n], in_=rs[:, :n])
                    e.scalar_tensor_tensor(out=rs[:, :n], in0=oT[:, m, c0:c0 + n], scalar=g_hggn[:, hd:hd + 1], in1=rs[:, :n], op0=ALU.mult, op1=ALU.mult)
                    return e.tensor_tensor(out=mixT[:, 8 + hd, c0:c0 + n], in0=rs[:, :n], in1=gT[:, m, c0:c0 + n], op=ALU.mult)
                P.op("dve", fn_, reads=[rsr, r_oT, r_gT, r_pft], writes=[rsr, r_mix[8 + hd]])

    def s5_bu(cols0, n, hbv, r_HB):
        kper = (512 // n) // 2
        cnt = 0
        for k0 in range(0, KC, kper):
            for j4 in range(4):
                ps, pres = psA.get()

                def fb(e, ps=ps, k0=k0, j4=j4):
                    for kk in range(kper):
                        kc = k0 + kk
                        for ri in range(2):
                            pi = kk * 2 + ri
                            ins = e.matmul(ps[:, pi * n:(pi + 1) * n], lhsT=lB[32 * j4:32 * j4 + 32, kc, ri, :], rhs=hT[32 * j4:32 * j4 + 32, kc, cols0:cols0 + n],
                                           start=True, stop=True, tile_position=(32 * j4, 0))
                    return ins
                P.op("pe", fb, reads=[r_lB, r_hT], writes=[pres])
                js = k0 * 4 + j4
                eng = "act"
                cnt += 1

                def fe(e, ps=ps, js=js, eng=eng):
                    src = ps[:, :kper * 2 * n].rearrange("p (j r t) -> p j r t", r=2, t=n)
                    dst = hbv[:, js:js + 4 * (kper - 1) + 1:4, :, :]
                    if eng == "act":
                        return e.activation(out=dst, in_=src, func=AF.Copy)
                    return e.tensor_copy(out=dst, in_=src)
                P.op(eng, fe, reads=[pres], writes=[r_HB])

    def s5_y_mm(n, hbb, r_hbb):
        ps, pres = psB.get()

        def fy(e, ps=ps):
            for kc in range(KC):
                for j4 in range(4):
                    j = kc * 4 + j4
                    for ri in range(2):
                        ins = e.matmul(ps[32 * j4:32 * j4 + 32, kc * n:(kc + 1) * n], lhsT=lC[:, j, ri, :], rhs=hbb[:, j, ri, :],
                                       start=(ri == 0), stop=(ri == 1), tile_position=(0, 32 * j4))
            return ins
        P.op("pe", fy, reads=[r_lC] + list(r_hbb), writes=[pres])
        return ps, pres

    def s5_gelu(ps, pres, cols0, n, dst_cols):
        yv, yvr = tmpf.get()
        t2, t2r = tmpf.get()

        def fg(e, ps=ps, yv=yv, t2=t2):
            e = Synced(P, e, KC * n < 300)
            y3 = yv[:, :KC * n].rearrange("p (k t) -> p k t", t=n)
            e.tensor_tensor(out=y3, in0=hT[:, :, cols0:cols0 + n], in1=bc_last(dcolT, n), op=ALU.mult)
            e.tensor_tensor(out=yv[:, :KC * n], in0=yv[:, :KC * n], in1=ps[:, :KC * n], op=ALU.add)
            e.tensor_tensor(out=t2[:, :KC * n], in0=yv[:, :KC * n], in1=yv[:, :KC * n], op=ALU.mult)
            e.tensor_scalar(out=t2[:, :KC * n], in0=t2[:, :KC * n], scalar1=0.044715, scalar2=1.0, op0=ALU.mult, op1=ALU.add)
            return e.tensor_tensor(out=t2[:, :KC * n], in0=t2[:, :KC * n], in1=yv[:, :KC * n], op=ALU.mult)
        P.op("dve", fg, reads=[pres, r_hT, r_pft], writes=[yvr, t2r])
        P.op("act", lambda e, t2=t2: e.activation(out=t2[:, :KC * n], in_=t2[:, :KC * n], func=AF.Sigmoid, scale=1.5957691216057308), reads=[t2r], writes=[t2r])
        P.op("dve", lambda e, yv=yv, t2=t2: e.tensor_tensor(out=mixT[:, :, dst_cols:dst_cols + n], in0=yv[:, :KC * n].rearrange("p (k t) -> p k t", t=n),
                                                          in1=t2[:, :KC * n].rearrange("p (k t) -> p k t", t=n), op=ALU.mult), reads=[yvr, t2r], writes=r_mix)

    def s5_layer(it, last, pre=False):
        NSC = TT // TC
        if it == SS:
            stage = cv_s5stage
            H0, Hn = HBs[1], HBs[0]
            for ri, src in enumerate((st_s5r, st_s5i)):
                for half in range(2):
                    P.op("sp", lambda e, src=src, half=half: e.dma_start(out=stage[:NS, :], in_=src[:, half * 4096:(half + 1) * 4096]), writes=[r_stage], dma=True)
                    ps, pres = psB.get()

                    def ftr(e, ps=ps):
                        for jj in range(32):
                            ins = e.transpose(ps[:, jj * NS:(jj + 1) * NS], stage[:NS, jj * 128:(jj + 1) * 128], identf[:NS, :NS])
                        return ins
                    P.op("pe", ftr, reads=[r_stage, r_cst], writes=[pres])
                    P.op("dve", lambda e, ps=ps, ri=ri, half=half: e.tensor_copy(out=H0[:, half * 32:(half + 1) * 32, ri, :], in_=ps.rearrange("p (j b) -> p j b", b=NS)),
                         reads=[pres], writes=[r_HBs[1]])
            s5_bu(TT, NS, Hn, r_HBs[0])
            HS1 = HBb_raw.rearrange("p (j r t) -> p j r t", r=2, t=NS)

            def fs(e):
                e.tensor_tensor(out=HS1, in0=H0, in1=bc_last(A1, NS), op=ALU.mult)
                e.tensor_tensor(out=Hn, in0=Hn, in1=HS1, op=ALU.add)
                e.tensor_tensor(out=HS1[:, :, 0, :], in0=H0[:, :, 1, :], in1=bc_last(A2[:, :, 0], NS), op=ALU.mult)
                e.tensor_tensor(out=HS1[:, :, 1, :], in0=H0[:, :, 0, :], in1=bc_last(A2[:, :, 1], NS), op=ALU.mult)
                return e.tensor_tensor(out=Hn, in0=Hn, in1=HS1, op=ALU.add)
            P.op("dve", fs, reads=[r_HBs[0], r_HBs[1], r_A], writes=[r_HBs[0], r_HBb, r_HBbs[0], r_HBbs[1]])
            hbs = HBbs[0]
            P.op("act", lambda e: e.activation(out=hbs, in_=Hn, func=AF.Copy), reads=[r_HBs[0]], writes=[r_HBb, r_HBbs[0]])
            ps, pres = s5_y_mm(NS, hbs, [r_HBbs[0]])
            s5_gelu(ps, pres, TT, NS, TT)
            for ri, dst in enumerate((s5r_s, s5i_s)):
                for half in range(2):
                    for g in range(8):
                        ps, pres = psB.get()

                        def fto(e, ps=ps, ri=ri, half=half, g=g):
                            for q in range(4):
                                j = half * 32 + g * 4 + q
                                ins = e.transpose(ps[:NS, q * 128:(q + 1) * 128], Hn[:, j, ri, :], identf)
                            return ins
                        P.op("pe", fto, reads=[r_HBs[0], r_cst], writes=[pres])
                        P.op("act", lambda e, ps=ps, g=g: e.activation(out=stage[:NS, g * 512:(g + 1) * 512], in_=ps[:NS, :], func=AF.Copy), reads=[pres], writes=[r_stage])
                    P.op("sp", lambda e, dst=dst, half=half: e.dma_start(out=dst[:, half * 4096:(half + 1) * 4096], in_=stage[:NS, :]), reads=[r_stage], dma=True)
        if pre:
            stage = cv_s5stage
            PW1 = stage[:, 0:2048].rearrange("p (t j r) -> p t j r", j=64, r=2)
            PW2 = stage[:, 2048:4096].rearrange("p (t j r) -> p t j r", j=64, r=2)

            def fgen(e):
                e = Synced(P, e)
                cur = Pw[0]
                e.memset(cur[:, :, 0], 1.0)
                e.memset(cur[:, :, 1], 0.0)
                for k_ in range(TC + 1):
                    if k_ < TC:
                        t_ = TC - 1 - k_
                        e.tensor_copy(out=PW1[:, t_], in_=bc_last(cur[:, :, 0], 2))
                        e.tensor_copy(out=PW2[:, t_, :, 1], in_=cur[:, :, 1])
                        e.tensor_scalar(out=PW2[:, t_, :, 0], in0=cur[:, :, 1], scalar1=-1.0, scalar2=None, op0=ALU.mult)
                    else:
                        e.tensor_copy(out=A16a, in_=bc_last(cur[:, :, 0], 2))
                        e.tensor_copy(out=A16b[:, :, 1], in_=cur[:, :, 1])
                        ins = e.tensor_scalar(out=A16b[:, :, 0], in0=cur[:, :, 1], scalar1=-1.0, scalar2=None, op0=ALU.mult)
                        break
                    nxt = Pw[(k_ + 1) % 2]
                    e.tensor_tensor(out=st1, in0=cur, in1=A1, op=ALU.mult)
                    e.tensor_tensor(out=st2, in0=cur[:, :, ::-1], in1=A2, op=ALU.mult)
                    e.tensor_tensor(out=nxt, in0=st1, in1=st2, op=ALU.add)
                    cur = nxt
                return ins
            P.op("dve", fgen, reads=[r_A], writes=[r_stage, r_pw, r_st, r_A16])
            TMP = HBb_raw.rearrange("p (t j r) -> p t j r", j=64, r=2)
            s5_bu(0, TC, HBs[0], r_HBs[0])
            for sc in range(NSC):
                k = sc % 2
                if sc + 1 < NSC:
                    s5_bu((sc + 1) * TC, TC, HBs[1 - k], r_HBs[1 - k])

                def fpre(e, HR=HBraw[k]):
                    e.tensor_tensor(out=TMP, in0=HR[:, :, :, ::-1], in1=PW2, op=ALU.mult)
                    e.tensor_tensor(out=HR, in0=HR, in1=PW1, op=ALU.mult)
                    e.tensor_tensor(out=HR, in0=HR, in1=TMP, op=ALU.add)
                    e.tensor_reduce(out=cbuf, in_=HR.rearrange("p t j r -> p j r t"), axis=mybir.AxisListType.X, op=ALU.add)
                    es = Synced(P, e)
                    es.tensor_tensor(out=st1, in0=Hst, in1=A16a, op=ALU.mult)
                    es.tensor_tensor(out=st2, in0=Hst[:, :, ::-1], in1=A16b, op=ALU.mult)
                    es.tensor_tensor(out=st1, in0=st1, in1=st2, op=ALU.add)
                    return es.tensor_tensor(out=Hst, in0=st1, in1=cbuf, op=ALU.add)
                P.op("dve", fpre, reads=[r_HBs[k], r_stage, r_A16, r_Hst], writes=[r_HBs[k], r_HBb, r_HBbs[0], r_HBbs[1], r_st, r_pw, r_Hst])
            return
        s5_bu(0, TC, HBs[0], r_HBs[0])
        pend = None
        for sc in range(NSC):
            k = sc % 2
            HB, rHB, HBb, rHBb = HBs[k], r_HBs[k], HBbs[k], r_HBbs[k]
            if sc + 1 < NSC:
                s5_bu((sc + 1) * TC, TC, HBs[1 - k], r_HBs[1 - k])

            def fscan(e, HB=HB, HR=HBraw[k]):
                for t in range(TC):
                    prev = Hst if t == 0 else HR[:, t - 1]
                    prevs = Hst[:, :, ::-1] if t == 0 else HR[:, t - 1, :, ::-1]
                    cur = HR[:, t]
                    e.tensor_tensor(out=st1, in0=prev, in1=A1, op=ALU.mult)
                    e.tensor_tensor(out=st2, in0=prevs, in1=A2, op=ALU.mult)
                    i3 = e.tensor_tensor(out=cur, in0=cur, in1=st1, op=ALU.add)
                    P.selfsync(i3)
                    i4 = e.tensor_tensor(out=cur, in0=cur, in1=st2, op=ALU.add)
                    P.selfsync(i4)
                return e.tensor_copy(out=Hst, in_=HR[:, TC - 1])
            P.op("dve", fscan, reads=[rHB, r_A, r_Hst], writes=[rHB, r_st, r_Hst])
            if pre:
                continue
            P.op("act", lambda e, HB=HB, HBb=HBb: e.activation(out=HBb, in_=HB, func=AF.Copy), reads=[rHB], writes=[rHBb, r_HBb])
            ps, pres = s5_y_mm(TC, HBb, [rHBb])
            if pend is not None:
                s5_gelu(*pend)
            pend = (ps, pres, sc * TC, TC, sc * TC)
        if pend is not None:
            s5_gelu(*pend)
        if last:
            for ri, dst in enumerate((s5r_p, s5i_p)):
                ps, pres = psB.get()
                P.op("pe", lambda e, ps=ps, ri=ri: e.transpose(ps[:64, 0:128], Hst[:, :, ri], identf), reads=[r_Hst, r_cst], writes=[pres])
                ta, tar = tmpf.get()
                P.op("act", lambda e, ps=ps, ta=ta: e.activation(out=ta[:64, 0:128], in_=ps[:64, 0:128], func=AF.Copy), reads=[pres], writes=[tar])
                P.op("sp", lambda e, dst=dst, ta=ta: e.dma_start(out=dst, in_=ta[:64, 0:128]), reads=[tar], dma=True)

    cv_s5stage = cv.get([4096]); r_stage = tok("stage")

    for it in range(NTILES):
        grps = groups(it)
        ntot = grps[-1][0] + grps[-1][1]
        t0 = it * TT
        last = (it == NTILES - 1)
        pre = it < NPRE
        tout = (it - NPRE) * TT
        if it == 0 or pre or it == NPRE:
            fence()
        for tb in range(NB):
            xa, xr = xin.get()
            P.op("sp", lambda e, xa=xa, tb=tb, t0=t0: e.dma_start(out=xa, in_=xp[t0 + tb * 128:t0 + (tb + 1) * 128, :]), writes=[xr], dma=True)
            for g4 in range(4):
                ps, pres = psB.get()

                def f(e, ps=ps, xa=xa, g4=g4):
                    for q in range(4):
                        kc = g4 * 4 + q
                        ins = e.transpose(ps[:, q * 128:(q + 1) * 128], xa[:, kc * 128:(kc + 1) * 128], identf)
                    return ins
                P.op("pe", f, reads=[xr, r_cst], writes=[pres])
                eng = "dve" if g4 % 2 == 0 else "act"

                def cp(e, ps=ps, g4=g4, tb=tb, eng=eng):
                    src = ps.rearrange("p (q c) -> p q c", c=128)
                    dst = xT[:, g4 * 4:g4 * 4 + 4, tb * 128:(tb + 1) * 128]
                    if eng == "dve":
                        return e.tensor_copy(out=dst, in_=src)
                    return e.activation(out=dst, in_=src, func=AF.Copy)
                P.op(eng, cp, reads=[pres], writes=r_xT[g4 * 4:g4 * 4 + 4])
        if it == SS:
            xa, xr = xin.get()
            P.op("sp", lambda e, xa=xa: e.dma_start(out=xa[:NS, :], in_=xs), writes=[xr], dma=True)
            ps, pres = psB.get()

            def f(e, ps=ps, xa=xa):
                for kc in range(KC):
                    ins = e.transpose(ps[:, kc * NS:(kc + 1) * NS], xa[:NS, kc * 128:(kc + 1) * 128], identf[:NS, :NS])
                return ins
            P.op("pe", f, reads=[xr, r_cst], writes=[pres])
            P.op("dve", lambda e, ps=ps: e.tensor_copy(out=xT[:, :, TT:TT + NS], in_=ps[:, :KC * NS].rearrange("p (k c) -> p k c", c=NS)),
                 reads=[pres], writes=r_xT)
        if dbg == "x":
            dump_dbg()
            break
        fence()
        if "nomix0" not in (dbg or ""):
            rmsnorm(g_attn, hT, [r_hT], grps)
            P.op("sp", lambda e, t0=t0: e.dma_start(out=rott[:, :, 0:TT], in_=rot[:, :, t0:t0 + TT].rearrange("a p n -> p a n")), writes=[r_rot], dma=True)
            if it == SS:
                P.op("sp", lambda e: e.dma_start(out=rott[:, :, TT:NT], in_=rot[:, :, T:T + NS].rearrange("a p n -> p a n")), writes=[r_rot], dma=True)
            for h in range(1 if (dbg and "small" in dbg) else RET_H):
                ret_head(h, it, grps, last)
            for hp in range(1 if (dbg and "small" in dbg) else 4):
                hg_pair(hp, it, grps, last)
            for blk in range(8):
                w, wr = load_w(w_out[blk], 4096)
                for m in range(2):
                    oc = blk * 2 + m
                    for (c0, n) in grps:
                        ps, pres = dense_ps(w, wr, KC, 256, m, mixT, r_mix, c0, n, psA)
                        add_resid(ps, pres, oc, c0, n)
        if dbg and dbg.startswith("mix0"):
            dump_dbg()
            break
        fence()
        ffn(0, grps)
        if dbg and dbg.startswith("ffn0"):
            dump_dbg()
            break
        fence()
        rmsnorm(g_ssm, hT, [r_hT], grps)
        s5_layer(it, last, pre)
        if pre:
            continue
        for blk in range(8):
            wa, war = load_w(w_ga[blk], 4096)
            wb, wbr = load_w(w_gb[blk], 4096)
            for m in range(2):
                oc = blk * 2 + m
                for (c0, n) in grps:
                    pa, par = dense_ps(wa, war, KC, 256, m, mixT, r_mix, c0, n, psA)
                    pb_, pbr = dense_ps(wb, wbr, KC, 256, m, mixT, r_mix, c0, n, psA)
                    ta, tar = tmpf.get()
                    P.op("act", lambda e, ta=ta, pb_=pb_, n=n: e.activation(out=ta[:, :n], in_=pb_[:, :n], func=AF.Sigmoid), reads=[pbr], writes=[tar])

                    def fglu(e, ta=ta, pa=pa, oc=oc, c0=c0, n=n):
                        e = Synced(P, e, n < 300)
                        e.tensor_tensor(out=ta[:, :n], in0=ta[:, :n], in1=pa[:, :n], op=ALU.mult)
                        return e.tensor_tensor(out=xT[:, oc, c0:c0 + n], in0=xT[:, oc, c0:c0 + n], in1=ta[:, :n], op=ALU.add)
                    P.op("dve", fglu, reads=[tar, par, r_xT[oc]], writes=[tar, r_xT[oc]])
        if dbg == "mix1":
            dump_dbg()
            break
        fence()
        ffn(1, grps)
        rmsnorm(g_fin, xT, r_xT, grps)
        if dbg == "final":
            dump_dbg()
            break
        fence()
        for tb in range(NB):
            xa, xr = xin.get()
            for g4 in range(4):
                ps, pres = psB.get()

                def f(e, ps=ps, g4=g4, tb=tb):
                    for q in range(4):
                        kc = g4 * 4 + q
                        ins = e.transpose(ps[:, q * 128:(q + 1) * 128], xT[:, kc, tb * 128:(tb + 1) * 128], identf)
                    return ins
                P.op("pe", f, reads=r_xT[g4 * 4:g4 * 4 + 4] + [r_cst], writes=[pres])
                eng = "dve" if g4 % 2 == 0 else "act"

                def cp(e, ps=ps, g4=g4, xa=xa, eng=eng):
                    if eng == "dve":
                        return e.tensor_copy(out=xa[:, g4 * 512:(g4 + 1) * 512], in_=ps)
                    return e.activation(out=xa[:, g4 * 512:(g4 + 1) * 512], in_=ps, func=AF.Copy)
                P.op(eng, cp, reads=[pres], writes=[xr])
            P.op("sp", lambda e, xa=xa, tb=tb, tout=tout: e.dma_start(out=yp[tout + tb * 128:tout + (tb + 1) * 128, :], in_=xa), reads=[xr], dma=True)
        if it == SS:
            xa, xr = xin.get()
            for g4 in range(4):
                ps, pres = psB.get()

                def f(e, ps=ps, g4=g4):
                    for q in range(4):
                        kc = g4 * 4 + q
                        ins = e.transpose(ps[:NS, q * 128:(q + 1) * 128], xT[:, kc, TT:NT], identf)
                    return ins
                P.op("pe", f, reads=r_xT[g4 * 4:g4 * 4 + 4] + [r_cst], writes=[pres])
                P.op("dve", lambda e, ps=ps, g4=g4, xa=xa: e.tensor_copy(out=xa[:NS, g4 * 512:(g4 + 1) * 512], in_=ps[:NS, :]), reads=[pres], writes=[xr])
            P.op("sp", lambda e, xa=xa: e.dma_start(out=ys, in_=xa[:NS, :]), reads=[xr], dma=True)

    P.emit(nc, stack)
    stack.close()
    return nc


def _prep_shared(inp, T, TT):
    consts_np, _ = _consts(TT)
    f = lambda a: np.asarray(a, np.float32)
    pf = np.concatenate([
        _fm(f(inp["attn_norm_g"])[0]), _fm(f(inp["ffn_norm_g"])[0]), _fm(f(inp["ssm_norm_g"])[0]),
        _fm(f(inp["ffn_norm_g"])[1]), _fm(f(inp["final_norm_g"])),
        _fm(f(inp["ret_gn_g"])[0]), _fm(f(inp["hg_gn_g"])[0]),
        np.concatenate([_fm(f(inp["hg_lb"])[l]) for l in range(3)], 1),
        _fm(f(inp["s5_d"])[0]),
    ], 1)

    def qj(a):
        a = f(a)
        rest = a.shape[2:]
        return np.ascontiguousarray(a.reshape((64, 2, 64) + rest).transpose((1, 2, 0) + tuple(range(3, 3 + len(rest)))).reshape((128, 64) + rest))
    lamr = qj(inp["s5_lam_re"][0]); lami = qj(inp["s5_lam_im"][0])
    logdt = qj(np.repeat(f(inp["s5_log_dt"])[0][:, None], 64, 1))
    bre = qj(inp["s5_b_re"][0]); bim = qj(inp["s5_b_im"][0])
    cre = qj(f(inp["s5_c_re"])[0].transpose(0, 2, 1)); cim = qj(f(inp["s5_c_im"])[0].transpose(0, 2, 1))
    s5p = np.concatenate([lamr, lami, logdt, bre.reshape(128, -1), bim.reshape(128, -1), cre.reshape(128, -1), cim.reshape(128, -1)], 1)
    wfd = []
    for l in range(2):
        w = f(inp["w_ffn_down"])[l]
        halves = []
        for hf_ in range(2):
            wh = w[hf_ * 2816:(hf_ + 1) * 2816]
            halves.append(_wblocks(wh, 128))
        wfd.append(np.stack(halves, 0))
    shared = {
        "consts": consts_np, "pf": np.ascontiguousarray(pf.astype(np.float32)), "s5p": np.ascontiguousarray(s5p.astype(np.float32)),
        "w_in": _wblocks(f(inp["w_in"])[0], 256), "w_out": _wblocks(f(inp["w_out"])[0], 256),
        "w_ga": _wblocks(f(inp["w_glu_a"])[0], 256), "w_gb": _wblocks(f(inp["w_glu_b"])[0], 256),
        "w_fg": np.stack([_wblocks(f(inp["w_ffn_gate"])[l], 256) for l in range(2)], 0),
        "w_fu": np.stack([_wblocks(f(inp["w_ffn_up"])[l], 256) for l in range(2)], 0),
        "w_fd": np.stack(wfd, 0),
    }
    return shared


_T, _TT, _NPRE = 2048, 512, 2


def kernel(**inp):
    T, TT, NPRE = _T, _TT, _NPRE
    TH = T // 2
    shared = _prep_shared(inp, T, TT)
    nc = build(T, TT, npre=NPRE)
    xpr = np.asarray(inp["x_prompt"], np.float32)
    xsm = np.asarray(inp["x_sample"], np.float32)[:, 0, :]
    samp_pos = np.full(NS, 16384.0)
    rot_a = np.ascontiguousarray(_rot_tables(np.concatenate([np.arange(TH, dtype=np.float64), np.arange(TH, dtype=np.float64), samp_pos])))
    rot_b = np.ascontiguousarray(_rot_tables(np.concatenate([np.arange(T, dtype=np.float64), samp_pos])))
    zeros_h = np.zeros((TH, D), np.float32)
    in_maps = []
    for c in range(NCORES):
        seq, half = c // 2, c % 2
        m = dict(shared)
        if half == 0:
            m["xp"] = np.ascontiguousarray(np.concatenate([zeros_h, xpr[seq, :TH]], 0))
            m["rot"] = rot_a
        else:
            m["xp"] = np.ascontiguousarray(xpr[seq])
            m["rot"] = rot_b
        m["xs"] = np.ascontiguousarray(xsm[c * NS:(c + 1) * NS])
        m["st_ret"] = np.ascontiguousarray(np.asarray(inp["state_ret"], np.float32)[0, c * NS:(c + 1) * NS])
        m["st_hg"] = np.ascontiguousarray(np.asarray(inp["state_hgrn"], np.float32)[0, c * NS:(c + 1) * NS])
        m["st_s5r"] = np.ascontiguousarray(np.asarray(inp["state_s5_re"], np.float32)[0, c * NS:(c + 1) * NS].reshape(NS, 8192))
        m["st_s5i"] = np.ascontiguousarray(np.asarray(inp["state_s5_im"], np.float32)[0, c * NS:(c + 1) * NS].reshape(NS, 8192))
        in_maps.append(m)
    res = run_bass_kernel_spmd(nc, in_maps, core_ids=list(range(NCORES))).results
    y_prompt = np.stack([np.concatenate([res[2 * q]["yp"], res[2 * q + 1]["yp"]], 0) for q in range(4)], 0)
    y_sample = np.concatenate([res[c]["ys"] for c in range(NCORES)], 0)[:, None, :]
    fin = [2 * q + 1 for q in range(4)]
    ret_p = np.stack([res[c]["ret_p"] for c in fin], 0)[None]
    ret_s = np.concatenate([res[c]["ret_s"] for c in range(NCORES)], 0)[None]
    hg_p = np.stack([res[c]["hg_p"] for c in fin], 0)[None]
    hg_s = np.concatenate([res[c]["hg_s"] for c in range(NCORES)], 0)[None]
    s5r_p = np.stack([res[c]["s5r_p"].reshape(128, 64) for c in fin], 0)[None]
    s5i_p = np.stack([res[c]["s5i_p"].reshape(128, 64) for c in fin], 0)[None]
    s5r_s = np.concatenate([res[c]["s5r_s"] for c in range(NCORES)], 0).reshape(1, 128, 128, 64)
    s5i_s = np.concatenate([res[c]["s5i_s"] for c in range(NCORES)], 0).reshape(1, 128, 128, 64)
    return (y_prompt, y_sample, ret_p, ret_s, hg_p, hg_s, s5r_p, s5i_p, s5r_s, s5i_s)
```
